# Optimizing a Trainium2 kernel written in Bass

```python
import jax, jax.numpy as jnp
from jax import lax
import numpy as np

D_MODEL = 4096
BATCH = 2
SEQ = 8192
DEPTH = 1

CHUNK = 64
D_MIX = D_MODEL
M_HEADS = 4
M_V_DIM = D_MIX // 2 // M_HEADS
M_QK_DIM = M_V_DIM // 2
M_WIDTH = M_HEADS * M_V_DIM
G_HEADS = 16
G_HEAD_DIM = (D_MIX - M_WIDTH) // G_HEADS
G_WIDTH = G_HEADS * G_HEAD_DIM
CONV_K = 4
D_FF = 4 * D_MODEL
GATE_CAP = 15.0
NORM_EPS = 1e-6
L2_EPS = 1e-6
SPLIT_SIZES = (M_HEADS * M_QK_DIM, M_HEADS * M_QK_DIM, M_WIDTH, M_WIDTH, M_HEADS, M_HEADS,
               3 * G_WIDTH, G_WIDTH, G_HEADS, G_HEADS)
D_IN_PROJ = sum(SPLIT_SIZES)

kernel_name = 'hybrid_mlstm_gdn_parallel_heads'


def _rmsnorm(x, g):
    xf = x.astype(jnp.float32)
    y = xf * lax.rsqrt(jnp.mean(xf * xf, axis=-1, keepdims=True) + NORM_EPS)
    return (y * g.astype(jnp.float32)).astype(x.dtype)


def _soft_cap(z):
    return GATE_CAP * jnp.tanh(z / GATE_CAP)


def _l2norm(z):
    return z * lax.rsqrt(jnp.sum(z * z, axis=-1, keepdims=True) + L2_EPS)


def _to_chunks(t):
    b, t_len, h, d = t.shape
    return t.reshape(b, t_len // CHUNK, CHUNK, h, d).transpose(1, 0, 3, 2, 4)


def _gate_chunks(g):
    b, t_len, h = g.shape
    return g.reshape(b, t_len // CHUNK, CHUNK, h).transpose(1, 0, 3, 2)


def _from_chunks(y):
    nc, b, h, l, d = y.shape
    return y.transpose(1, 0, 3, 2, 4).reshape(b, nc * l, h, d)


def mlstm_chunkwise(q, k, v, i_pre, f_pre):
    b, _, h, dqk = q.shape
    dv = v.shape[-1]
    k = k * (dqk ** -0.5)
    log_i = _soft_cap(i_pre)
    log_f = jax.nn.log_sigmoid(_soft_cap(f_pre))
    causal = jnp.tril(jnp.ones((CHUNK, CHUNK), dtype=bool))

    def step(carry, inp):
        c_st, n_st, m_st = carry
        qc, kc, vc, ic, fc = inp
        bcum = jnp.cumsum(fc, axis=-1)
        d_mat = jnp.where(causal, bcum[..., :, None] - bcum[..., None, :] + ic[..., None, :], -jnp.inf)
        g_inter = bcum + m_st[..., None]
        m_t = jnp.maximum(g_inter, jnp.max(d_mat, axis=-1))
        w_inter = jnp.exp(g_inter - m_t)
        s_mat = jnp.einsum('bhld,bhsd->bhls', qc, kc) * jnp.exp(d_mat - m_t[..., None])
        num = (w_inter[..., None] * jnp.einsum('bhld,bhde->bhle', qc, c_st)
               + jnp.einsum('bhls,bhse->bhle', s_mat, vc))
        den = w_inter * jnp.einsum('bhld,bhd->bhl', qc, n_st) + jnp.sum(s_mat, axis=-1)
        h_out = num / jnp.maximum(jnp.abs(den), jnp.exp(-m_t))[..., None]
        b_last = bcum[..., -1]
        g_state = b_last[..., None] - bcum + ic
        m_new = jnp.maximum(b_last + m_st, jnp.max(g_state, axis=-1))
        w_old = jnp.exp(b_last + m_st - m_new)
        k_w = kc * jnp.exp(g_state - m_new[..., None])[..., None]
        c_new = w_old[..., None, None] * c_st + jnp.einsum('bhld,bhle->bhde', k_w, vc)
        n_new = w_old[..., None] * n_st + jnp.sum(k_w, axis=2)
        return (c_new, n_new, m_new), h_out

    init = (jnp.zeros((b, h, dqk, dv), jnp.float32),
            jnp.zeros((b, h, dqk), jnp.float32),
            jnp.zeros((b, h), jnp.float32))
    _, h_all = lax.scan(step, init, (_to_chunks(q), _to_chunks(k), _to_chunks(v),
                                     _gate_chunks(log_i), _gate_chunks(log_f)))
    return _from_chunks(h_all)


def causal_conv_silu(x, w):
    c = x.shape[-1]
    y = lax.conv_general_dilated(x, w[:, None, :], window_strides=(1,), padding=[(CONV_K - 1, 0)],
                                 dimension_numbers=('NWC', 'WIO', 'NWC'), feature_group_count=c)
    return jax.nn.silu(y)


def gated_deltanet_chunkwise(q, k, v, a_pre, b_pre, a_log, dt_bias):
    b, _, h, dk = q.shape
    dv = v.shape[-1]
    q = _l2norm(q) * (dk ** -0.5)
    k = _l2norm(k)
    beta = jax.nn.sigmoid(b_pre)
    g = -jnp.exp(a_log.astype(jnp.float32)) * jax.nn.softplus(a_pre + dt_bias.astype(jnp.float32))
    causal = jnp.tril(jnp.ones((CHUNK, CHUNK), dtype=bool))
    strict = jnp.tril(jnp.ones((CHUNK, CHUNK), dtype=bool), k=-1)
    eye = jnp.eye(CHUNK, dtype=jnp.float32)

    def step(s_st, inp):
        qc, kc, vc, gc, bc = inp
        decay = jnp.cumsum(gc, axis=-1)
        gam = jnp.exp(jnp.where(causal, decay[..., :, None] - decay[..., None, :], -jnp.inf))
        kb = kc * bc[..., None]
        vb = vc * bc[..., None]
        a_mat = jnp.where(strict, jnp.einsum('bhld,bhsd->bhls', kb, kc) * gam, 0.0)
        rhs = jnp.concatenate([vb, kb * jnp.exp(decay)[..., None]], axis=-1)
        sol = lax.linalg.triangular_solve(eye + a_mat, rhs, left_side=True, lower=True,
                                          unit_diagonal=True)
        u, w = sol[..., :dv], sol[..., dv:]
        v_new = u - jnp.einsum('bhld,bhde->bhle', w, s_st)
        attn = jnp.einsum('bhld,bhsd->bhls', qc, kc) * gam
        o = (jnp.einsum('bhld,bhde->bhle', qc * jnp.exp(decay)[..., None], s_st)
             + jnp.einsum('bhls,bhse->bhle', attn, v_new))
        d_last = decay[..., -1]
        k_dec = kc * jnp.exp(d_last[..., None] - decay)[..., None]
        s_new = s_st * jnp.exp(d_last)[..., None, None] + jnp.einsum('bhld,bhle->bhde', k_dec, v_new)
        return s_new, o

    init = jnp.zeros((b, h, dk, dv), jnp.float32)
    _, o_all = lax.scan(step, init, (_to_chunks(q), _to_chunks(k), _to_chunks(v),
                                     _gate_chunks(g), _gate_chunks(beta)))
    return _from_chunks(o_all)


def setup_inputs(seed: int = 0) -> dict:
    key = jax.random.key(seed)
    ks = jax.random.split(key, 16)
    nrm = jax.random.normal
    x = nrm(ks[0], (BATCH, SEQ, D_MODEL), jnp.float32)
    norm1_g = 1.0 + 0.02 * nrm(ks[1], (DEPTH, D_MODEL), jnp.float32)
    w_in = nrm(ks[2], (DEPTH, D_MODEL, D_IN_PROJ), jnp.float32) * (D_MODEL ** -0.5)
    mlstm_i_bias = 0.1 * nrm(ks[3], (DEPTH, M_HEADS), jnp.float32)
    mlstm_f_bias = 3.0 + 0.5 * nrm(ks[4], (DEPTH, M_HEADS), jnp.float32)
    mlstm_norm_g = 1.0 + 0.02 * nrm(ks[5], (DEPTH, M_WIDTH), jnp.float32)
    gdn_conv_w = nrm(ks[6], (DEPTH, CONV_K, 3 * G_WIDTH), jnp.float32) * (CONV_K ** -0.5)
    gdn_a_log = jnp.log(jax.random.uniform(ks[7], (DEPTH, G_HEADS), jnp.float32, 1.0, 16.0))
    dt = jnp.exp(jax.random.uniform(ks[8], (DEPTH, G_HEADS), jnp.float32,
                                    float(np.log(1e-3)), float(np.log(1e-1))))
    gdn_dt_bias = dt + jnp.log(-jnp.expm1(-dt))
    gdn_norm_g = 1.0 + 0.02 * nrm(ks[9], (DEPTH, G_HEAD_DIM), jnp.float32)
    w_out = nrm(ks[10], (DEPTH, D_MIX, D_MODEL), jnp.float32) * (D_MIX ** -0.5)
    norm2_g = 1.0 + 0.02 * nrm(ks[11], (DEPTH, D_MODEL), jnp.float32)
    w_up = nrm(ks[12], (DEPTH, D_MODEL, D_FF), jnp.float32) * (D_MODEL ** -0.5)
    w_down = nrm(ks[13], (DEPTH, D_FF, D_MODEL), jnp.float32) * (D_FF ** -0.5)
    norm_f_g = 1.0 + 0.02 * nrm(ks[14], (D_MODEL,), jnp.float32)
    return {'x': x, 'norm1_g': norm1_g, 'w_in': w_in, 'mlstm_i_bias': mlstm_i_bias,
            'mlstm_f_bias': mlstm_f_bias, 'mlstm_norm_g': mlstm_norm_g, 'gdn_conv_w': gdn_conv_w,
            'gdn_a_log': gdn_a_log, 'gdn_dt_bias': gdn_dt_bias, 'gdn_norm_g': gdn_norm_g,
            'w_out': w_out, 'norm2_g': norm2_g, 'w_up': w_up, 'w_down': w_down,
            'norm_f_g': norm_f_g}


def reference(x, norm1_g, w_in, mlstm_i_bias, mlstm_f_bias, mlstm_norm_g, gdn_conv_w,
              gdn_a_log, gdn_dt_bias, gdn_norm_g, w_out, norm2_g, w_up, w_down, norm_f_g):
    b, t_len, _ = x.shape
    f32 = jnp.float32
    split_points = np.cumsum(np.array(SPLIT_SIZES))[:-1].tolist()
    h = x
    for l in range(DEPTH):
        xn = _rmsnorm(h, norm1_g[l])
        proj = jnp.einsum('btd,de->bte', xn, w_in[l])
        m_q, m_k, m_v, m_o, m_i, m_f, g_qkv, g_z, g_a, g_b = jnp.split(proj, split_points, axis=-1)
        m_h = mlstm_chunkwise(
            m_q.astype(f32).reshape(b, t_len, M_HEADS, M_QK_DIM),
            m_k.astype(f32).reshape(b, t_len, M_HEADS, M_QK_DIM),
            m_v.astype(f32).reshape(b, t_len, M_HEADS, M_V_DIM),
            m_i.astype(f32) + mlstm_i_bias[l].astype(f32),
            m_f.astype(f32) + mlstm_f_bias[l].astype(f32))
        m_h = (_rmsnorm(m_h, mlstm_norm_g[l].reshape(M_HEADS, M_V_DIM))
               * jax.nn.sigmoid(m_o.astype(f32)).reshape(b, t_len, M_HEADS, M_V_DIM))
        g_qkv = causal_conv_silu(g_qkv.astype(f32), gdn_conv_w[l].astype(f32))
        g_q, g_k, g_v = jnp.split(g_qkv, 3, axis=-1)
        g_o = gated_deltanet_chunkwise(
            g_q.reshape(b, t_len, G_HEADS, G_HEAD_DIM),
            g_k.reshape(b, t_len, G_HEADS, G_HEAD_DIM),
            g_v.reshape(b, t_len, G_HEADS, G_HEAD_DIM),
            g_a.astype(f32), g_b.astype(f32), gdn_a_log[l], gdn_dt_bias[l])
        g_o = (_rmsnorm(g_o, gdn_norm_g[l])
               * jax.nn.silu(g_z.astype(f32)).reshape(b, t_len, G_HEADS, G_HEAD_DIM))
        mix = jnp.concatenate([m_h.reshape(b, t_len, M_WIDTH), g_o.reshape(b, t_len, G_WIDTH)],
                              axis=-1).astype(h.dtype)
        h = h + jnp.einsum('btm,md->btd', mix, w_out[l])
        hn = _rmsnorm(h, norm2_g[l])
        up = jnp.square(jax.nn.relu(jnp.einsum('btd,df->btf', hn, w_up[l])))
        h = h + jnp.einsum('btf,fd->btd', up, w_down[l])
    return _rmsnorm(h, norm_f_g)
```

```python
import contextlib
import numpy as np
import ml_dtypes
import concourse.bass as bass
import concourse.mybir as mybir
from concourse.bass_utils import run_bass_kernel_spmd

F32 = mybir.dt.float32
BF16 = mybir.dt.bfloat16
ALU = mybir.AluOpType
AF = mybir.ActivationFunctionType
NPBF16 = ml_dtypes.bfloat16

NCORES = 8
NORM_EPS = 1e-6
L2_EPS = 1e-6
GATE_CAP = 15.0


class Prog:
    ENGS = ("pe", "act", "dve", "pool", "sp")

    def __init__(self, nc):
        self.nc = nc
        self.ops = {e: [] for e in self.ENGS}
        self.nseq = {e: 0 for e in self.ENGS}
        self.last_write = {}
        self.readers = {}
        self.waited = {e: {} for e in self.ENGS}
        self.dma_slots = {}
        self.marked = {e: set() for e in self.ENGS}
        self.kids = {}
        self.parent = {}
        self.pending = {}
        self.attach_waits = True
        self._cap = None

    def alias(self, parent, kids):
        self.kids.setdefault(parent, []).extend(kids)
        for k in kids:
            self.parent[k] = parent

    def _expand(self, keys):
        out = []
        for k in keys:
            out.append(k)
            if k in self.parent:
                out.append(self.parent[k])
            out.extend(self.kids.get(k, ()))
        return out

    def _deps(self, eng, reads, writes):
        toks = []
        for k in self._expand(list(reads) + list(writes)):
            t = self.last_write.get(k)
            if t is not None:
                toks.append((t, "raw"))
        for k in self._expand(writes):
            for t in self.readers.get(k, {}).values():
                toks.append((t, "war"))
        waits = []
        w = self.waited[eng]
        for t, kind in toks:
            if t[0] == "eng":
                _, src, seq = t
                if src == eng and (eng == "pe" or kind == "war"):
                    continue
                if w.get(("eng", src), 0) >= seq:
                    continue
                w[("eng", src)] = seq
                self.marked[src].add(seq)
                waits.append(t)
            else:
                _, slot, cnt = t
                if w.get(("dma", slot), 0) >= cnt:
                    continue
                w[("dma", slot)] = cnt
                waits.append(t)
        return waits

    def _commit(self, tok, reads, writes):
        rk = (tok[0], tok[1])
        for k in reads:
            self.readers.setdefault(k, {})[rk] = tok
        for k in writes:
            self.last_write[k] = tok
            self.readers[k] = {}

    def barrier(self):
        toks = [("eng", e, self.nseq[e]) for e in self.ENGS if self.nseq[e] > 0]
        toks += [("dma", s_, v[1]) for s_, v in self.dma_slots.items()]
        self.pending = {e: list(toks) for e in self.ENGS}

    def _take_pending(self, eng):
        out = []
        w = self.waited[eng]
        for t in self.pending.pop(eng, ()):
            if t[0] == "eng":
                _, src, seq = t
                if src == eng or w.get(("eng", src), 0) >= seq:
                    continue
                w[("eng", src)] = seq
                self.marked[src].add(seq)
                out.append(t)
            else:
                _, slot, cnt = t
                if w.get(("dma", slot), 0) >= cnt:
                    continue
                w[("dma", slot)] = cnt
                out.append(t)
        return out

    def begin_capture(self):
        self._cap = []

    def end_capture(self):
        c, self._cap = self._cap, None
        return c

    def play(self, *lists):
        lists = [l for l in lists if l]
        idx = [0] * len(lists)
        total = sum(len(l) for l in lists)
        for _ in range(total):
            k = min((i for i in range(len(lists)) if idx[i] < len(lists[i])),
                    key=lambda i: (idx[i] + 1) / len(lists[i]))
            kind, a = lists[k][idx[k]]
            idx[k] += 1
            if kind == "op":
                self.op(*a)
            else:
                self.dma(*a)

    def op(self, eng, fn, reads=(), writes=()):
        if self._cap is not None:
            self._cap.append(("op", (eng, fn, tuple(reads), tuple(writes))))
            return None
        waits = self._take_pending(eng) + self._deps(eng, reads, writes)
        self.nseq[eng] += 1
        seq = self.nseq[eng]
        tok = ("eng", eng, seq)
        self.ops[eng].append(dict(fn=fn, waits=waits, seq=seq, dma=None))
        self._commit(tok, reads, writes)
        return tok

    def dma(self, eng, out, in_, reads=(), writes=(), slot=None):
        if self._cap is not None:
            self._cap.append(("dma", (eng, out, in_, tuple(reads), tuple(writes), slot)))
            return None
        waits = self._take_pending(eng) + self._deps(eng, reads, writes)
        st = self.dma_slots.setdefault(slot, [len(self.dma_slots), 0])
        st[1] += 16
        tok = ("dma", slot, st[1])
        self.ops[eng].append(dict(fn=lambda e, o=out, i=in_: e.dma_start(out=o, in_=i), waits=waits, seq=None,
                                  dma=slot))
        self._commit(tok, reads, writes)
        return tok

    def emit(self, final_tokens):
        nc = self.nc
        with contextlib.ExitStack() as es:
            esem = {e: es.enter_context(nc.semaphore("sem_" + e)) for e in self.ENGS}
            dsem = {s: es.enter_context(nc.semaphore("dsem_%d" % v[0])) for s, v in self.dma_slots.items()}
            rank = {}
            for e in self.ENGS:
                m = sorted(self.marked[e])
                rank[e] = {s: i + 1 for i, s in enumerate(m)}
            block = es.enter_context(nc.Block())

            def semval(t):
                if t[0] == "eng":
                    return esem[t[1]], rank[t[1]][t[2]]
                return dsem[t[1]], t[2]

            def replay(ename, eng):
                for o in self.ops[ename]:
                    waits = o["waits"]
                    attach = None
                    if waits and o["dma"] is None and self.attach_waits:
                        attach, waits = waits[-1], waits[:-1]
                    for t in waits:
                        eng.wait_ge(*semval(t))
                    ins = o["fn"](eng)
                    if attach is not None:
                        ins._wait_ge(*semval(attach))
                    if o["dma"] is not None:
                        ins.then_inc(dsem[o["dma"]], 16)
                    elif o["seq"] in rank[ename]:
                        ins.then_inc(esem[ename], 1)
                if ename == "sp":
                    for t in final_tokens:
                        eng.wait_ge(dsem[t[1]], t[2])

            @block.tensor
            def _(e):
                replay("pe", e)

            @block.scalar
            def _(e):
                replay("act", e)

            @block.vector
            def _(e):
                replay("dve", e)

            @block.gpsimd
            def _(e):
                replay("pool", e)

            @block.sync
            def _(e):
                replay("sp", e)


class WStream:
    def __init__(self, P, bufs, name, eng="pool"):
        self.P, self.bufs, self.name, self.eng = P, bufs, name, eng
        self.pieces = []
        self.issued = 0

    def add(self, src_ap, view):
        self.pieces.append((src_ap, view))
        return len(self.pieces) - 1

    def acquire(self, k):
        nb = len(self.bufs)
        while self.issued < min(len(self.pieces), k + nb):
            i = self.issued
            src, view = self.pieces[i]
            b = i % nb
            self.P.dma(self.eng, view(self.bufs[b]), src, writes=[(self.name, b)], slot=(self.name, b))
            self.issued += 1
        return self.bufs[k % nb], (self.name, k % nb)


def _mm_group(P, ps_ap, ps_key, pairs, reads):
    n = len(pairs)
    for i, (l, r) in enumerate(pairs):
        P.op("pe", lambda e, l=l, r=r, i=i: e.matmul(ps_ap, l, r, start=(i == 0), stop=(i == n - 1)),
             reads=reads, writes=[ps_key])


def build_phase2(D, F, NTOK, TILE=512):
    KC = D // 128
    NSUB = TILE // 128
    NT = NTOK // TILE
    FG = min(F, 2048)
    NFG = F // FG
    FCG = FG // 128
    CG = D // 512
    RG = 2 if KC >= 2 else 1
    KCR = KC // RG
    nc = bass.Bass("TRN2", target_bir_lowering=False)
    x = nc.dram_tensor("x", [NTOK, D], F32, kind="ExternalInput").ap()
    mixT = nc.dram_tensor("mixT", [D, NTOK], BF16, kind="ExternalInput").ap()
    w_out = nc.dram_tensor("w_out", [D, D], F32, kind="ExternalInput").ap()
    w_up = nc.dram_tensor("w_up", [D, F], F32, kind="ExternalInput").ap()
    w_down = nc.dram_tensor("w_down", [F, D], F32, kind="ExternalInput").ap()
    g2 = nc.dram_tensor("g2rep", [128, D], F32, kind="ExternalInput").ap()
    gf = nc.dram_tensor("gfrep", [128, D], F32, kind="ExternalInput").ap()
    identb = nc.dram_tensor("identb", [128, 128], BF16, kind="ExternalInput").ap()
    y = nc.dram_tensor("y", [NTOK, D], F32, kind="ExternalOutput").ap()

    P = Prog(nc)
    with contextlib.ExitStack() as es:
        def sb(name, shape, dt):
            return es.enter_context(nc.sbuf_tensor("sb_" + name, shape, dt))

        def ps(name, shape, dt=F32):
            return es.enter_context(nc.psum_tensor("ps_" + name, shape, dt))

        h = sb("h", [128, NSUB, D], F32)
        actT = sb("actT", [128, KC, TILE], BF16)
        upT = sb("upT", [128, FCG, TILE], BF16)
        WB = max(KC * 256, FCG * 512, KCR * 512)
        wb = [sb("wb%d" % i, [128, WB], BF16) for i in range(3)]
        grep = sb("grep", [128, D], F32)
        hb = sb("hb", [128, D], BF16)
        rl = [sb("rl%d" % i, [128, TILE], F32) for i in range(2)]
        idb = sb("idb", [128, 128], BF16)
        st = sb("st", [128, 8], F32)
        ps_o = [ps("ps_o%d" % i, [128, 512]) for i in range(4)]
        ps_u = [ps("ps_u%d" % i, [128, 512]) for i in range(2)]
        ps_t = ps("ps_t", [128, 1024], BF16)

        epsc = sb("epsc", [128, 1], F32)
        P.dma("sp", idb[:, :], identb, writes=["idb"], slot="c_id")
        P.op("dve", lambda e: e.memset(epsc[:, :], NORM_EPS), writes=["epsc"])

        ws = WStream(P, wb, "wb")
        sched = []
        for t in range(NT):
            for cg in range(CG):
                for rg in range(RG):
                    src = w_out[rg * KCR * 128:(rg + 1) * KCR * 128, cg * 512:(cg + 1) * 512] \
                        .rearrange("(c p) n -> p c n", p=128)
                    ws.add(src, lambda b: b[:, 0:KCR * 512].rearrange("p (c n) -> p c n", n=512))
            for fg in range(NFG):
                for pc in range(FG // 256):
                    c0 = fg * FG + pc * 256
                    src = w_up[:, c0:c0 + 256].rearrange("(c p) n -> p c n", p=128)
                    ws.add(src, lambda b: b[:, 0:KC * 256].rearrange("p (c n) -> p c n", n=256))
                for cg in range(CG):
                    src = w_down[fg * FG:(fg + 1) * FG, cg * 512:(cg + 1) * 512] \
                        .rearrange("(c p) n -> p c n", p=128)
                    ws.add(src, lambda b: b[:, 0:FCG * 512].rearrange("p (c n) -> p c n", n=512))
        wk = [0]

        def next_w():
            k = wk[0]
            wk[0] += 1
            return ws.acquire(k)

        def rmsnorm_to_T(t, gain_key):
            for i in range(NSUB):
                P.op("act", lambda e, i=i: e.activation(out=hb[:, :], in_=h[:, i, :], func=AF.Square,
                                                      accum_out=st[:, 0:1]),
                     reads=[("h", i)], writes=["hb", "st0"])
                P.op("act", lambda e: e.activation(out=st[:, 1:2], in_=st[:, 0:1], func=AF.Sqrt, scale=1.0 / D,
                                                   bias=epsc[:, 0:1]),
                     reads=["st0", "epsc"], writes=["st1"])
                P.op("dve", lambda e: e.reciprocal(out=st[:, 2:3], in_=st[:, 1:2]),
                     reads=["st1"], writes=["st2"])
                P.op("dve", lambda e, i=i: e.scalar_tensor_tensor(out=hb[:, :], in0=h[:, i, :], scalar=st[:, 2:3],
                                                                in1=grep[:, :], op0=ALU.mult, op1=ALU.mult),
                     reads=[("h", i), "st2", gain_key], writes=["hb"])
                for c8 in range(0, KC, 8):
                    n8 = min(8, KC - c8)
                    for c in range(n8):
                        P.op("pe", lambda e, c=c, c8=c8: e.transpose(ps_t[:, c * 128:(c + 1) * 128],
                                                                     hb[:, (c8 + c) * 128:(c8 + c + 1) * 128],
                                                                     idb[:, :]),
                             reads=["hb", "idb"], writes=["ps_t"])
                    P.op("act", lambda e, c8=c8, n8=n8, i=i: e.copy(
                        out=actT[:, c8:c8 + n8, i * 128:(i + 1) * 128],
                        in_=ps_t[:, 0:n8 * 128].rearrange("p (c n) -> p c n", n=128)),
                         reads=["ps_t"], writes=[("actT", i)])

        out_tok = []
        for t in range(NT):
            r0 = t * TILE
            for i in range(NSUB):
                P.dma("sp", h[:, i, :], x[r0 + i * 128:r0 + (i + 1) * 128, :], writes=[("h", i)], slot=("h", i))
            P.dma("sp", actT[:, :, :], mixT[:, r0:r0 + TILE].rearrange("(c p) n -> p c n", p=128),
                  writes=[("actT", i) for i in range(NSUB)], slot="actT")
            P.dma("sp", grep[:, :], g2, writes=["grep"], slot="grep")
            for cg in range(CG):
                for rg in range(RG):
                    buf, key = next_w()
                    wv = buf[:, 0:KCR * 512].rearrange("p (c n) -> p c n", n=512)
                    for i in range(NSUB):
                        for c in range(KCR):
                            kc = rg * KCR + c
                            first = (rg == 0 and c == 0)
                            last = (rg == RG - 1 and c == KCR - 1)
                            P.op("pe", lambda e, i=i, kc=kc, c=c, wv=wv, first=first, last=last: e.matmul(
                                ps_o[i][:, :], actT[:, kc, i * 128:(i + 1) * 128], wv[:, c, :],
                                start=first, stop=last),
                                 reads=[("actT", i), key], writes=[("ps_o", i)])
                for i in range(NSUB):
                    P.op("dve", lambda e, i=i, cg=cg: e.tensor_tensor(
                        out=h[:, i, cg * 512:(cg + 1) * 512], in0=ps_o[i][:, :],
                        in1=h[:, i, cg * 512:(cg + 1) * 512], op=ALU.add),
                         reads=[("ps_o", i), ("h", i)], writes=[("h", i)])
            rmsnorm_to_T(t, "grep")
            for fg in range(NFG):
                for pc in range(FG // 256):
                    buf, key = next_w()
                    wv = buf[:, 0:KC * 256].rearrange("p (c n) -> p c n", n=256)
                    for half in range(2):
                        fc = pc * 2 + half
                        pu = ps_u[fc % 2]
                        for kc in range(KC):
                            P.op("pe", lambda e, kc=kc, wv=wv, half=half, pu=pu: e.matmul(
                                pu[:, 0:TILE], wv[:, kc, half * 128:(half + 1) * 128], actT[:, kc, :],
                                start=(kc == 0), stop=(kc == KC - 1)),
                                 reads=[("actT", i) for i in range(NSUB)] + [key],
                                 writes=[("ps_u", fc % 2)])
                        r = rl[fc % 2]
                        P.op("act", lambda e, pu=pu, r=r: e.activation(out=r[:, :], in_=pu[:, 0:TILE], func=AF.Relu),
                             reads=[("ps_u", fc % 2)], writes=[("rl", fc % 2)])
                        P.op("dve", lambda e, r=r, fc=fc: e.tensor_tensor(out=upT[:, fc, :], in0=r[:, :], in1=r[:, :],
                                                                        op=ALU.mult),
                             reads=[("rl", fc % 2)], writes=["upT"])
                for cg in range(CG):
                    buf, key = next_w()
                    wv = buf[:, 0:FCG * 512].rearrange("p (c n) -> p c n", n=512)
                    for i in range(NSUB):
                        for fc in range(FCG):
                            P.op("pe", lambda e, i=i, fc=fc, wv=wv: e.matmul(
                                ps_o[i][:, :], upT[:, fc, i * 128:(i + 1) * 128], wv[:, fc, :],
                                start=(fc == 0), stop=(fc == FCG - 1)),
                                 reads=["upT", key], writes=[("ps_o", i)])
                        P.op("dve", lambda e, i=i, cg=cg: e.tensor_tensor(
                            out=h[:, i, cg * 512:(cg + 1) * 512], in0=ps_o[i][:, :],
                            in1=h[:, i, cg * 512:(cg + 1) * 512], op=ALU.add),
                             reads=[("ps_o", i), ("h", i)], writes=[("h", i)])
            P.dma("sp", grep[:, :], gf, writes=["grep"], slot="grep")
            for i in range(NSUB):
                P.op("act", lambda e, i=i: e.activation(out=hb[:, :], in_=h[:, i, :], func=AF.Square,
                                                      accum_out=st[:, 4:5]),
                     reads=[("h", i)], writes=["hb", "st4"])
                P.op("act", lambda e: e.activation(out=st[:, 5:6], in_=st[:, 4:5], func=AF.Sqrt, scale=1.0 / D,
                                                   bias=epsc[:, 0:1]),
                     reads=["st4", "epsc"], writes=["st5"])
                P.op("dve", lambda e: e.reciprocal(out=st[:, 6:7], in_=st[:, 5:6]),
                     reads=["st5"], writes=["st6"])
                P.op("dve", lambda e, i=i: e.scalar_tensor_tensor(out=h[:, i, :], in0=h[:, i, :], scalar=st[:, 6:7],
                                                                in1=grep[:, :], op0=ALU.mult, op1=ALU.mult),
                     reads=[("h", i), "st6", "grep"], writes=[("h", i)])
                tok = P.dma("sp", y[r0 + i * 128:r0 + (i + 1) * 128, :], h[:, i, :], reads=[("h", i)],
                            slot=("yout", i))
                out_tok.append(tok)
        final = {}
        for tk in out_tok:
            final[tk[1]] = tk
        P.emit(list(final.values()))
    return nc


def _ident(dt):
    return np.eye(128, dtype=np.float32).astype(dt)


def run_phase2(x2, mixT, w_out, w_up, w_down, g2, gf, TILE=512):
    n, NTOK, D = x2.shape
    F = w_up.shape[1]
    nc = build_phase2(D, F, NTOK, TILE)
    g2rep = np.ascontiguousarray(np.broadcast_to(g2.reshape(1, D), (128, D)))
    gfrep = np.ascontiguousarray(np.broadcast_to(gf.reshape(1, D), (128, D)))
    ident = _ident(NPBF16)
    in_maps = [dict(x=np.ascontiguousarray(x2[c]), mixT=np.ascontiguousarray(mixT[c]), w_out=w_out, w_up=w_up,
                    w_down=w_down, g2rep=g2rep, gfrep=gfrep, identb=ident) for c in range(n)]
    res = run_bass_kernel_spmd(nc, in_maps, core_ids=list(range(n)))
    return np.stack([res.results[c]["y"] for c in range(n)])


FM_COLS = 2048
TM_COLS = 1536
SM_COLS = 10
NLEV = 6


def build_phase1(D, T, TILE=512, stage=9):
    KC = D // 128
    NSUB = TILE // 128
    NT = T // TILE
    nc = bass.Bass("TRN2", target_bir_lowering=False)

    def din(name, shape, dt=F32):
        return nc.dram_tensor(name, shape, dt, kind="ExternalInput").ap()

    x = din("x", [T, D])
    w_fm = din("w_fm", [D, FM_COLS])
    w_tm = din("w_tm", [D, TM_COLS])
    w_sm = din("w_sm", [D, SM_COLS])
    g1rep_d = din("g1rep", [128, D])
    cst_d = din("cst", [128, 64])
    gmrep_d = din("gmrep", [128, 512])
    gnrep_d = din("gnrep", [128, 512])
    convw_d = din("convw", [128, 48])
    masks_d = din("masks", [128, 4 * 128 + 3 * 512])
    identb_d = din("identb", [128, 128], BF16)
    mix = nc.dram_tensor("mix", [T, 1024], BF16, kind="ExternalOutput").ap()

    P = Prog(nc)
    with contextlib.ExitStack() as es:
        def sb(name, shape, dt=F32):
            return es.enter_context(nc.sbuf_tensor("sb_" + name, shape, dt))

        def ps(name, shape, dt=F32):
            return es.enter_context(nc.psum_tensor("ps_" + name, shape, dt))

        def V(fn, r=(), w=()):
            return P.op("dve", fn, reads=r, writes=w)

        def A(fn, r=(), w=()):
            return P.op("act", fn, reads=r, writes=w)

        def MM(ps_ap, key, pairs, reads):
            n = len(pairs)
            for i, (l, r_) in enumerate(pairs):
                P.op("pe", lambda e, l=l, r_=r_, i=i: e.matmul(ps_ap, l, r_, start=(i == 0), stop=(i == n - 1)),
                     reads=reads, writes=[key])

        def TR(ps_ap, key, in_ap, ident, reads):
            P.op("pe", lambda e: e.transpose(ps_ap, in_ap, ident), reads=reads, writes=[key])

        g1rep = sb("g1rep", [128, D])
        cst = sb("cst", [128, 64])
        gmrep = sb("gmrep", [128, 512])
        gnrep = sb("gnrep", [128, 512])
        convw = sb("convw", [128, 48])
        masks = sb("masks", [128, 4 * 128 + 3 * 512])
        idb = sb("idb", [128, 128], BF16)
        onesb = sb("onesb", [128, 2], BF16)
        wsm = sb("wsm", [128, KC, SM_COLS], BF16)
        Cst = sb("Cst", [128, 2, 512])
        Cb = sb("Cb", [128, 2, 512], BF16)
        nst = sb("nst", [128, 2])
        nbb = sb("nbb", [128, 2], BF16)
        S4 = sb("S4", [128, 4, 128])
        S4b = sb("S4b", [128, 4, 128], BF16)
        halo = sb("halo", [128, 3, 4, 3])
        drv = sb("drv", [128, 32])
        U = masks[:, 0:128]
        Wm = masks[:, 128:256]
        I32 = masks[:, 256:384]
        ones32 = masks[:, 384:512]
        NEG4 = masks[:, 512:1024]
        STRICT4 = masks[:, 1024:1536]
        I4 = masks[:, 1536:2048]
        onec = cst[:, 2:3]
        epsc = cst[:, 3:4]
        epsl2 = cst[:, 4:5]
        xs = sb("xs", [128, D])
        xb = sb("xb", [128, D], BF16)
        xnT = sb("xnT", [128, KC, TILE], BF16)
        wb = [sb("wb%d" % i, [128, KC * 256], BF16) for i in range(2)]
        QmT = sb("QmT", [128, 2, TILE], BF16)
        KmT = sb("KmT", [128, 2, TILE], BF16)
        Gpre = sb("Gpre", [128, 4, TILE + 3])
        Gpost = sb("Gpost", [128, 12, TILE], BF16)
        Vm = sb("Vm", [128, NSUB, 512], BF16)
        Og = sb("Og", [128, NSUB, 512], BF16)
        Zg = sb("Zg", [128, NSUB, 512], BF16)
        smg = sb("smg", [128, NSUB, SM_COLS])
        cacc = [sb("cacc%d" % i, [128, TILE]) for i in range(2)]
        tmpg = [sb("tmpg%d" % i, [128, 256]) for i in range(2)]
        sqb = sb("sqb", [128, 4, TILE], BF16)
        gt = sb("gt", [128, 16, 16])
        gm_ = sb("gm_", [128, 12, 4])
        mixo = sb("mixo", [128, NSUB, 1024], BF16)
        st = sb("st", [128, 16])
        if D >= 4096:
            ov = [xs[:, k * 512:(k + 1) * 512].rearrange("p (h l) -> p h l", l=128) for k in range(8)]
            P.alias("xs", ["gU4", "ET4", "X0", "X1", "Y0", "Y1", "P0", "P1"])
        else:
            ovt = sb("ovt", [128, 8 * 512])
            ov = [ovt[:, k * 512:(k + 1) * 512].rearrange("p (h l) -> p h l", l=128) for k in range(8)]
        gU4, ET4 = ov[0], ov[1]
        Xb_ = [ov[2], ov[3]]
        Yb_ = [ov[4], ov[5]]
        Pb_ = [ov[6], ov[7]]
        ETS = sb("ETS", [128, 4, 128])
        at0 = sb("at0", [128, 4, 128], BF16)
        kdec4 = sb("kdec4", [128, 4, 128], BF16)
        vsc4 = sb("vsc4", [128, 4, 128])
        R0 = sb("R0", [128, 4, 128])
        vnew = sb("vnew", [128, 4, 128], BF16)
        t1 = sb("t1", [128, 512])
        opre = sb("opre", [128, 4, 128])
        osq = sb("osq", [128, 4, 128])
        num = sb("num", [128, 512])
        lfU = sb("lfU", [128, 128])
        EmT = sb("EmT", [128, 128])
        smT = sb("smT", [128, 128], BF16)
        kw = sb("kw", [128, 256], BF16)
        sc = sb("sc", [128, 32])
        pM = [ps("pM%d" % i, [128, 512]) for i in range(2)]
        pA = ps("pA", [128, 512])
        pB = ps("pB", [128, 512])
        pC = ps("pC", [128, 512])
        pD = ps("pD", [128, 512])
        pE = ps("pE", [128, 512])
        pT = ps("pT", [128, 1024], BF16)

        for i, (dst, src, k) in enumerate([(g1rep, g1rep_d, "g1rep"), (cst, cst_d, "cst"), (gmrep, gmrep_d, "gmrep"),
                                           (gnrep, gnrep_d, "gnrep"), (convw, convw_d, "convw"),
                                           (masks, masks_d, "masks"), (idb, identb_d, "idb")]):
            P.dma("sp", dst[:, :], src, writes=[k], slot=("c", i))
        P.dma("pool", wsm[:, :, :], w_sm.rearrange("(c p) n -> p c n", p=128), writes=["wsm"], slot=("c", "wsm"))
        V(lambda e: e.memset(onesb[:, :], 1.0), w=["onesb"])
        V(lambda e: e.memset(Cst[:, :, :], 0.0), w=["Cst"])
        V(lambda e: e.memset(Cb[:, :, :], 0.0), w=["Cb"])
        V(lambda e: e.memset(nst[:, :], 0.0), w=["nst"])
        V(lambda e: e.memset(nbb[:, :], 0.0), w=["nbb"])
        V(lambda e: e.memset(S4[:, :, :], 0.0), w=["S4"])
        V(lambda e: e.memset(S4b[:, :, :], 0.0), w=["S4b"])
        V(lambda e: e.memset(halo[:, :, :, :], 0.0), w=["halo"])
        V(lambda e: e.tensor_scalar(out=drv[:, 0:2], in0=cst[:, 0:2], scalar1=1.0 / GATE_CAP, scalar2=None,
                                    op0=ALU.mult), r=["cst"], w=["drv"])
        A(lambda e: e.activation(out=drv[:, 16:32], in_=cst[:, 16:32], func=AF.Exp), r=["cst"], w=["drvA"])

        ws = WStream(P, wb, "wb")
        for t in range(NT):
            for pc in range(FM_COLS // 256):
                ws.add(w_fm[:, pc * 256:(pc + 1) * 256].rearrange("(c p) n -> p c n", p=128),
                       lambda b: b[:, :].rearrange("p (c n) -> p c n", n=256))
            for pc in range(TM_COLS // 256):
                ws.add(w_tm[:, pc * 256:(pc + 1) * 256].rearrange("(c p) n -> p c n", p=128),
                       lambda b: b[:, :].rearrange("p (c n) -> p c n", n=256))
        wk = [0]

        def next_w():
            k = wk[0]
            wk[0] += 1
            buf, key = ws.acquire(k)
            return buf[:, :].rearrange("p (c n) -> p c n", n=256), key

        xkeys = [("xnT", i) for i in range(NSUB)]
        out_tok = []
        GTK = dict(BETA=0, GG=1, RQ=2, NK=3, RK=4, NCC=5, BRK=6, DEC=7, EDEC=8, NEDEC=9, C1=10, EDL=11, RQ2=12,
                   TMP=13, TMP2=14, DL=15)
        GMK = dict(LI=0, LF=1, BC=2, BL=3, EK=4, EGS=5, WOLD=6, TMP=7, TMP2=8)

        def G(kind, j=None, hh=None):
            k = GTK[kind]
            if j is None:
                return gt[:, k, :]
            if hh is None:
                return gt[:, k, j * 4:(j + 1) * 4]
            return gt[:, k, j * 4 + hh:j * 4 + hh + 1]

        def M_(kind, j=None):
            k = GMK[kind]
            if j is None:
                return gm_[:, k, :]
            return gm_[:, k, j:j + 1]

        for t in range(NT):
            r0 = t * TILE
            for i in range(NSUB):
                P.dma("sp", xs[:, :], x[r0 + i * 128:r0 + (i + 1) * 128, :], writes=["xs"], slot="xs")
                A(lambda e: e.activation(out=xb[:, :], in_=xs[:, :], func=AF.Square, accum_out=st[:, 0:1]),
                  r=["xs"], w=["xb", "st0"])
                A(lambda e: e.activation(out=st[:, 1:2], in_=st[:, 0:1], func=AF.Sqrt, scale=1.0 / D, bias=epsc),
                  r=["st0", "cst"], w=["st1"])
                V(lambda e: e.reciprocal(out=st[:, 2:3], in_=st[:, 1:2]), r=["st1"], w=["st2"])
                V(lambda e: e.scalar_tensor_tensor(out=xb[:, :], in0=xs[:, :], scalar=st[:, 2:3], in1=g1rep[:, :],
                                                   op0=ALU.mult, op1=ALU.mult), r=["xs", "st2", "g1rep"], w=["xb"])
                for c8 in range(0, KC, 8):
                    n8 = min(8, KC - c8)
                    for c in range(n8):
                        TR(pT[:, c * 128:(c + 1) * 128], "pT", xb[:, (c8 + c) * 128:(c8 + c + 1) * 128], idb[:, :],
                           ["xb", "idb"])
                    A(lambda e, c8=c8, n8=n8, i=i: e.copy(out=xnT[:, c8:c8 + n8, i * 128:(i + 1) * 128],
                                                          in_=pT[:, 0:n8 * 128].rearrange("p (c n) -> p c n", n=128)),
                      r=["pT"], w=[("xnT", i)])
            for pc in range(FM_COLS // 256):
                wv, key = next_w()
                for half in range(2):
                    oc = pc * 2 + half
                    pm = pM[oc % 2]
                    pk = ("pM", oc % 2)
                    MM(pm[:, 0:TILE], pk, [(wv[:, kc, half * 128:(half + 1) * 128], xnT[:, kc, :]) for kc in range(KC)],
                       xkeys + [key])
                    if oc < 2:
                        A(lambda e, oc=oc, pm=pm: e.copy(out=QmT[:, oc, :], in_=pm[:, 0:TILE]), r=[pk], w=["QmT"])
                    elif oc < 4:
                        A(lambda e, oc=oc, pm=pm: e.mul(out=KmT[:, oc - 2, :], in_=pm[:, 0:TILE], mul=1.0 / 16.0),
                          r=[pk], w=["KmT"])
                    else:
                        s_, hh = (oc - 4) // 4, (oc - 4) % 4
                        A(lambda e, hh=hh, pm=pm: e.copy(out=Gpre[:, hh, 3:3 + TILE], in_=pm[:, 0:TILE]),
                          r=[pk], w=[("Gpre", hh)])
                        if hh == 3:
                            V(lambda e, s_=s_: e.tensor_copy(out=Gpre[:, :, 0:3], in_=halo[:, s_, :, :]),
                              r=["halo"], w=[("Gpre", h_) for h_ in range(4)])
                            for h_ in range(4):
                                ca = cacc[h_ % 2]
                                ck = ("cacc", h_ % 2)
                                wcol = lambda j_, s_=s_, h_=h_: convw[:, (s_ * 4 + h_) * 4 + j_:(s_ * 4 + h_) * 4 + j_ + 1]
                                V(lambda e, h_=h_, ca=ca, wcol=wcol: e.tensor_scalar(
                                    out=ca[:, :], in0=Gpre[:, h_, 0:TILE], scalar1=wcol(0), scalar2=None, op0=ALU.mult),
                                  r=[("Gpre", h_), "convw"], w=[ck])
                                for j_ in range(1, 4):
                                    V(lambda e, h_=h_, ca=ca, wcol=wcol, j_=j_: e.scalar_tensor_tensor(
                                        out=ca[:, :], in0=Gpre[:, h_, j_:j_ + TILE], scalar=wcol(j_), in1=ca[:, :],
                                        op0=ALU.mult, op1=ALU.add), r=[("Gpre", h_), ck], w=[ck])
                                A(lambda e, s_=s_, h_=h_, ca=ca: e.activation(out=Gpost[:, s_ * 4 + h_, :], in_=ca[:, :],
                                                                            func=AF.Silu), r=[ck], w=[("Gpost", s_)])
                            V(lambda e, s_=s_: e.tensor_copy(out=halo[:, s_, :, :], in_=Gpre[:, :, TILE:TILE + 3]),
                              r=[("Gpre", h_) for h_ in range(4)], w=["halo"])
            for pc in range(TM_COLS // 256):
                wv, key = next_w()
                kind, cpc = pc // 2, pc % 2
                for i in range(NSUB):
                    pm = pM[i % 2]
                    pk = ("pM", i % 2)
                    MM(pm[:, 0:256], pk, [(xnT[:, kc, i * 128:(i + 1) * 128], wv[:, kc, :]) for kc in range(KC)],
                       [("xnT", i), key])
                    cs = slice(cpc * 256, (cpc + 1) * 256)
                    if kind == 0:
                        A(lambda e, i=i, pm=pm, cs=cs: e.copy(out=Vm[:, i, cs], in_=pm[:, 0:256]), r=[pk], w=["Vm"])
                    else:
                        tg = tmpg[i % 2]
                        tk = ("tmpg", i % 2)
                        fn = AF.Sigmoid if kind == 1 else AF.Silu
                        A(lambda e, pm=pm, tg=tg, fn=fn: e.activation(out=tg[:, :], in_=pm[:, 0:256], func=fn),
                          r=[pk], w=[tk])
                        dst = Og if kind == 1 else Zg
                        gsrc = gmrep if kind == 1 else gnrep
                        V(lambda e, i=i, tg=tg, cs=cs, dst=dst, gsrc=gsrc: e.tensor_tensor(
                            out=dst[:, i, cs], in0=tg[:, :], in1=gsrc[:, cs], op=ALU.mult),
                          r=[tk, "gmrep", "gnrep"], w=["Og" if kind == 1 else "Zg"])
            for i in range(NSUB):
                pm = pM[i % 2]
                pk = ("pM", i % 2)
                MM(pm[:, 0:SM_COLS], pk, [(xnT[:, kc, i * 128:(i + 1) * 128], wsm[:, kc, :]) for kc in range(KC)],
                   [("xnT", i), "wsm"])
                A(lambda e, i=i, pm=pm: e.copy(out=smg[:, i, :], in_=pm[:, 0:SM_COLS]), r=[pk], w=["smg"])

            GT, GM = ["gt"], ["gm"]
            if stage < 2:
                for i in range(NSUB):
                    V(lambda e, i=i: e.memset(mixo[:, i, :], 0.0), w=[("mixo", i)])
                    tok = P.dma("sp", mix[r0 + i * 128:r0 + (i + 1) * 128, :], mixo[:, i, :], reads=[("mixo", i)],
                                slot=("mixo", i))
                    out_tok.append(tok)
                continue
            A(lambda e: e.activation(out=M_("TMP"), in_=smg[:, :, 0], func=AF.Tanh, scale=1.0 / GATE_CAP,
                                     bias=drv[:, 0:1]), r=["smg", "drv"], w=GM)
            V(lambda e: e.tensor_scalar(out=M_("LI"), in0=M_("TMP"), scalar1=GATE_CAP, scalar2=None, op0=ALU.mult),
              r=GM, w=GM)
            A(lambda e: e.activation(out=M_("TMP"), in_=smg[:, :, 1], func=AF.Tanh, scale=1.0 / GATE_CAP,
                                     bias=drv[:, 1:2]), r=["smg", "drv"] + GM, w=GM)
            A(lambda e: e.activation(out=M_("TMP2"), in_=M_("TMP"), func=AF.Exp, scale=-GATE_CAP), r=GM, w=GM)
            A(lambda e: e.activation(out=M_("TMP"), in_=M_("TMP2"), func=AF.Ln, bias=onec), r=GM + ["cst"], w=GM)
            V(lambda e: e.tensor_scalar(out=M_("LF"), in0=M_("TMP"), scalar1=-1.0, scalar2=None, op0=ALU.mult),
              r=GM, w=GM)
            MM(pE[:, 0:4], "pE", [(U, M_("LF"))], GM + ["masks"])
            V(lambda e: e.tensor_copy(out=M_("BC"), in_=pE[:, 0:4]), r=["pE"], w=GM)
            MM(pE[:, 4:8], "pE", [(ones32, M_("LF"))], GM + ["masks"])
            V(lambda e: e.tensor_copy(out=M_("BL"), in_=pE[:, 4:8]), r=["pE"], w=GM)
            A(lambda e: e.activation(out=M_("EK"), in_=M_("BC"), func=AF.Exp), r=GM, w=GM)
            A(lambda e: e.activation(out=M_("WOLD"), in_=M_("BL"), func=AF.Exp), r=GM, w=GM)
            V(lambda e: e.tensor_tensor(out=M_("TMP"), in0=M_("BL"), in1=M_("BC"), op=ALU.subtract), r=GM, w=GM)
            V(lambda e: e.tensor_tensor(out=M_("TMP2"), in0=M_("TMP"), in1=M_("LI"), op=ALU.add), r=GM, w=GM)
            A(lambda e: e.activation(out=M_("EGS"), in_=M_("TMP2"), func=AF.Exp), r=GM, w=GM)
            g16 = lambda ap: ap.rearrange("p (s h) -> p s h", h=4)
            A(lambda e: e.activation(out=g16(G("BETA")), in_=smg[:, :, 6:10], func=AF.Sigmoid), r=["smg"], w=GT)
            V(lambda e: e.tensor_tensor(out=g16(G("TMP")), in0=smg[:, :, 2:6], in1=g16(cst[:, 32:48]), op=ALU.add),
              r=["smg", "cst"] + GT, w=GT)
            V(lambda e: e.tensor_scalar(out=G("TMP2"), in0=G("TMP"), scalar1=-1.0, scalar2=None, op0=ALU.mult),
              r=GT, w=GT)
            V(lambda e: e.tensor_tensor(out=G("TMP2"), in0=G("TMP2"), in1=G("TMP"), op=ALU.max), r=GT, w=GT)
            A(lambda e: e.activation(out=G("GG"), in_=G("TMP2"), func=AF.Exp, scale=-1.0), r=GT, w=GT)
            A(lambda e: e.activation(out=G("TMP2"), in_=G("GG"), func=AF.Ln, bias=onec), r=GT + ["cst"], w=GT)
            V(lambda e: e.tensor_scalar(out=G("GG"), in0=G("TMP"), scalar1=0.0, scalar2=None, op0=ALU.max), r=GT, w=GT)
            V(lambda e: e.tensor_tensor(out=G("TMP"), in0=G("GG"), in1=G("TMP2"), op=ALU.add), r=GT, w=GT)
            V(lambda e: e.scalar_tensor_tensor(out=G("GG"), in0=G("TMP"), scalar=-1.0, in1=drv[:, 16:32],
                                               op0=ALU.mult, op1=ALU.mult), r=GT + ["drvA"], w=GT)
            MM(pE[:, 16:32], "pE", [(U, G("GG"))], GT + ["masks"])
            V(lambda e: e.tensor_copy(out=G("DEC"), in_=pE[:, 16:32]), r=["pE"], w=GT)
            MM(pE[:, 32:48], "pE", [(ones32, G("GG"))], GT + ["masks"])
            V(lambda e: e.tensor_copy(out=G("DL"), in_=pE[:, 32:48]), r=["pE"], w=GT)
            A(lambda e: e.activation(out=G("EDEC"), in_=G("DEC"), func=AF.Exp), r=GT, w=GT)
            A(lambda e: e.activation(out=G("EDL"), in_=G("DL"), func=AF.Exp), r=GT, w=GT)
            V(lambda e: e.tensor_scalar(out=G("NEDEC"), in0=G("EDEC"), scalar1=-1.0, scalar2=None, op0=ALU.mult),
              r=GT, w=GT)
            V(lambda e: e.tensor_tensor(out=G("TMP"), in0=G("DL"), in1=G("DEC"), op=ALU.subtract), r=GT, w=GT)
            A(lambda e: e.activation(out=G("C1"), in_=G("TMP"), func=AF.Exp), r=GT, w=GT)
            for s_, dstk in ((0, "RQ"), (1, "NK")):
                V(lambda e, s_=s_: e.tensor_tensor(out=sqb[:, :, :], in0=Gpost[:, s_ * 4:(s_ + 1) * 4, :],
                                                   in1=Gpost[:, s_ * 4:(s_ + 1) * 4, :], op=ALU.mult),
                  r=[("Gpost", s_)], w=["sqb"])
                for j in range(NSUB):
                    for hh in range(4):
                        c = 64 + j * 4 + hh
                        MM(pE[:, c:c + 1], "pE", [(sqb[:, hh, j * 128:(j + 1) * 128], onesb[:, 0:1])], ["sqb", "onesb"])
                A(lambda e, dstk=dstk: e.activation(out=G(dstk), in_=pE[:, 64:80], func=AF.Sqrt, bias=epsl2),
                  r=["pE", "cst"] + GT, w=GT)
            V(lambda e: e.reciprocal(out=G("TMP"), in_=G("RQ")), r=GT, w=GT)
            V(lambda e: e.tensor_scalar(out=G("RQ"), in0=G("TMP"), scalar1=128.0 ** -0.5, scalar2=None, op0=ALU.mult),
              r=GT, w=GT)
            V(lambda e: e.tensor_tensor(out=G("TMP"), in0=G("RQ"), in1=G("RQ"), op=ALU.mult), r=GT, w=GT)
            V(lambda e: e.tensor_scalar(out=G("RQ2"), in0=G("TMP"), scalar1=1.0 / 128.0, scalar2=None, op0=ALU.mult),
              r=GT, w=GT)
            V(lambda e: e.reciprocal(out=G("RK"), in_=G("NK")), r=GT, w=GT)
            V(lambda e: e.tensor_tensor(out=G("BRK"), in0=G("BETA"), in1=G("RK"), op=ALU.mult), r=GT, w=GT)
            V(lambda e: e.scalar_tensor_tensor(out=G("NCC"), in0=G("BRK"), scalar=-1.0, in1=G("RK"),
                                               op0=ALU.mult, op1=ALU.mult), r=GT, w=GT)
            V(lambda e: e.tensor_tensor(out=G("TMP"), in0=G("C1"), in1=G("RK"), op=ALU.mult), r=GT, w=GT)
            V(lambda e: e.tensor_copy(out=G("C1"), in_=G("TMP")), r=GT, w=GT)

            if stage < 3:
                for i in range(NSUB):
                    V(lambda e, i=i: e.memset(mixo[:, i, :], 0.0), w=[("mixo", i)])
            for j in range(NSUB if stage >= 3 else 0):
                cs = slice(j * 128, (j + 1) * 128)
                for hh in range(4):
                    V(lambda e, hh=hh, j=j: e.tensor_scalar(out=gU4[:, hh, :], in0=U, scalar1=G("GG", j, hh),
                                                            scalar2=None, op0=ALU.mult), r=GT + ["masks"], w=["gU4"])
                MM(pA[:, :], "pA", [(Wm, gU4[:, :, :].rearrange("p h l -> p (h l)")), (I32, NEG4)], ["gU4", "masks"])
                A(lambda e: e.activation(out=ET4[:, :, :].rearrange("p h l -> p (h l)"), in_=pA[:, :], func=AF.Exp),
                  r=["pA"], w=["ET4"])
                for hh in range(4):
                    MM(pB[:, hh * 128:(hh + 1) * 128], "pB", [(Gpost[:, 4 + hh, cs], Gpost[:, 4 + hh, cs])],
                       [("Gpost", 1)])
                for hh in range(4):
                    MM(pC[:, hh * 128:(hh + 1) * 128], "pC", [(Gpost[:, 4 + hh, cs], Gpost[:, hh, cs])],
                       [("Gpost", 1), ("Gpost", 0)])
                V(lambda e: e.tensor_tensor(out=ETS[:, :, :].rearrange("p h l -> p (h l)"),
                                            in0=ET4[:, :, :].rearrange("p h l -> p (h l)"), in1=STRICT4, op=ALU.mult),
                  r=["ET4", "masks"], w=["ETS"])
                X, Y, Pm = Xb_[0], Yb_[0], Pb_[0]
                for hh in range(4):
                    V(lambda e, hh=hh, j=j, X=X: e.scalar_tensor_tensor(
                        out=X[:, hh, :], in0=pB[:, hh * 128:(hh + 1) * 128], scalar=G("NCC", j, hh), in1=ETS[:, hh, :],
                        op0=ALU.mult, op1=ALU.mult), r=["pB", "ETS"] + GT, w=["X0"])
                    V(lambda e, hh=hh, j=j: e.scalar_tensor_tensor(
                        out=at0[:, hh, :], in0=pC[:, hh * 128:(hh + 1) * 128], scalar=G("RK", j, hh), in1=ET4[:, hh, :],
                        op0=ALU.mult, op1=ALU.mult), r=["pC", "ET4"] + GT, w=["at0"])
                for hh in range(4):
                    TR(pD[:, hh * 128:(hh + 1) * 128], "pD", X[:, hh, :], I32, ["X0", "masks"])
                A(lambda e, Y=Y: e.copy(out=Y[:, :, :].rearrange("p h l -> p (h l)"), in_=pD[:, :]), r=["pD"], w=["Y0"])
                V(lambda e, Pm=Pm, X=X: e.tensor_tensor(out=Pm[:, :, :].rearrange("p h l -> p (h l)"),
                                                        in0=X[:, :, :].rearrange("p h l -> p (h l)"), in1=I4,
                                                        op=ALU.add), r=["X0", "masks"], w=["P0"])
                for lv in range(1, NLEV + 1):
                    a, b = (lv - 1) % 2, lv % 2
                    Xp, Yp, Pp = Xb_[a], Yb_[a], Pb_[a]
                    Xn, Yn, Pn = Xb_[b], Yb_[b], Pb_[b]
                    xa, ya, pa = "X%d" % a, "Y%d" % a, "P%d" % a
                    xb_, yb_, pb_ = "X%d" % b, "Y%d" % b, "P%d" % b
                    last = (lv == NLEV)
                    for hh in range(4):
                        MM(pD[:, hh * 128:(hh + 1) * 128], "pD", [(Xp[:, hh, :], Yp[:, hh, :])], [xa, ya])
                    if not last:
                        for hh in range(4):
                            MM(pB[:, hh * 128:(hh + 1) * 128], "pB", [(Yp[:, hh, :], Xp[:, hh, :])], [xa, ya])
                    V(lambda e, Yn=Yn: e.tensor_copy(out=Yn[:, :, :].rearrange("p h l -> p (h l)"), in_=pD[:, :]),
                      r=["pD"], w=[yb_])
                    if not last:
                        A(lambda e, Xn=Xn: e.copy(out=Xn[:, :, :].rearrange("p h l -> p (h l)"), in_=pB[:, :]),
                          r=["pB"], w=[xb_])
                    for hh in range(4):
                        MM(pC[:, hh * 128:(hh + 1) * 128], "pC", [(Yn[:, hh, :], Pp[:, hh, :])], [yb_, pa])
                    V(lambda e, Pn=Pn, Pp=Pp: e.tensor_tensor(out=Pn[:, :, :].rearrange("p h l -> p (h l)"),
                                                              in0=pC[:, :],
                                                              in1=Pp[:, :, :].rearrange("p h l -> p (h l)"), op=ALU.add),
                      r=["pC", pa], w=[pb_])
                NT_ = Pb_[NLEV % 2]
                ntk = "P%d" % (NLEV % 2)
                if stage < 4:
                    V(lambda e, j=j: e.memset(mixo[:, j, :], 0.0), w=[("mixo", j)])
                    continue
                V(lambda e, j=j: e.tensor_scalar(out=lfU[:, :], in0=U, scalar1=M_("LF", j), scalar2=None, op0=ALU.mult),
                  r=GM + ["masks"], w=["lfU"])
                MM(pM[0][:, 0:128], ("pM", 0), [(Wm, lfU[:, :]), (I32, NEG4[:, 0:128])], ["lfU", "masks"])
                A(lambda e, j=j: e.activation(out=EmT[:, :], in_=pM[0][:, 0:128], func=AF.Exp, bias=M_("LI", j)),
                  r=[("pM", 0)] + GM, w=["EmT"])
                MM(pM[1][:, 0:128], ("pM", 1), [(KmT[:, 0, cs], QmT[:, 0, cs]), (KmT[:, 1, cs], QmT[:, 1, cs])],
                   ["KmT", "QmT"])
                V(lambda e: e.tensor_tensor(out=smT[:, :], in0=pM[1][:, 0:128], in1=EmT[:, :], op=ALU.mult),
                  r=[("pM", 1), "EmT"], w=["smT"])
                for dc in range(2):
                    TR(pT[:, dc * 128:(dc + 1) * 128], "pT", KmT[:, dc, cs], idb[:, :], ["KmT", "idb"])
                A(lambda e, j=j: e.activation(out=kw[:, :], in_=pT[:, 0:256], func=AF.Copy, scale=M_("EGS", j)),
                  r=["pT"] + GM, w=["kw"])
                for hh in range(4):
                    TR(pT[:, hh * 128:(hh + 1) * 128], "pT", Gpost[:, 4 + hh, cs], idb[:, :], [("Gpost", 1), "idb"])
                for hh in range(4):
                    TR(pT[:, 512 + hh * 128:512 + (hh + 1) * 128], "pT", Gpost[:, 8 + hh, cs], idb[:, :],
                       [("Gpost", 2), "idb"])
                for hh in range(4):
                    A(lambda e, hh=hh, j=j: e.activation(out=kdec4[:, hh, :], in_=pT[:, hh * 128:(hh + 1) * 128],
                                                         func=AF.Copy, scale=G("C1", j, hh)), r=["pT"] + GT, w=["kdec4"])
                    A(lambda e, hh=hh, j=j: e.activation(out=vsc4[:, hh, :], in_=pT[:, 512 + hh * 128:512 + (hh + 1) * 128],
                                                         func=AF.Copy, scale=G("NK", j, hh)), r=["pT"] + GT, w=["vsc4"])
                for hh in range(4):
                    MM(pA[:, hh * 128:(hh + 1) * 128], "pA", [(Gpost[:, 4 + hh, cs], S4b[:, hh, :])], [("Gpost", 1), "S4b"])
                for hh in range(4):
                    MM(pE[:, hh * 128:(hh + 1) * 128], "pE", [(Gpost[:, hh, cs], S4b[:, hh, :])], [("Gpost", 0), "S4b"])
                for hh in range(4):
                    V(lambda e, hh=hh, j=j: e.scalar_tensor_tensor(
                        out=R0[:, hh, :], in0=pA[:, hh * 128:(hh + 1) * 128], scalar=G("NEDEC", j, hh), in1=vsc4[:, hh, :],
                        op0=ALU.mult, op1=ALU.add), r=["pA", "vsc4"] + GT, w=["R0"])
                for hh in range(4):
                    MM(pB[:, hh * 128:(hh + 1) * 128], "pB", [(NT_[:, hh, :], R0[:, hh, :])], [ntk, "R0"])
                for hh in range(4):
                    A(lambda e, hh=hh, j=j: e.activation(out=vnew[:, hh, :], in_=pB[:, hh * 128:(hh + 1) * 128],
                                                         func=AF.Copy, scale=G("BRK", j, hh)), r=["pB"] + GT, w=["vnew"])
                for hh in range(4):
                    MM(pD[:, hh * 128:(hh + 1) * 128], "pD", [(at0[:, hh, :], vnew[:, hh, :])], ["at0", "vnew"])
                for hh in range(4):
                    A(lambda e, hh=hh, j=j: e.activation(out=t1[:, hh * 128:(hh + 1) * 128],
                                                         in_=pE[:, hh * 128:(hh + 1) * 128], func=AF.Copy,
                                                         scale=G("EDEC", j, hh)), r=["pE"] + GT, w=["t1"])
                V(lambda e: e.tensor_tensor(out=opre[:, :, :].rearrange("p h l -> p (h l)"), in0=pD[:, :], in1=t1[:, :],
                                            op=ALU.add), r=["pD", "t1"], w=["opre"])
                for hh in range(4):
                    MM(pA[:, hh * 128:(hh + 1) * 128], "pA", [(kdec4[:, hh, :], vnew[:, hh, :])], ["kdec4", "vnew"])
                for hh in range(4):
                    V(lambda e, hh=hh, j=j: e.scalar_tensor_tensor(
                        out=S4[:, hh, :], in0=S4[:, hh, :], scalar=G("EDL", j, hh), in1=pA[:, hh * 128:(hh + 1) * 128],
                        op0=ALU.mult, op1=ALU.add), r=["pA", "S4"] + GT, w=["S4"])
                A(lambda e: e.copy(out=S4b[:, :, :].rearrange("p h l -> p (h l)"),
                                   in_=S4[:, :, :].rearrange("p h l -> p (h l)")), r=["S4"], w=["S4b"])
                V(lambda e: e.tensor_tensor(out=osq[:, :, :], in0=opre[:, :, :], in1=opre[:, :, :], op=ALU.mult),
                  r=["opre"], w=["osq"])
                V(lambda e: e.tensor_reduce(out=sc[:, 0:4], in_=osq[:, :, :], axis=mybir.AxisListType.X, op=ALU.add),
                  r=["osq"], w=["sc0"])
                V(lambda e, j=j: e.tensor_tensor(out=sc[:, 4:8], in0=sc[:, 0:4], in1=G("RQ2", j), op=ALU.mult),
                  r=["sc0"] + GT, w=["sc1"])
                A(lambda e: e.activation(out=sc[:, 8:12], in_=sc[:, 4:8], func=AF.Ln, bias=epsc), r=["sc1", "cst"],
                  w=["sc2"])
                A(lambda e: e.activation(out=sc[:, 12:16], in_=sc[:, 8:12], func=AF.Exp, scale=-0.5), r=["sc2"], w=["sc3"])
                V(lambda e, j=j: e.tensor_tensor(out=sc[:, 16:20], in0=sc[:, 12:16], in1=G("RQ", j), op=ALU.mult),
                  r=["sc3"] + GT, w=["sc4"])
                for hh in range(4):
                    V(lambda e, hh=hh, j=j: e.scalar_tensor_tensor(
                        out=mixo[:, j, 512 + hh * 128:512 + (hh + 1) * 128], in0=opre[:, hh, :],
                        scalar=sc[:, 16 + hh:17 + hh], in1=Zg[:, j, hh * 128:(hh + 1) * 128], op0=ALU.mult, op1=ALU.mult),
                      r=["opre", "sc4", "Zg"], w=[("mixo", j)])
                MM(pB[:, :], "pB", [(smT[:, :], Vm[:, j, :])], ["smT", "Vm"])
                MM(pC[:, :], "pC", [(QmT[:, 0, cs], Cb[:, 0, :]), (QmT[:, 1, cs], Cb[:, 1, :])], ["QmT", "Cb"])
                MM(pE[:, 0:1], "pE", [(QmT[:, 0, cs], nbb[:, 0:1]), (QmT[:, 1, cs], nbb[:, 1:2])], ["QmT", "nbb"])
                MM(pE[:, 1:2], "pE", [(smT[:, :], onesb[:, 0:1])], ["smT", "onesb"])
                A(lambda e, j=j: e.activation(out=t1[:, :], in_=pC[:, :], func=AF.Copy, scale=M_("EK", j)),
                  r=["pC"] + GM, w=["t1"])
                V(lambda e: e.tensor_tensor(out=num[:, :], in0=pB[:, :], in1=t1[:, :], op=ALU.add), r=["pB", "t1"],
                  w=["num"])
                A(lambda e: e.copy(out=sc[:, 20:22], in_=pE[:, 0:2]), r=["pE"], w=["sc5"])
                V(lambda e, j=j: e.scalar_tensor_tensor(out=sc[:, 22:23], in0=sc[:, 20:21], scalar=M_("EK", j),
                                                        in1=sc[:, 21:22], op0=ALU.mult, op1=ALU.add),
                  r=["sc5"] + GM, w=["sc6"])
                V(lambda e: e.tensor_scalar(out=sc[:, 23:24], in0=sc[:, 22:23], scalar1=-1.0, scalar2=None,
                                            op0=ALU.mult), r=["sc6"], w=["sc7"])
                V(lambda e: e.tensor_tensor(out=sc[:, 23:24], in0=sc[:, 23:24], in1=sc[:, 22:23], op=ALU.max),
                  r=["sc6", "sc7"], w=["sc7"])
                V(lambda e: e.tensor_scalar(out=sc[:, 23:24], in0=sc[:, 23:24], scalar1=1.0, scalar2=None,
                                            op0=ALU.max), r=["sc7"], w=["sc7"])
                V(lambda e: e.reciprocal(out=sc[:, 24:25], in_=sc[:, 23:24]), r=["sc7"], w=["sc8"])
                A(lambda e: e.activation(out=t1[:, :], in_=num[:, :], func=AF.Square, accum_out=sc[:, 25:26]),
                  r=["num"], w=["t1", "sc9"])
                V(lambda e: e.tensor_tensor(out=sc[:, 26:27], in0=sc[:, 24:25], in1=sc[:, 24:25], op=ALU.mult),
                  r=["sc8"], w=["sc10"])
                V(lambda e: e.tensor_tensor(out=sc[:, 27:28], in0=sc[:, 26:27], in1=sc[:, 25:26], op=ALU.mult),
                  r=["sc10", "sc9"], w=["sc11"])
                A(lambda e: e.activation(out=sc[:, 28:29], in_=sc[:, 27:28], func=AF.Ln, scale=1.0 / 512.0, bias=epsc),
                  r=["sc11", "cst"], w=["sc12"])
                A(lambda e: e.activation(out=sc[:, 29:30], in_=sc[:, 28:29], func=AF.Exp, scale=-0.5), r=["sc12"],
                  w=["sc13"])
                V(lambda e: e.tensor_tensor(out=sc[:, 30:31], in0=sc[:, 29:30], in1=sc[:, 24:25], op=ALU.mult),
                  r=["sc13", "sc8"], w=["sc14"])
                V(lambda e, j=j: e.scalar_tensor_tensor(out=mixo[:, j, 0:512], in0=num[:, :], scalar=sc[:, 30:31],
                                                        in1=Og[:, j, :], op0=ALU.mult, op1=ALU.mult),
                  r=["num", "sc14", "Og"], w=[("mixo", j)])
                MM(pB[:, :], "pB", [(kw[:, 0:128], Vm[:, j, :])], ["kw", "Vm"])
                MM(pC[:, :], "pC", [(kw[:, 128:256], Vm[:, j, :])], ["kw", "Vm"])
                MM(pE[:, 8:9], "pE", [(kw[:, 0:128], onesb[:, 0:1])], ["kw", "onesb"])
                MM(pE[:, 9:10], "pE", [(kw[:, 128:256], onesb[:, 0:1])], ["kw", "onesb"])
                V(lambda e, j=j: e.scalar_tensor_tensor(out=Cst[:, 0, :], in0=Cst[:, 0, :], scalar=M_("WOLD", j),
                                                        in1=pB[:, :], op0=ALU.mult, op1=ALU.add),
                  r=["pB", "Cst"] + GM, w=["Cst"])
                V(lambda e, j=j: e.scalar_tensor_tensor(out=Cst[:, 1, :], in0=Cst[:, 1, :], scalar=M_("WOLD", j),
                                                        in1=pC[:, :], op0=ALU.mult, op1=ALU.add),
                  r=["pC", "Cst"] + GM, w=["Cst"])
                A(lambda e: e.copy(out=Cb[:, :, :].rearrange("p a b -> p (a b)"),
                                   in_=Cst[:, :, :].rearrange("p a b -> p (a b)")), r=["Cst"], w=["Cb"])
                V(lambda e, j=j: e.scalar_tensor_tensor(out=nst[:, :], in0=nst[:, :], scalar=M_("WOLD", j),
                                                        in1=pE[:, 8:10], op0=ALU.mult, op1=ALU.add),
                  r=["pE", "nst"] + GM, w=["nst"])
                A(lambda e: e.copy(out=nbb[:, :], in_=nst[:, :]), r=["nst"], w=["nbb"])
            for i in range(NSUB):
                tok = P.dma("sp", mix[r0 + i * 128:r0 + (i + 1) * 128, :], mixo[:, i, :], reads=[("mixo", i)],
                            slot=("mixo", i))
                out_tok.append(tok)
        final = {}
        for tk in out_tok:
            final[tk[1]] = tk
        P.emit(list(final.values()))
    return nc


OFF_MQ, OFF_MK, OFF_MV, OFF_MO, OFF_MI, OFF_MF = 0, 1024, 2048, 4096, 6144, 6148
OFF_GQ, OFF_GK, OFF_GV, OFF_GZ, OFF_GA, OFF_GB = 6152, 6152 + 2048, 6152 + 4096, 12296, 14344, 14360


def make_masks():
    j = np.arange(128)[:, None]
    l = np.arange(128)[None, :]
    U = (j <= l).astype(np.float32)
    Wm = (j > l).astype(np.float32)
    I = np.eye(128, dtype=np.float32)
    ones = np.ones((128, 128), np.float32)
    NEG = np.where(l < j, -30000.0, 0.0).astype(np.float32)
    STRICT = (l > j).astype(np.float32)
    return np.ascontiguousarray(np.concatenate([U, Wm, I, ones] + [NEG] * 4 + [STRICT] * 4 + [I] * 4, axis=1))


def rep128(v):
    v = np.asarray(v, np.float32).reshape(1, -1)
    return np.ascontiguousarray(np.broadcast_to(v, (128, v.shape[1])))


def phase1_group_inputs(g, w_in, norm1_g, i_bias, f_bias, mnorm_g, conv_w, a_log, dt_bias, gnorm_g):
    sl = lambda o, n: slice(o, o + n)
    w_fm = np.concatenate([w_in[:, sl(OFF_MQ + g * 256, 256)], w_in[:, sl(OFF_MK + g * 256, 256)],
                           w_in[:, sl(OFF_GQ + g * 512, 512)], w_in[:, sl(OFF_GK + g * 512, 512)],
                           w_in[:, sl(OFF_GV + g * 512, 512)]], axis=1)
    w_tm = np.concatenate([w_in[:, sl(OFF_MV + g * 512, 512)], w_in[:, sl(OFF_MO + g * 512, 512)],
                           w_in[:, sl(OFF_GZ + g * 512, 512)]], axis=1)
    w_sm = np.concatenate([w_in[:, sl(OFF_MI + g, 1)], w_in[:, sl(OFF_MF + g, 1)], w_in[:, sl(OFF_GA + 4 * g, 4)],
                           w_in[:, sl(OFF_GB + 4 * g, 4)]], axis=1)
    cst = np.zeros((128, 64), np.float32)
    cst[:, 0] = i_bias[g]
    cst[:, 1] = f_bias[g]
    cst[:, 2] = 1.0
    cst[:, 3] = NORM_EPS
    cst[:, 4] = L2_EPS
    cst[:, 16:32] = np.tile(a_log[4 * g:4 * g + 4], 4)[None, :]
    cst[:, 32:48] = np.tile(dt_bias[4 * g:4 * g + 4], 4)[None, :]
    convw = np.zeros((128, 48), np.float32)
    for s in range(3):
        for h in range(4):
            c0 = s * 2048 + (4 * g + h) * 128
            convw[:, (s * 4 + h) * 4:(s * 4 + h) * 4 + 4] = conv_w[:, c0:c0 + 128].T
    return dict(w_fm=np.ascontiguousarray(w_fm), w_tm=np.ascontiguousarray(w_tm), w_sm=np.ascontiguousarray(w_sm),
                g1rep=rep128(norm1_g), cst=cst, gmrep=rep128(mnorm_g[g * 512:(g + 1) * 512]),
                gnrep=rep128(np.tile(gnorm_g, 4)), convw=convw, masks=make_masks(), identb=_ident(NPBF16))


_NC_CACHE = {}


def _get_nc(key, builder):
    if key not in _NC_CACHE:
        _NC_CACHE[key] = builder()
    return _NC_CACHE[key]


def kernel_two_launch(x, norm1_g, w_in, mlstm_i_bias, mlstm_f_bias, mlstm_norm_g, gdn_conv_w, gdn_a_log, gdn_dt_bias,
           gdn_norm_g, w_out, norm2_g, w_up, w_down, norm_f_g):
    x = np.asarray(x, np.float32)
    B, T, D = x.shape
    F = w_up.shape[-1]
    f32 = lambda a: np.ascontiguousarray(np.asarray(a, np.float32))
    w_in0, w_out0, w_up0, w_down0 = f32(w_in[0]), f32(w_out[0]), f32(w_up[0]), f32(w_down[0])
    nc1 = _get_nc(("p1", D, T), lambda: build_phase1(D, T))
    groups = [phase1_group_inputs(g, w_in0, f32(norm1_g[0]), f32(mlstm_i_bias[0]), f32(mlstm_f_bias[0]),
                                  f32(mlstm_norm_g[0]), f32(gdn_conv_w[0]), f32(gdn_a_log[0]), f32(gdn_dt_bias[0]),
                                  f32(gdn_norm_g[0])) for g in range(4)]
    in1 = []
    for c in range(NCORES):
        d = dict(groups[c % 4])
        d["x"] = np.ascontiguousarray(x[c // 4])
        in1.append(d)
    r1 = run_bass_kernel_spmd(nc1, in1, core_ids=list(range(NCORES)))
    NTOK = B * T // NCORES
    mixT = np.empty((NCORES, D, NTOK), NPBF16)
    for c in range(NCORES):
        b, q = (c * NTOK) // T, (c * NTOK) % T
        for g in range(4):
            m = r1.results[b * 4 + g]["mix"][q:q + NTOK]
            mixT[c, g * 512:(g + 1) * 512, :] = m[:, 0:512].T
            mixT[c, 2048 + g * 512:2048 + (g + 1) * 512, :] = m[:, 512:1024].T
    nc2 = _get_nc(("p2", D, F, NTOK), lambda: build_phase2(D, F, NTOK))
    g2rep, gfrep, ident = rep128(norm2_g[0]), rep128(norm_f_g), _ident(NPBF16)
    xf = x.reshape(NCORES, NTOK, D)
    in2 = [dict(x=np.ascontiguousarray(xf[c]), mixT=mixT[c], w_out=w_out0, w_up=w_up0, w_down=w_down0,
                g2rep=g2rep, gfrep=gfrep, identb=ident) for c in range(NCORES)]
    r2 = run_bass_kernel_spmd(nc2, in2, core_ids=list(range(NCORES)))
    y = np.stack([r2.results[c]["y"] for c in range(NCORES)]).reshape(B, T, D)
    return y.astype(np.float32)


def _prod(sh):
    n = 1
    for v in sh:
        n *= v
    return n


def _carve(base, off, shape, dt):
    n = _prod(shape)
    if dt == BF16:
        words = (n + 1) // 2
        v = base[:, off:off + words].bitcast(BF16)
        if 2 * words != n:
            v = v[:, 0:n]
    else:
        words = n
        v = base[:, off:off + n]
    if len(shape) == 2:
        v = v.rearrange("p (a b) -> p a b", b=shape[1])
    elif len(shape) == 3:
        v = v.rearrange("p (a b c) -> p a b c", b=shape[1], c=shape[2])
    return v, off + words


def build_fused(D, F, TP, TO, TILE=512):
    KC = D // 128
    NSUB = TILE // 128
    NTP, NTO = TP // TILE, TO // TILE
    NG = 4
    nc = bass.Bass("TRN2", target_bir_lowering=False)

    def din(name, shape, dt=F32):
        return nc.dram_tensor(name, shape, dt, kind="ExternalInput").ap()

    xp = din("xp", [max(TP, 128), D])
    xo = din("xo", [TO, D])
    w_fm = din("w_fm", [D, NG * FM_COLS])
    w_tm = din("w_tm", [D, NG * TM_COLS])
    w_sm = din("w_sm", [D, NG * SM_COLS])
    g1rep_d = din("g1rep", [128, D])
    cst_d = din("cst", [128, NG * 64])
    gmrep_d = din("gmrep", [128, NG * 512])
    gnrep_d = din("gnrep", [128, 512])
    convw_d = din("convw", [128, NG * 48])
    masks_d = din("masks", [128, 4 * 128 + 3 * 512])
    identb_d = din("identb", [128, 128], BF16)
    DM, KM = 4096, 32
    w_out = din("w_out", [DM, D])
    w_up = din("w_up", [D, F])
    w_down = din("w_down", [F, D])
    g2_d = din("g2rep", [128, D])
    gf_d = din("gfrep", [128, D])
    y = nc.dram_tensor("y", [TO, D], F32, kind="ExternalOutput").ap()
    mix_d = nc.dram_tensor("mix_scratch", [TO, DM], BF16).ap()

    P = Prog(nc)
    out_tok = []
    with contextlib.ExitStack() as es_all:
        with contextlib.ExitStack() as es:
            def ps(name, shape, dt=F32):
                return es.enter_context(nc.psum_tensor("f1_" + name, shape, dt))[:, :]

            def V(fn, r=(), w=()):
                return P.op("dve", fn, reads=r, writes=w)

            def A(fn, r=(), w=()):
                return P.op("act", fn, reads=r, writes=w)

            def MM(ps_ap, key, pairs, reads):
                n = len(pairs)
                for i, (l, r_) in enumerate(pairs):
                    P.op("pe", lambda e, l=l, r_=r_, i=i: e.matmul(ps_ap, l, r_, start=(i == 0), stop=(i == n - 1)),
                         reads=reads, writes=[key])

            def TR(ps_ap, key, in_ap, ident, reads):
                P.op("pe", lambda e: e.transpose(ps_ap, in_ap, ident), reads=reads, writes=[key])

            AW = 53100
            arena_t = es.enter_context(nc.sbuf_tensor("arena1", [128, AW], F32))
            arena = arena_t[:, :]
            off = [0]

            def al(shape, dt=F32):
                v, off[0] = _carve(arena, off[0], shape, dt)
                assert off[0] <= AW, "phase-1 arena overflow"
                return v

            def sub(parent, words0, shape, dt=F32):
                v, _ = _carve(parent, words0, shape, dt)
                return v

            cst4 = al([NG, 64])
            drv4 = al([NG, 32])
            gnrep = al([512])
            convw4 = al([NG, 48])
            masks = al([4 * 128 + 3 * 512])
            idb = al([128], BF16)
            onesb = al([2], BF16)
            wsm = al([KC, NG * SM_COLS], BF16)
            CstG = [al([2, 512]) for _ in range(NG)]
            nstG = [al([2]) for _ in range(NG)]
            S4G = [al([4, 128]) for _ in range(NG)]
            haloG = [al([3, 4, 3]) for _ in range(NG)]
            Cb = al([2, 512], BF16)
            nbb = al([2], BF16)
            S4b = al([4, 128], BF16)
            U = masks[:, 0:128]
            Wm = masks[:, 128:256]
            I32 = masks[:, 256:384]
            ones32 = masks[:, 384:512]
            NEG4 = masks[:, 512:1024]
            STRICT4 = masks[:, 1024:1536]
            I4 = masks[:, 1536:2048]
            xs = al([D])
            assert D >= 4096 or True
            if D >= 4096:
                ovb = xs
                P.alias("xs", ["ET4", "X", "Y", ("P", 0), ("P", 1), ("P", 2), ("P", 3)])
            else:
                ovb = al([4096])
            ET4 = sub(ovb, 0, [4, 128])
            Xc = sub(ovb, 512, [4, 128])
            Yc = sub(ovb, 1024, [4, 128])
            P4 = [sub(ovb, 1536 + k_ * 512, [4, 128]) for k_ in range(4)]
            xb_words = max(D // 2, 3968)
            xbp = al([xb_words])
            xb = sub(xbp, 0, [D], BF16)
            sqb = sub(xbp, 0, [4, TILE], BF16)
            at0 = [sub(xbp, 1024, [4, 128], BF16), sub(xbp, 1280, [4, 128], BF16)]
            kdec4 = [sub(xbp, 1536, [4, 128], BF16), sub(xbp, 1792, [4, 128], BF16)]
            vnew = sub(xbp, 2048, [4, 128], BF16)
            smT = [sub(xbp, 2304, [128], BF16), sub(xbp, 2368, [128], BF16)]
            kw = [sub(xbp, 2432, [256], BF16), sub(xbp, 2560, [256], BF16)]
            Pbb = [sub(xbp, 2688, [4, 128], BF16), sub(xbp, 2944, [4, 128], BF16)]
            assert 3200 <= xb_words and 4 * TILE // 2 <= 1024
            P.alias("xb", ["sqb", ("at0", 0), ("at0", 1), ("kdec4", 0), ("kdec4", 1), "vnew", ("smT", 0), ("smT", 1),
                           ("kw", 0), ("kw", 1), "Pb0", "Pb1"])
            g1w = max(D, 12 * TILE // 2 + NSUB * 256)
            g1p = al([g1w])
            g1rep = sub(g1p, 0, [D])
            Gpost = sub(g1p, 0, [12, TILE], BF16)
            Vm = sub(g1p, 12 * TILE // 2, [NSUB, 512], BF16)
            P.alias("g1rep", [("Gpost", 0), ("Gpost", 1), ("Gpost", 2), "Vm"])
            xnT = al([KC, TILE], BF16)
            wb = [al([KC * 256], BF16) for _ in range(2)]
            QmT = al([2, TILE], BF16)
            KmT = al([2, TILE], BF16)
            Gpre = al([4, TILE + 3])
            Og = al([NSUB, 512], BF16)
            Zg = al([NSUB, 512], BF16)
            smg = al([NSUB, NG * SM_COLS])
            cacc = [al([TILE]) for _ in range(2)]
            assert TILE >= 512
            tmpg = [al([256]) for _ in range(2)]
            gt = al([16, 16])
            gm_ = al([12, 4])
            mixo = al([NSUB, 1024], BF16)
            st = al([16])
            ETS = al([4, 128])
            gU4 = ETS
            vsc4 = [al([4, 128]), sub(cacc[0], 0, [4, 128])]
            P.alias(("cacc", 0), [("vsc4", 1)])
            R0 = al([4, 128])
            t1 = al([512])
            opre = al([4, 128])
            num = al([512])
            lfU = al([128])
            EmT = al([128])
            sc = al([32])
            gmrep = al([512])
            print("phase-1 arena words used:", off[0], "of", AW)
            pM = [ps("pM%d" % i, [128, 512]) for i in range(2)]
            pA = ps("pA", [128, 512])
            pB = ps("pB", [128, 512])
            pC = ps("pC", [128, 512])
            pD = ps("pD", [128, 512])
            pE = ps("pE", [128, 512])
            pT = ps("pT", [128, 1024], BF16)

            for i, (dst, src, k) in enumerate([(cst4, cst_d.rearrange("p (g c) -> p g c", c=64), "cst"),
                                               (gnrep, gnrep_d, "gnrep"),
                                               (convw4, convw_d.rearrange("p (g c) -> p g c", c=48), "convw"),
                                               (masks, masks_d, "masks"), (idb, identb_d, "idb")]):
                P.dma("sp", dst, src, writes=[k], slot=("c", i))
            P.dma("pool", wsm, w_sm.rearrange("(c p) n -> p c n", p=128), writes=["wsm"], slot=("c", "wsm"))
            V(lambda e: e.memset(onesb, 1.0), w=["onesb"])
            for g in range(NG):
                V(lambda e, g=g: e.memset(CstG[g], 0.0), w=[("Cst", g)])
                V(lambda e, g=g: e.memset(nstG[g], 0.0), w=[("nst", g)])
                V(lambda e, g=g: e.memset(S4G[g], 0.0), w=[("S4", g)])
                V(lambda e, g=g: e.memset(haloG[g], 0.0), w=[("halo", g)])
            V(lambda e: e.tensor_scalar(out=drv4[:, :, 0:2], in0=cst4[:, :, 0:2], scalar1=1.0 / GATE_CAP, scalar2=None,
                                        op0=ALU.mult), r=["cst"], w=["drv"])
            A(lambda e: e.activation(out=drv4[:, :, 16:32], in_=cst4[:, :, 16:32], func=AF.Exp), r=["cst"], w=["drvA"])

            def fm_list(t_):
                if t_ >= NTP:
                    return tuple(range(FM_COLS // 256))
                return (1, 2, 3, 4, 5, 6, 7) if t_ == NTP - 1 else (1, 4, 5, 6, 7)

            ws = WStream(P, wb, "wb")
            wview = lambda b: b.rearrange("p (c n) -> p c n", n=256)
            for t in range(NTP + NTO):
                own = t >= NTP
                for g in range(NG):
                    for pc in [4, 5] + [pc_ for pc_ in fm_list(t) if pc_ not in (4, 5)]:
                        c0 = g * FM_COLS + pc * 256
                        ws.add(w_fm[:, c0:c0 + 256].rearrange("(c p) n -> p c n", p=128), wview)
                    for pc in (range(TM_COLS // 256) if own else (0, 1)):
                        c0 = g * TM_COLS + pc * 256
                        ws.add(w_tm[:, c0:c0 + 256].rearrange("(c p) n -> p c n", p=128), wview)
            wk = [0]

            def next_w():
                k = wk[0]
                wk[0] += 1
                buf, key = ws.acquire(k)
                return wview(buf), key

            xkeys = [("xnT", i) for i in range(NSUB)]
            GTK = dict(BETA=0, GG=1, RQ=2, NK=3, RK=4, NCC=5, BRK=6, DEC=7, EDEC=8, NEDEC=9, C1=10, EDL=11, RQ2=12,
                       TMP=13, TMP2=14, DL=15)
            GMK = dict(LI=0, LF=1, BC=2, BL=3, EK=4, EGS=5, WOLD=6, TMP=7, TMP2=8)

            def G(kind, j=None, hh=None):
                k = GTK[kind]
                if j is None:
                    return gt[:, k, :]
                if hh is None:
                    return gt[:, k, j * 4:(j + 1) * 4]
                return gt[:, k, j * 4 + hh:j * 4 + hh + 1]

            def M_(kind, j=None):
                k = GMK[kind]
                if j is None:
                    return gm_[:, k, :]
                return gm_[:, k, j:j + 1]

            flat = lambda ap: ap.rearrange("p h l -> p (h l)")
            GT, GM = ["gt"], ["gm"]

            for t in range(NTP + NTO):
                own = t >= NTP
                xsrc = xo if own else xp
                r0 = (t - NTP) * TILE if own else t * TILE
                P.dma("sp", g1rep, g1rep_d, writes=["g1rep"], slot="g1rep")
                for i in range(NSUB):
                    P.dma("sp", xs, xsrc[r0 + i * 128:r0 + (i + 1) * 128, :], writes=["xs"], slot="xs")
                    A(lambda e: e.activation(out=xb, in_=xs, func=AF.Square, accum_out=st[:, 0:1]),
                      r=["xs"], w=["xb", "st0"])
                    A(lambda e: e.activation(out=st[:, 1:2], in_=st[:, 0:1], func=AF.Sqrt, scale=1.0 / D,
                                             bias=cst4[:, 0, 3:4]), r=["st0", "cst"], w=["st1"])
                    V(lambda e: e.reciprocal(out=st[:, 2:3], in_=st[:, 1:2]), r=["st1"], w=["st2"])
                    V(lambda e: e.scalar_tensor_tensor(out=xb, in0=xs, scalar=st[:, 2:3], in1=g1rep,
                                                       op0=ALU.mult, op1=ALU.mult), r=["xs", "st2", "g1rep"], w=["xb"])
                    for c8 in range(0, KC, 8):
                        n8 = min(8, KC - c8)
                        for c in range(n8):
                            TR(pT[:, c * 128:(c + 1) * 128], "pT", xb[:, (c8 + c) * 128:(c8 + c + 1) * 128], idb,
                               ["xb", "idb"])
                        A(lambda e, c8=c8, n8=n8, i=i: e.copy(out=xnT[:, c8:c8 + n8, i * 128:(i + 1) * 128],
                                                              in_=pT[:, 0:n8 * 128].rearrange("p (c n) -> p c n", n=128)),
                          r=["pT"], w=[("xnT", i)])
                for i in range(NSUB):
                    pm = pM[i % 2]
                    pk = ("pM", i % 2)
                    MM(pm[:, 0:NG * SM_COLS], pk, [(xnT[:, kc, i * 128:(i + 1) * 128], wsm[:, kc, :]) for kc in range(KC)],
                       [("xnT", i), "wsm"])
                    A(lambda e, i=i, pm=pm: e.copy(out=smg[:, i, :], in_=pm[:, 0:NG * SM_COLS]), r=[pk], w=["smg"])
                for g in range(NG):
                    Cst, nst, S4, halo = CstG[g], nstG[g], S4G[g], haloG[g]
                    sg0 = g * SM_COLS
                    kC, kn, kS, kh = ("Cst", g), ("nst", g), ("S4", g), ("halo", g)
                    cst, drv, convw = cst4[:, g, :], drv4[:, g, :], convw4[:, g, :]
                    onec, epsc, epsl2 = cst[:, 2:3], cst[:, 3:4], cst[:, 4:5]
                    A(lambda e, Cst=Cst: e.copy(out=Cb.rearrange("p a b -> p (a b)"),
                                                in_=Cst.rearrange("p a b -> p (a b)")), r=[kC], w=["Cb"])
                    A(lambda e, nst=nst: e.copy(out=nbb, in_=nst), r=[kn], w=["nbb"])
                    A(lambda e, S4=S4: e.copy(out=flat(S4b), in_=flat(S4)), r=[kS], w=["S4b"])
                    if own:
                        P.dma("sp", gmrep, gmrep_d[:, g * 512:(g + 1) * 512], writes=["gmrep"], slot="gmrep")
                    def emit_fm(pcs, own=own, halo=halo, kh=kh, convw=convw):
                        for pc in pcs:
                            wv, key = next_w()
                            for half in range(2):
                                oc = pc * 2 + half
                                pm = pM[oc % 2]
                                pk = ("pM", oc % 2)
                                MM(pm[:, 0:TILE], pk,
                                   [(wv[:, kc, half * 128:(half + 1) * 128], xnT[:, kc, :]) for kc in range(KC)],
                                   xkeys + [key])
                                if oc < 2:
                                    A(lambda e, oc=oc, pm=pm: e.copy(out=QmT[:, oc, :], in_=pm[:, 0:TILE]), r=[pk], w=["QmT"])
                                elif oc < 4:
                                    A(lambda e, oc=oc, pm=pm: e.mul(out=KmT[:, oc - 2, :], in_=pm[:, 0:TILE], mul=1.0 / 16.0),
                                      r=[pk], w=["KmT"])
                                else:
                                    s_, hh = (oc - 4) // 4, (oc - 4) % 4
                                    A(lambda e, hh=hh, pm=pm: e.copy(out=Gpre[:, hh, 3:3 + TILE], in_=pm[:, 0:TILE]),
                                      r=[pk], w=[("Gpre", hh)])
                                    if hh == 3:
                                        do_conv = own or s_ != 0
                                        if do_conv:
                                            V(lambda e, s_=s_, halo=halo: e.tensor_copy(out=Gpre[:, :, 0:3],
                                                                                        in_=halo[:, s_, :, :]),
                                              r=[kh], w=[("Gpre", h_) for h_ in range(4)])
                                        for h_ in (range(4) if do_conv else ()):
                                            ca = cacc[h_ % 2]
                                            ck = ("cacc", h_ % 2)
                                            wcol = lambda j_, s_=s_, h_=h_, convw=convw: \
                                                convw[:, (s_ * 4 + h_) * 4 + j_:(s_ * 4 + h_) * 4 + j_ + 1]
                                            V(lambda e, h_=h_, ca=ca, wcol=wcol: e.tensor_scalar(
                                                out=ca, in0=Gpre[:, h_, 0:TILE], scalar1=wcol(0), scalar2=None,
                                                op0=ALU.mult), r=[("Gpre", h_), "convw"], w=[ck])
                                            for j_ in range(1, 4):
                                                V(lambda e, h_=h_, ca=ca, wcol=wcol, j_=j_: e.scalar_tensor_tensor(
                                                    out=ca, in0=Gpre[:, h_, j_:j_ + TILE], scalar=wcol(j_), in1=ca,
                                                    op0=ALU.mult, op1=ALU.add), r=[("Gpre", h_), ck], w=[ck])
                                            A(lambda e, s_=s_, h_=h_, ca=ca: e.activation(out=Gpost[:, s_ * 4 + h_, :], in_=ca,
                                                                                        func=AF.Silu),
                                              r=[ck], w=[("Gpost", s_)])
                                        V(lambda e, s_=s_, halo=halo: e.tensor_copy(out=halo[:, s_, :, :],
                                                                                    in_=Gpre[:, :, TILE:TILE + 3]),
                                          r=[("Gpre", h_) for h_ in range(4)], w=[kh])
                    def emit_tm(own=own):
                        for pc in (range(TM_COLS // 256) if own else (0, 1)):
                            wv, key = next_w()
                            kind, cpc = pc // 2, pc % 2
                            for i in range(NSUB):
                                pm = pM[i % 2]
                                pk = ("pM", i % 2)
                                MM(pm[:, 0:256], pk, [(xnT[:, kc, i * 128:(i + 1) * 128], wv[:, kc, :]) for kc in range(KC)],
                                   [("xnT", i), key])
                                cs = slice(cpc * 256, (cpc + 1) * 256)
                                if kind == 0:
                                    A(lambda e, i=i, pm=pm, cs=cs: e.copy(out=Vm[:, i, cs], in_=pm[:, 0:256]), r=[pk], w=["Vm"])
                                else:
                                    tg = tmpg[i % 2]
                                    tk = ("tmpg", i % 2)
                                    fn = AF.Sigmoid if kind == 1 else AF.Silu
                                    A(lambda e, pm=pm, tg=tg, fn=fn: e.activation(out=tg, in_=pm[:, 0:256], func=fn),
                                      r=[pk], w=[tk])
                                    dst = Og if kind == 1 else Zg
                                    gsrc = gmrep if kind == 1 else gnrep
                                    V(lambda e, i=i, tg=tg, cs=cs, dst=dst, gsrc=gsrc: e.tensor_tensor(
                                        out=dst[:, i, cs], in0=tg, in1=gsrc[:, cs], op=ALU.mult),
                                      r=[tk, "gmrep", "gnrep"], w=["Og" if kind == 1 else "Zg"])

                    def gate_prep1(own=own, drv=drv, cst=cst, onec=onec, epsl2=epsl2, sg0=sg0):
                        A(lambda e, drv=drv, sg0=sg0: e.activation(out=M_("TMP"), in_=smg[:, :, sg0], func=AF.Tanh, scale=1.0 / GATE_CAP,
                                                          bias=drv[:, 0:1]), r=["smg", "drv"], w=GM)
                        V(lambda e: e.tensor_scalar(out=M_("LI"), in0=M_("TMP"), scalar1=GATE_CAP, scalar2=None,
                                                    op0=ALU.mult), r=GM, w=GM)
                        A(lambda e, drv=drv, sg0=sg0: e.activation(out=M_("TMP"), in_=smg[:, :, sg0 + 1], func=AF.Tanh, scale=1.0 / GATE_CAP,
                                                          bias=drv[:, 1:2]), r=["smg", "drv"] + GM, w=GM)
                        A(lambda e: e.activation(out=M_("TMP2"), in_=M_("TMP"), func=AF.Exp, scale=-GATE_CAP), r=GM, w=GM)
                        A(lambda e, onec=onec: e.activation(out=M_("TMP"), in_=M_("TMP2"), func=AF.Ln, bias=onec),
                          r=GM + ["cst"], w=GM)
                        V(lambda e: e.tensor_scalar(out=M_("LF"), in0=M_("TMP"), scalar1=-1.0, scalar2=None, op0=ALU.mult),
                          r=GM, w=GM)
                        MM(pE[:, 0:4], "pE", [(U, M_("LF"))], GM + ["masks"])
                        V(lambda e: e.tensor_copy(out=M_("BC"), in_=pE[:, 0:4]), r=["pE"], w=GM)
                        MM(pE[:, 4:8], "pE", [(ones32, M_("LF"))], GM + ["masks"])
                        V(lambda e: e.tensor_copy(out=M_("BL"), in_=pE[:, 4:8]), r=["pE"], w=GM)
                        A(lambda e: e.activation(out=M_("EK"), in_=M_("BC"), func=AF.Exp), r=GM, w=GM)
                        A(lambda e: e.activation(out=M_("WOLD"), in_=M_("BL"), func=AF.Exp), r=GM, w=GM)
                        V(lambda e: e.tensor_tensor(out=M_("TMP"), in0=M_("BL"), in1=M_("BC"), op=ALU.subtract), r=GM, w=GM)
                        V(lambda e: e.tensor_tensor(out=M_("TMP2"), in0=M_("TMP"), in1=M_("LI"), op=ALU.add), r=GM, w=GM)
                        A(lambda e: e.activation(out=M_("EGS"), in_=M_("TMP2"), func=AF.Exp), r=GM, w=GM)
                        g16 = lambda ap: ap.rearrange("p (s h) -> p s h", h=4)
                        A(lambda e, sg0=sg0: e.activation(out=g16(G("BETA")), in_=smg[:, :, sg0 + 6:sg0 + 10], func=AF.Sigmoid), r=["smg"], w=GT)
                        V(lambda e, cst=cst, sg0=sg0: e.tensor_tensor(out=g16(G("TMP")), in0=smg[:, :, sg0 + 2:sg0 + 6], in1=g16(cst[:, 32:48]),
                                                             op=ALU.add), r=["smg", "cst"] + GT, w=GT)
                        V(lambda e: e.tensor_scalar(out=G("TMP2"), in0=G("TMP"), scalar1=-1.0, scalar2=None, op0=ALU.mult),
                          r=GT, w=GT)
                        V(lambda e: e.tensor_tensor(out=G("TMP2"), in0=G("TMP2"), in1=G("TMP"), op=ALU.max), r=GT, w=GT)
                        A(lambda e: e.activation(out=G("GG"), in_=G("TMP2"), func=AF.Exp, scale=-1.0), r=GT, w=GT)
                        A(lambda e, onec=onec: e.activation(out=G("TMP2"), in_=G("GG"), func=AF.Ln, bias=onec),
                          r=GT + ["cst"], w=GT)
                        V(lambda e: e.tensor_scalar(out=G("GG"), in0=G("TMP"), scalar1=0.0, scalar2=None, op0=ALU.max),
                          r=GT, w=GT)
                        V(lambda e: e.tensor_tensor(out=G("TMP"), in0=G("GG"), in1=G("TMP2"), op=ALU.add), r=GT, w=GT)
                        V(lambda e, drv=drv: e.scalar_tensor_tensor(out=G("GG"), in0=G("TMP"), scalar=-1.0, in1=drv[:, 16:32],
                                                                    op0=ALU.mult, op1=ALU.mult), r=GT + ["drvA"], w=GT)
                        MM(pE[:, 16:32], "pE", [(U, G("GG"))], GT + ["masks"])
                        V(lambda e: e.tensor_copy(out=G("DEC"), in_=pE[:, 16:32]), r=["pE"], w=GT)
                        MM(pE[:, 32:48], "pE", [(ones32, G("GG"))], GT + ["masks"])
                        V(lambda e: e.tensor_copy(out=G("DL"), in_=pE[:, 32:48]), r=["pE"], w=GT)
                        A(lambda e: e.activation(out=G("EDEC"), in_=G("DEC"), func=AF.Exp), r=GT, w=GT)
                        A(lambda e: e.activation(out=G("EDL"), in_=G("DL"), func=AF.Exp), r=GT, w=GT)
                        V(lambda e: e.tensor_scalar(out=G("NEDEC"), in0=G("EDEC"), scalar1=-1.0, scalar2=None, op0=ALU.mult),
                          r=GT, w=GT)
                        V(lambda e: e.tensor_tensor(out=G("TMP"), in0=G("DL"), in1=G("DEC"), op=ALU.subtract), r=GT, w=GT)
                        A(lambda e: e.activation(out=G("C1"), in_=G("TMP"), func=AF.Exp), r=GT, w=GT)
                        for s_, dstk in ((1, "NK"),):
                            V(lambda e, s_=s_: e.tensor_tensor(out=sqb, in0=Gpost[:, s_ * 4:(s_ + 1) * 4, :],
                                                               in1=Gpost[:, s_ * 4:(s_ + 1) * 4, :], op=ALU.mult),
                              r=[("Gpost", s_)], w=["sqb"])
                            for j in range(NSUB):
                                for hh in range(4):
                                    c = 64 + j * 4 + hh
                                    MM(pE[:, c:c + 1], "pE", [(sqb[:, hh, j * 128:(j + 1) * 128], onesb[:, 0:1])],
                                       ["sqb", "onesb"])
                            A(lambda e, dstk=dstk, epsl2=epsl2: e.activation(out=G(dstk), in_=pE[:, 64:80], func=AF.Sqrt,
                                                                           bias=epsl2), r=["pE", "cst"] + GT, w=GT)
                        V(lambda e: e.reciprocal(out=G("RK"), in_=G("NK")), r=GT, w=GT)
                        V(lambda e: e.tensor_tensor(out=G("BRK"), in0=G("BETA"), in1=G("RK"), op=ALU.mult), r=GT, w=GT)
                        V(lambda e: e.scalar_tensor_tensor(out=G("NCC"), in0=G("BRK"), scalar=-1.0, in1=G("RK"),
                                                           op0=ALU.mult, op1=ALU.mult), r=GT, w=GT)
                        V(lambda e: e.tensor_tensor(out=G("TMP"), in0=G("C1"), in1=G("RK"), op=ALU.mult), r=GT, w=GT)
                        V(lambda e: e.tensor_copy(out=G("C1"), in_=G("TMP")), r=GT, w=GT)

                    def gate_prep2(epsl2=epsl2):
                        for s_, dstk in ((0, "RQ"),):
                            V(lambda e, s_=s_: e.tensor_tensor(out=sqb, in0=Gpost[:, s_ * 4:(s_ + 1) * 4, :],
                                                               in1=Gpost[:, s_ * 4:(s_ + 1) * 4, :], op=ALU.mult),
                              r=[("Gpost", s_)], w=["sqb"])
                            for j in range(NSUB):
                                for hh in range(4):
                                    c = 64 + j * 4 + hh
                                    MM(pE[:, c:c + 1], "pE", [(sqb[:, hh, j * 128:(j + 1) * 128], onesb[:, 0:1])],
                                       ["sqb", "onesb"])
                            A(lambda e, dstk=dstk, epsl2=epsl2: e.activation(out=G(dstk), in_=pE[:, 64:80], func=AF.Sqrt,
                                                                           bias=epsl2), r=["pE", "cst"] + GT, w=GT)
                        if True:
                            V(lambda e: e.reciprocal(out=G("TMP"), in_=G("RQ")), r=GT, w=GT)
                            V(lambda e: e.tensor_scalar(out=G("RQ"), in0=G("TMP"), scalar1=128.0 ** -0.5, scalar2=None,
                                                        op0=ALU.mult), r=GT, w=GT)
                            V(lambda e: e.tensor_tensor(out=G("TMP"), in0=G("RQ"), in1=G("RQ"), op=ALU.mult), r=GT, w=GT)
                            V(lambda e: e.tensor_scalar(out=G("RQ2"), in0=G("TMP"), scalar1=1.0 / 128.0, scalar2=None,
                                                        op0=ALU.mult), r=GT, w=GT)
                    def chainA(j):
                        cs = slice(j * 128, (j + 1) * 128)
                        Pj, pk = P4[j], ("P", j)
                        for hh in range(4):
                            V(lambda e, hh=hh: e.tensor_scalar(out=gU4[:, hh, :], in0=U, scalar1=G("GG", j, hh),
                                                               scalar2=None, op0=ALU.mult), r=GT + ["masks"], w=["ETS"])
                        MM(pA, "pA", [(Wm, flat(gU4)), (I32, NEG4)], ["ETS", "masks"])
                        A(lambda e: e.activation(out=flat(ET4), in_=pA, func=AF.Exp), r=["pA"], w=["ET4"])
                        for hh in range(4):
                            MM(pB[:, hh * 128:(hh + 1) * 128], "pB", [(Gpost[:, 4 + hh, cs], Gpost[:, 4 + hh, cs])],
                               [("Gpost", 1)])
                        V(lambda e: e.tensor_tensor(out=flat(ETS), in0=flat(ET4), in1=STRICT4, op=ALU.mult),
                          r=["ET4", "masks"], w=["ETS"])
                        for hh in range(4):
                            V(lambda e, hh=hh: e.scalar_tensor_tensor(
                                out=Xc[:, hh, :], in0=pB[:, hh * 128:(hh + 1) * 128], scalar=G("NCC", j, hh),
                                in1=ETS[:, hh, :], op0=ALU.mult, op1=ALU.mult), r=["pB", "ETS"] + GT, w=["X"])
                        for hh in range(4):
                            TR(pD[:, hh * 128:(hh + 1) * 128], "pD", Xc[:, hh, :], I32, ["X", "masks"])
                        A(lambda e: e.copy(out=flat(Yc), in_=pD), r=["pD"], w=["Y"])
                        V(lambda e: e.tensor_tensor(out=flat(Pj), in0=flat(Xc), in1=I4, op=ALU.add),
                          r=["X", "masks"], w=[pk])
                        for lv in range(1, NLEV + 1):
                            last = (lv == NLEV)
                            for hh in range(4):
                                MM(pD[:, hh * 128:(hh + 1) * 128], "pD", [(Xc[:, hh, :], Yc[:, hh, :])], ["X", "Y"])
                            if not last:
                                for hh in range(4):
                                    MM(pB[:, hh * 128:(hh + 1) * 128], "pB", [(Yc[:, hh, :], Xc[:, hh, :])], ["X", "Y"])
                            V(lambda e: e.tensor_copy(out=flat(Yc), in_=pD), r=["pD"], w=["Y"])
                            if not last:
                                A(lambda e: e.copy(out=flat(Xc), in_=pB), r=["pB"], w=["X"])
                            for hh in range(4):
                                MM(pC[:, hh * 128:(hh + 1) * 128], "pC", [(Yc[:, hh, :], Pj[:, hh, :])], ["Y", pk])
                            V(lambda e: e.tensor_tensor(out=flat(Pj), in0=pC, in1=flat(Pj), op=ALU.add),
                              r=["pC", pk], w=[pk])

                    def late(j, own=own):
                        jp = j % 2
                        cs = slice(j * 128, (j + 1) * 128)
                        if own:
                            for hh in range(4):
                                V(lambda e, hh=hh: e.tensor_scalar(out=gU4[:, hh, :], in0=U, scalar1=G("GG", j, hh),
                                                                   scalar2=None, op0=ALU.mult), r=GT + ["masks"], w=["ETS"])
                            MM(pA, "pA", [(Wm, flat(gU4)), (I32, NEG4)], ["ETS", "masks"])
                            A(lambda e: e.activation(out=flat(ET4), in_=pA, func=AF.Exp), r=["pA"], w=["ET4"])
                            for hh in range(4):
                                MM(pC[:, hh * 128:(hh + 1) * 128], "pC", [(Gpost[:, 4 + hh, cs], Gpost[:, hh, cs])],
                                   [("Gpost", 1), ("Gpost", 0)])
                            for hh in range(4):
                                V(lambda e, hh=hh: e.scalar_tensor_tensor(
                                    out=at0[jp][:, hh, :], in0=pC[:, hh * 128:(hh + 1) * 128], scalar=G("RK", j, hh),
                                    in1=ET4[:, hh, :], op0=ALU.mult, op1=ALU.mult), r=["pC", "ET4"] + GT, w=[("at0", jp)])
                            V(lambda e: e.tensor_scalar(out=lfU, in0=U, scalar1=M_("LF", j), scalar2=None, op0=ALU.mult),
                              r=GM + ["masks"], w=["lfU"])
                            MM(pB[:, 0:128], "pB", [(Wm, lfU), (I32, NEG4[:, 0:128])], ["lfU", "masks"])
                            A(lambda e: e.activation(out=EmT, in_=pB[:, 0:128], func=AF.Exp, bias=M_("LI", j)),
                              r=["pB"] + GM, w=["EmT"])
                            MM(pD[:, 0:128], "pD", [(KmT[:, 0, cs], QmT[:, 0, cs]), (KmT[:, 1, cs], QmT[:, 1, cs])],
                               ["KmT", "QmT"])
                            V(lambda e: e.tensor_tensor(out=smT[jp], in0=pD[:, 0:128], in1=EmT, op=ALU.mult),
                              r=["pD", "EmT"], w=[("smT", jp)])
                        for dc in range(2):
                            TR(pT[:, dc * 128:(dc + 1) * 128], "pT", KmT[:, dc, cs], idb, ["KmT", "idb"])
                        A(lambda e: e.activation(out=kw[jp], in_=pT[:, 0:256], func=AF.Copy, scale=M_("EGS", j)),
                          r=["pT"] + GM, w=[("kw", jp)])
                        for hh in range(4):
                            TR(pT[:, hh * 128:(hh + 1) * 128], "pT", Gpost[:, 4 + hh, cs], idb, [("Gpost", 1), "idb"])
                        for hh in range(4):
                            TR(pT[:, 512 + hh * 128:512 + (hh + 1) * 128], "pT", Gpost[:, 8 + hh, cs], idb,
                               [("Gpost", 2), "idb"])
                        for hh in range(4):
                            A(lambda e, hh=hh: e.activation(out=kdec4[jp][:, hh, :], in_=pT[:, hh * 128:(hh + 1) * 128],
                                                            func=AF.Copy, scale=G("C1", j, hh)),
                              r=["pT"] + GT, w=[("kdec4", jp)])
                            A(lambda e, hh=hh: e.activation(out=vsc4[jp][:, hh, :],
                                                            in_=pT[:, 512 + hh * 128:512 + (hh + 1) * 128],
                                                            func=AF.Copy, scale=G("NK", j, hh)),
                              r=["pT"] + GT, w=[("vsc4", jp)])

                    def dep(j, S4=S4, Cst=Cst, nst=nst, kS=kS, kC=kC, kn=kn, epsc=epsc, own=own):
                        jp = j % 2
                        cs = slice(j * 128, (j + 1) * 128)
                        NT_ = P4[j]
                        ntk = ("P", j)
                        q0, q1 = pM[0], pM[1]
                        k0, k1 = ("pM", 0), ("pM", 1)
                        for hh in range(4):
                            MM(q0[:, hh * 128:(hh + 1) * 128], k0, [(Gpost[:, 4 + hh, cs], S4b[:, hh, :])],
                               [("Gpost", 1), "S4b"])
                        if own:
                            for hh in range(4):
                                MM(pE[:, hh * 128:(hh + 1) * 128], "pE", [(Gpost[:, hh, cs], S4b[:, hh, :])],
                                   [("Gpost", 0), "S4b"])
                        for hh in range(4):
                            V(lambda e, hh=hh: e.scalar_tensor_tensor(
                                out=R0[:, hh, :], in0=q0[:, hh * 128:(hh + 1) * 128], scalar=G("NEDEC", j, hh),
                                in1=vsc4[jp][:, hh, :], op0=ALU.mult, op1=ALU.add), r=[k0, ("vsc4", jp)] + GT, w=["R0"])
                        for hh in range(4):
                            MM(q1[:, hh * 128:(hh + 1) * 128], k1, [(NT_[:, hh, :], R0[:, hh, :])], [ntk, "R0"])
                        for hh in range(4):
                            A(lambda e, hh=hh: e.activation(out=vnew[:, hh, :], in_=q1[:, hh * 128:(hh + 1) * 128],
                                                            func=AF.Copy, scale=G("BRK", j, hh)), r=[k1] + GT, w=["vnew"])
                        if own:
                            for hh in range(4):
                                MM(q0[:, hh * 128:(hh + 1) * 128], k0, [(at0[jp][:, hh, :], vnew[:, hh, :])],
                                   [("at0", jp), "vnew"])
                            for hh in range(4):
                                A(lambda e, hh=hh: e.activation(out=t1[:, hh * 128:(hh + 1) * 128],
                                                                in_=pE[:, hh * 128:(hh + 1) * 128], func=AF.Copy,
                                                                scale=G("EDEC", j, hh)), r=["pE"] + GT, w=["t1"])
                            V(lambda e: e.tensor_tensor(out=flat(opre), in0=q0, in1=t1, op=ALU.add),
                              r=[k0, "t1"], w=["opre"])
                        for hh in range(4):
                            MM(q1[:, hh * 128:(hh + 1) * 128], k1, [(kdec4[jp][:, hh, :], vnew[:, hh, :])],
                               [("kdec4", jp), "vnew"])
                        for hh in range(4):
                            V(lambda e, hh=hh: e.scalar_tensor_tensor(
                                out=S4[:, hh, :], in0=S4[:, hh, :], scalar=G("EDL", j, hh),
                                in1=q1[:, hh * 128:(hh + 1) * 128], op0=ALU.mult, op1=ALU.add),
                              r=[k1, kS] + GT, w=[kS])
                        A(lambda e: e.copy(out=flat(S4b), in_=flat(S4)), r=[kS], w=["S4b"])
                        if own:
                            V(lambda e: e.tensor_tensor(out=R0, in0=opre, in1=opre, op=ALU.mult), r=["opre"], w=["R0"])
                            V(lambda e: e.tensor_reduce(out=sc[:, 0:4], in_=R0, axis=mybir.AxisListType.X, op=ALU.add),
                              r=["R0"], w=["sc0"])
                            V(lambda e: e.tensor_tensor(out=sc[:, 4:8], in0=sc[:, 0:4], in1=G("RQ2", j), op=ALU.mult),
                              r=["sc0"] + GT, w=["sc1"])
                            A(lambda e: e.activation(out=sc[:, 8:12], in_=sc[:, 4:8], func=AF.Ln, bias=epsc),
                              r=["sc1", "cst"], w=["sc2"])
                            A(lambda e: e.activation(out=sc[:, 12:16], in_=sc[:, 8:12], func=AF.Exp, scale=-0.5),
                              r=["sc2"], w=["sc3"])
                            V(lambda e: e.tensor_tensor(out=sc[:, 16:20], in0=sc[:, 12:16], in1=G("RQ", j), op=ALU.mult),
                              r=["sc3"] + GT, w=["sc4"])
                            for hh in range(4):
                                V(lambda e, hh=hh: e.scalar_tensor_tensor(
                                    out=mixo[:, j, 512 + hh * 128:512 + (hh + 1) * 128], in0=opre[:, hh, :],
                                    scalar=sc[:, 16 + hh:17 + hh], in1=Zg[:, j, hh * 128:(hh + 1) * 128],
                                    op0=ALU.mult, op1=ALU.mult), r=["opre", "sc4", "Zg"], w=[("mixo", j)])
                            MM(q0, k0, [(smT[jp], Vm[:, j, :])], [("smT", jp), "Vm"])
                            MM(q1, k1, [(QmT[:, 0, cs], Cb[:, 0, :]), (QmT[:, 1, cs], Cb[:, 1, :])], ["QmT", "Cb"])
                            MM(pE[:, 0:1], "pE", [(QmT[:, 0, cs], nbb[:, 0:1]), (QmT[:, 1, cs], nbb[:, 1:2])],
                               ["QmT", "nbb"])
                            MM(pE[:, 1:2], "pE", [(smT[jp], onesb[:, 0:1])], [("smT", jp), "onesb"])
                            A(lambda e: e.activation(out=t1, in_=q1, func=AF.Copy, scale=M_("EK", j)),
                              r=[k1] + GM, w=["t1"])
                            V(lambda e: e.tensor_tensor(out=num, in0=q0, in1=t1, op=ALU.add), r=[k0, "t1"], w=["num"])
                            A(lambda e: e.copy(out=sc[:, 20:22], in_=pE[:, 0:2]), r=["pE"], w=["sc5"])
                            V(lambda e: e.scalar_tensor_tensor(out=sc[:, 22:23], in0=sc[:, 20:21], scalar=M_("EK", j),
                                                               in1=sc[:, 21:22], op0=ALU.mult, op1=ALU.add),
                              r=["sc5"] + GM, w=["sc6"])
                            V(lambda e: e.tensor_scalar(out=sc[:, 23:24], in0=sc[:, 22:23], scalar1=-1.0, scalar2=None,
                                                        op0=ALU.mult), r=["sc6"], w=["sc7"])
                            V(lambda e: e.tensor_tensor(out=sc[:, 23:24], in0=sc[:, 23:24], in1=sc[:, 22:23], op=ALU.max),
                              r=["sc6", "sc7"], w=["sc7"])
                            V(lambda e: e.tensor_scalar(out=sc[:, 23:24], in0=sc[:, 23:24], scalar1=1.0, scalar2=None,
                                                        op0=ALU.max), r=["sc7"], w=["sc7"])
                            V(lambda e: e.reciprocal(out=sc[:, 24:25], in_=sc[:, 23:24]), r=["sc7"], w=["sc8"])
                            A(lambda e: e.activation(out=t1, in_=num, func=AF.Square, accum_out=sc[:, 25:26]),
                              r=["num"], w=["t1", "sc9"])
                            V(lambda e: e.tensor_tensor(out=sc[:, 26:27], in0=sc[:, 24:25], in1=sc[:, 24:25], op=ALU.mult),
                              r=["sc8"], w=["sc10"])
                            V(lambda e: e.tensor_tensor(out=sc[:, 27:28], in0=sc[:, 26:27], in1=sc[:, 25:26], op=ALU.mult),
                              r=["sc10", "sc9"], w=["sc11"])
                            A(lambda e: e.activation(out=sc[:, 28:29], in_=sc[:, 27:28], func=AF.Ln, scale=1.0 / 512.0,
                                                     bias=epsc), r=["sc11", "cst"], w=["sc12"])
                            A(lambda e: e.activation(out=sc[:, 29:30], in_=sc[:, 28:29], func=AF.Exp, scale=-0.5),
                              r=["sc12"], w=["sc13"])
                            V(lambda e: e.tensor_tensor(out=sc[:, 30:31], in0=sc[:, 29:30], in1=sc[:, 24:25], op=ALU.mult),
                              r=["sc13", "sc8"], w=["sc14"])
                            V(lambda e: e.scalar_tensor_tensor(out=mixo[:, j, 0:512], in0=num, scalar=sc[:, 30:31],
                                                               in1=Og[:, j, :], op0=ALU.mult, op1=ALU.mult),
                              r=["num", "sc14", "Og"], w=[("mixo", j)])
                        MM(q0, k0, [(kw[jp][:, 0:128], Vm[:, j, :])], [("kw", jp), "Vm"])
                        MM(q1, k1, [(kw[jp][:, 128:256], Vm[:, j, :])], [("kw", jp), "Vm"])
                        MM(pE[:, 8:9], "pE", [(kw[jp][:, 0:128], onesb[:, 0:1])], [("kw", jp), "onesb"])
                        MM(pE[:, 9:10], "pE", [(kw[jp][:, 128:256], onesb[:, 0:1])], [("kw", jp), "onesb"])
                        V(lambda e: e.scalar_tensor_tensor(out=Cst[:, 0, :], in0=Cst[:, 0, :], scalar=M_("WOLD", j),
                                                           in1=q0, op0=ALU.mult, op1=ALU.add), r=[k0, kC] + GM, w=[kC])
                        V(lambda e: e.scalar_tensor_tensor(out=Cst[:, 1, :], in0=Cst[:, 1, :], scalar=M_("WOLD", j),
                                                           in1=q1, op0=ALU.mult, op1=ALU.add), r=[k1, kC] + GM, w=[kC])
                        A(lambda e: e.copy(out=Cb.rearrange("p a b -> p (a b)"), in_=Cst.rearrange("p a b -> p (a b)")),
                          r=[kC], w=["Cb"])
                        V(lambda e: e.scalar_tensor_tensor(out=nst, in0=nst, scalar=M_("WOLD", j), in1=pE[:, 8:10],
                                                           op0=ALU.mult, op1=ALU.add), r=["pE", kn] + GM, w=[kn])
                        A(lambda e: e.copy(out=nbb, in_=nst), r=[kn], w=["nbb"])

                    gk_p = (4, 5)
                    emit_fm(gk_p)
                    gate_prep1()
                    P.begin_capture()
                    emit_fm([pc for pc in fm_list(t) if pc not in gk_p])
                    emit_tm()
                    rest_ = P.end_capture()
                    P.begin_capture()
                    for j in range(NSUB):
                        chainA(j)
                    chains_ = P.end_capture()
                    P.play(rest_, chains_)
                    if own:
                        gate_prep2()
                    lates, deps = [], []
                    for j in range(NSUB):
                        P.begin_capture()
                        late(j)
                        lates.append(P.end_capture())
                        P.begin_capture()
                        dep(j)
                        deps.append(P.end_capture())
                    P.play(lates[0])
                    for j in range(NSUB):
                        P.play(deps[j], lates[j + 1] if j + 1 < NSUB else [])
                    if own:
                        for i in range(NSUB):
                            rr = slice(r0 + i * 128, r0 + (i + 1) * 128)
                            P.dma("sp", mix_d[rr, g * 512:(g + 1) * 512], mixo[:, i, 0:512], reads=[("mixo", i)],
                                  writes=["mix_d"], slot=("mixo", i, 0))
                            P.dma("sp", mix_d[rr, 2048 + g * 512:2048 + (g + 1) * 512], mixo[:, i, 512:1024],
                                  reads=[("mixo", i)], writes=["mix_d"], slot=("mixo", i, 1))

        P.barrier()
        with contextlib.ExitStack() as es:
            def sb(name, shape, dt):
                return es.enter_context(nc.sbuf_tensor("f2_" + name, shape, dt))

            def ps2(name, shape, dt=F32):
                return es.enter_context(nc.psum_tensor("f2_" + name, shape, dt))

            NT = NTO
            FG = min(F, 2048)
            NFG = F // FG
            FCG = FG // 128
            CG = D // 512
            RG = 2
            KCR = KM // RG
            h = sb("h", [128, NSUB, D], F32)
            actT = sb("actT", [128, max(KC, KM), TILE], BF16)
            upT = sb("upT", [128, FCG, TILE], BF16)
            WB = max(KC * 256, FCG * 512, KCR * 512)
            wb2 = [sb("wb%d" % i, [128, WB], BF16) for i in range(3)]
            grep = sb("grep", [128, D], F32)
            hb = sb("hb", [128, max(D, DM)], BF16)
            rl = [sb("rl%d" % i, [128, TILE], F32) for i in range(2)]
            idb2 = sb("idb", [128, 128], BF16)
            st2 = sb("st", [128, 8], F32)
            epsc2 = sb("epsc", [128, 1], F32)
            ps_o = [ps2("ps_o%d" % i, [128, 512]) for i in range(4)]
            ps_u = [ps2("ps_u%d" % i, [128, 512]) for i in range(2)]
            ps_t = ps2("ps_t", [128, 1024], BF16)
            P.dma("sp", idb2[:, :], identb_d, writes=["idb2"], slot="c_id2")
            P.op("dve", lambda e: e.memset(epsc2[:, :], NORM_EPS), writes=["epsc2"])
            ws2 = WStream(P, wb2, "wb2")
            for t in range(NT):
                for cg in range(CG):
                    for rg in range(RG):
                        src = w_out[rg * KCR * 128:(rg + 1) * KCR * 128, cg * 512:(cg + 1) * 512] \
                            .rearrange("(c p) n -> p c n", p=128)
                        ws2.add(src, lambda b: b[:, 0:KCR * 512].rearrange("p (c n) -> p c n", n=512))
                for fg in range(NFG):
                    for pc in range(FG // 256):
                        c0 = fg * FG + pc * 256
                        src = w_up[:, c0:c0 + 256].rearrange("(c p) n -> p c n", p=128)
                        ws2.add(src, lambda b: b[:, 0:KC * 256].rearrange("p (c n) -> p c n", n=256))
                    for cg in range(CG):
                        src = w_down[fg * FG:(fg + 1) * FG, cg * 512:(cg + 1) * 512] \
                            .rearrange("(c p) n -> p c n", p=128)
                        ws2.add(src, lambda b: b[:, 0:FCG * 512].rearrange("p (c n) -> p c n", n=512))
            wk2 = [0]

            def next_w2():
                k = wk2[0]
                wk2[0] += 1
                return ws2.acquire(k)

            def transpose_hb(i, nch):
                for c8 in range(0, nch, 8):
                    n8 = min(8, nch - c8)
                    for c in range(n8):
                        P.op("pe", lambda e, c=c, c8=c8: e.transpose(ps_t[:, c * 128:(c + 1) * 128],
                                                                     hb[:, (c8 + c) * 128:(c8 + c + 1) * 128],
                                                                     idb2[:, :]),
                             reads=["hb", "idb2"], writes=["ps_t"])
                    P.op("act", lambda e, c8=c8, n8=n8, i=i: e.copy(
                        out=actT[:, c8:c8 + n8, i * 128:(i + 1) * 128],
                        in_=ps_t[:, 0:n8 * 128].rearrange("p (c n) -> p c n", n=128)),
                         reads=["ps_t"], writes=[("actT", i)])

            def rms_scale(i, c0, gkey):
                P.op("act", lambda e, i=i: e.activation(out=hb[:, 0:D], in_=h[:, i, :], func=AF.Square,
                                                      accum_out=st2[:, c0:c0 + 1]),
                     reads=[("h", i)], writes=["hb", ("st2", c0)])
                P.op("act", lambda e: e.activation(out=st2[:, c0 + 1:c0 + 2], in_=st2[:, c0:c0 + 1], func=AF.Sqrt,
                                                   scale=1.0 / D, bias=epsc2[:, 0:1]),
                     reads=[("st2", c0), "epsc2"], writes=[("st2", c0 + 1)])
                P.op("dve", lambda e: e.reciprocal(out=st2[:, c0 + 2:c0 + 3], in_=st2[:, c0 + 1:c0 + 2]),
                     reads=[("st2", c0 + 1)], writes=[("st2", c0 + 2)])

            for t in range(NT):
                r0 = t * TILE
                for i in range(NSUB):
                    P.dma("sp", h[:, i, :], xo[r0 + i * 128:r0 + (i + 1) * 128, :], writes=[("h", i)], slot=("h", i))
                P.dma("sp", grep[:, :], g2_d, writes=["grep"], slot="grep")
                for i in range(NSUB):
                    P.dma("sp", hb[:, 0:DM], mix_d[r0 + i * 128:r0 + (i + 1) * 128, :], reads=["mix_d"], writes=["hb"],
                          slot="hb")
                    transpose_hb(i, KM)
                for cg in range(CG):
                    for rg in range(RG):
                        buf, key = next_w2()
                        wv = buf[:, 0:KCR * 512].rearrange("p (c n) -> p c n", n=512)
                        for i in range(NSUB):
                            for c in range(KCR):
                                kc = rg * KCR + c
                                first = (rg == 0 and c == 0)
                                last = (rg == RG - 1 and c == KCR - 1)
                                P.op("pe", lambda e, i=i, kc=kc, c=c, wv=wv, first=first, last=last: e.matmul(
                                    ps_o[i][:, :], actT[:, kc, i * 128:(i + 1) * 128], wv[:, c, :],
                                    start=first, stop=last), reads=[("actT", i), key], writes=[("ps_o", i)])
                    for i in range(NSUB):
                        P.op("dve", lambda e, i=i, cg=cg: e.tensor_tensor(
                            out=h[:, i, cg * 512:(cg + 1) * 512], in0=ps_o[i][:, :],
                            in1=h[:, i, cg * 512:(cg + 1) * 512], op=ALU.add),
                             reads=[("ps_o", i), ("h", i)], writes=[("h", i)])
                for i in range(NSUB):
                    rms_scale(i, 0, "grep")
                    P.op("dve", lambda e, i=i: e.scalar_tensor_tensor(out=hb[:, 0:D], in0=h[:, i, :], scalar=st2[:, 2:3],
                                                                    in1=grep[:, :], op0=ALU.mult, op1=ALU.mult),
                         reads=[("h", i), ("st2", 2), "grep"], writes=["hb"])
                    transpose_hb(i, KC)
                for fg in range(NFG):
                    for pc in range(FG // 256):
                        buf, key = next_w2()
                        wv = buf[:, 0:KC * 256].rearrange("p (c n) -> p c n", n=256)
                        for half in range(2):
                            fc = pc * 2 + half
                            pu = ps_u[fc % 2]
                            for kc in range(KC):
                                P.op("pe", lambda e, kc=kc, wv=wv, half=half, pu=pu: e.matmul(
                                    pu[:, 0:TILE], wv[:, kc, half * 128:(half + 1) * 128], actT[:, kc, :],
                                    start=(kc == 0), stop=(kc == KC - 1)),
                                     reads=[("actT", i) for i in range(NSUB)] + [key], writes=[("ps_u", fc % 2)])
                            r = rl[fc % 2]
                            P.op("act", lambda e, pu=pu, r=r: e.activation(out=r[:, :], in_=pu[:, 0:TILE], func=AF.Relu),
                                 reads=[("ps_u", fc % 2)], writes=[("rl", fc % 2)])
                            P.op("dve", lambda e, r=r, fc=fc: e.tensor_tensor(out=upT[:, fc, :], in0=r[:, :], in1=r[:, :],
                                                                            op=ALU.mult),
                                 reads=[("rl", fc % 2)], writes=["upT"])
                    for cg in range(CG):
                        buf, key = next_w2()
                        wv = buf[:, 0:FCG * 512].rearrange("p (c n) -> p c n", n=512)
                        for i in range(NSUB):
                            for fc in range(FCG):
                                P.op("pe", lambda e, i=i, fc=fc, wv=wv: e.matmul(
                                    ps_o[i][:, :], upT[:, fc, i * 128:(i + 1) * 128], wv[:, fc, :],
                                    start=(fc == 0), stop=(fc == FCG - 1)), reads=["upT", key], writes=[("ps_o", i)])
                            P.op("dve", lambda e, i=i, cg=cg: e.tensor_tensor(
                                out=h[:, i, cg * 512:(cg + 1) * 512], in0=ps_o[i][:, :],
                                in1=h[:, i, cg * 512:(cg + 1) * 512], op=ALU.add),
                                 reads=[("ps_o", i), ("h", i)], writes=[("h", i)])
                P.dma("sp", grep[:, :], gf_d, writes=["grep"], slot="grep")
                for i in range(NSUB):
                    rms_scale(i, 4, "grep")
                    P.op("dve", lambda e, i=i: e.scalar_tensor_tensor(out=h[:, i, :], in0=h[:, i, :], scalar=st2[:, 6:7],
                                                                    in1=grep[:, :], op0=ALU.mult, op1=ALU.mult),
                         reads=[("h", i), ("st2", 6), "grep"], writes=[("h", i)])
                    tok = P.dma("sp", y[r0 + i * 128:r0 + (i + 1) * 128, :], h[:, i, :], reads=[("h", i)],
                                slot=("yout", i))
                    out_tok.append(tok)
            final = {}
            for tk in out_tok:
                final[tk[1]] = tk
            P.emit(list(final.values()))
    return nc


def fused_inputs(s_idx, xb_full, TP, TO, shared):
    T, D = xb_full.shape
    o = s_idx * TO
    xp = np.zeros((max(TP, 128), D), np.float32)
    if o > 0:
        xp[TP - o:TP] = xb_full[0:o]
    d = dict(shared)
    d["xp"] = xp
    d["xo"] = np.ascontiguousarray(xb_full[o:o + TO])
    return d


def fused_shared_inputs(w_in, norm1_g, i_bias, f_bias, mnorm_g, conv_w, a_log, dt_bias, gnorm_g, w_out, w_up, w_down,
                        norm2_g, norm_f_g):
    gs = [phase1_group_inputs(g, w_in, norm1_g, i_bias, f_bias, mnorm_g, conv_w, a_log, dt_bias, gnorm_g)
          for g in range(4)]
    cat = lambda k: np.ascontiguousarray(np.concatenate([g_[k] for g_ in gs], axis=1))
    return dict(w_fm=cat("w_fm"), w_tm=cat("w_tm"), w_sm=cat("w_sm"), g1rep=gs[0]["g1rep"], cst=cat("cst"),
                gmrep=cat("gmrep"), gnrep=gs[0]["gnrep"], convw=cat("convw"), masks=gs[0]["masks"],
                identb=gs[0]["identb"], w_out=w_out, w_up=w_up, w_down=w_down, g2rep=rep128(norm2_g),
                gfrep=rep128(norm_f_g))


def kernel(x, norm1_g, w_in, mlstm_i_bias, mlstm_f_bias, mlstm_norm_g, gdn_conv_w, gdn_a_log, gdn_dt_bias,
           gdn_norm_g, w_out, norm2_g, w_up, w_down, norm_f_g):
    x = np.asarray(x, np.float32)
    B, T, D = x.shape
    F = w_up.shape[-1]
    f32 = lambda a: np.ascontiguousarray(np.asarray(a, np.float32))
    per_b = NCORES // B
    TO = T // per_b
    TP = (per_b - 1) * TO
    nc = _get_nc(("fused", D, F, TP, TO), lambda: build_fused(D, F, TP, TO))
    shared = fused_shared_inputs(f32(w_in[0]), f32(norm1_g[0]), f32(mlstm_i_bias[0]), f32(mlstm_f_bias[0]),
                                 f32(mlstm_norm_g[0]), f32(gdn_conv_w[0]), f32(gdn_a_log[0]), f32(gdn_dt_bias[0]),
                                 f32(gdn_norm_g[0]), f32(w_out[0]), f32(w_up[0]), f32(w_down[0]), f32(norm2_g[0]),
                                 f32(norm_f_g))
    in_maps = [fused_inputs(c % per_b, x[c // per_b], TP, TO, shared) for c in range(NCORES)]
    res = run_bass_kernel_spmd(nc, in_maps, core_ids=list(range(NCORES)))
    y = np.stack([res.results[c]["y"] for c in range(NCORES)]).reshape(B, T, D)
    return y.astype(np.float32)
```

```python
import contextlib
import numpy as np
import ml_dtypes
import concourse.bass as bass
import concourse.mybir as mybir
from concourse.bass_utils import run_bass_kernel_spmd

F32 = mybir.dt.float32
BF16 = mybir.dt.bfloat16
ALU = mybir.AluOpType
AF = mybir.ActivationFunctionType
NPBF16 = ml_dtypes.bfloat16

NCORES = 8
NORM_EPS = 1e-6
L2_EPS = 1e-6
GATE_CAP = 15.0


class Prog:
    ENGS = ("pe", "act", "dve", "pool", "sp")

    def __init__(self, nc):
        self.nc = nc
        self.ops = {e: [] for e in self.ENGS}
        self.nseq = {e: 0 for e in self.ENGS}
        self.last_write = {}
        self.readers = {}
        self.waited = {e: {} for e in self.ENGS}
        self.dma_slots = {}
        self.marked = {e: set() for e in self.ENGS}
        self.kids = {}
        self.parent = {}
        self.pending = {}
        self.attach_waits = True
        self._cap = None

    def alias(self, parent, kids):
        self.kids.setdefault(parent, []).extend(kids)
        for k in kids:
            self.parent[k] = parent

    def _expand(self, keys):
        out = []
        for k in keys:
            out.append(k)
            if k in self.parent:
                out.append(self.parent[k])
            out.extend(self.kids.get(k, ()))
        return out

    def _deps(self, eng, reads, writes):
        toks = []
        for k in self._expand(list(reads) + list(writes)):
            t = self.last_write.get(k)
            if t is not None:
                toks.append((t, "raw"))
        for k in self._expand(writes):
            for t in self.readers.get(k, {}).values():
                toks.append((t, "war"))
        waits = []
        w = self.waited[eng]
        for t, kind in toks:
            if t[0] == "eng":
                _, src, seq = t
                if src == eng and (eng == "pe" or kind == "war"):
                    continue
                if w.get(("eng", src), 0) >= seq:
                    continue
                w[("eng", src)] = seq
                self.marked[src].add(seq)
                waits.append(t)
            else:
                _, slot, cnt = t
                if w.get(("dma", slot), 0) >= cnt:
                    continue
                w[("dma", slot)] = cnt
                waits.append(t)
        return waits

    def _commit(self, tok, reads, writes):
        rk = (tok[0], tok[1])
        for k in reads:
            self.readers.setdefault(k, {})[rk] = tok
        for k in writes:
            self.last_write[k] = tok
            self.readers[k] = {}

    def barrier(self):
        toks = [("eng", e, self.nseq[e]) for e in self.ENGS if self.nseq[e] > 0]
        toks += [("dma", s_, v[1]) for s_, v in self.dma_slots.items()]
        self.pending = {e: list(toks) for e in self.ENGS}

    def _take_pending(self, eng):
        out = []
        w = self.waited[eng]
        for t in self.pending.pop(eng, ()):
            if t[0] == "eng":
                _, src, seq = t
                if src == eng or w.get(("eng", src), 0) >= seq:
                    continue
                w[("eng", src)] = seq
                self.marked[src].add(seq)
                out.append(t)
            else:
                _, slot, cnt = t
                if w.get(("dma", slot), 0) >= cnt:
                    continue
                w[("dma", slot)] = cnt
                out.append(t)
        return out

    def begin_capture(self):
        self._cap = []

    def end_capture(self):
        c, self._cap = self._cap, None
        return c

    def play(self, *lists):
        lists = [l for l in lists if l]
        idx = [0] * len(lists)
        total = sum(len(l) for l in lists)
        for _ in range(total):
            k = min((i for i in range(len(lists)) if idx[i] < len(lists[i])),
                    key=lambda i: (idx[i] + 1) / len(lists[i]))
            kind, a = lists[k][idx[k]]
            idx[k] += 1
            if kind == "op":
                self.op(*a)
            else:
                self.dma(*a)

    def op(self, eng, fn, reads=(), writes=()):
        if self._cap is not None:
            self._cap.append(("op", (eng, fn, tuple(reads), tuple(writes))))
            return None
        waits = self._take_pending(eng) + self._deps(eng, reads, writes)
        self.nseq[eng] += 1
        seq = self.nseq[eng]
        tok = ("eng", eng, seq)
        self.ops[eng].append(dict(fn=fn, waits=waits, seq=seq, dma=None))
        self._commit(tok, reads, writes)
        return tok

    def dma(self, eng, out, in_, reads=(), writes=(), slot=None):
        if self._cap is not None:
            self._cap.append(("dma", (eng, out, in_, tuple(reads), tuple(writes), slot)))
            return None
        waits = self._take_pending(eng) + self._deps(eng, reads, writes)
        st = self.dma_slots.setdefault(slot, [len(self.dma_slots), 0])
        st[1] += 16
        tok = ("dma", slot, st[1])
        self.ops[eng].append(dict(fn=lambda e, o=out, i=in_: e.dma_start(out=o, in_=i), waits=waits, seq=None,
                                  dma=slot))
        self._commit(tok, reads, writes)
        return tok

    def emit(self, final_tokens):
        nc = self.nc
        with contextlib.ExitStack() as es:
            esem = {e: es.enter_context(nc.semaphore("sem_" + e)) for e in self.ENGS}
            dsem = {s: es.enter_context(nc.semaphore("dsem_%d" % v[0])) for s, v in self.dma_slots.items()}
            rank = {}
            for e in self.ENGS:
                m = sorted(self.marked[e])
                rank[e] = {s: i + 1 for i, s in enumerate(m)}
            block = es.enter_context(nc.Block())

            def semval(t):
                if t[0] == "eng":
                    return esem[t[1]], rank[t[1]][t[2]]
                return dsem[t[1]], t[2]

            def replay(ename, eng):
                for o in self.ops[ename]:
                    waits = o["waits"]
                    attach = None
                    if waits and o["dma"] is None and self.attach_waits:
                        attach, waits = waits[-1], waits[:-1]
                    for t in waits:
                        eng.wait_ge(*semval(t))
                    ins = o["fn"](eng)
                    if attach is not None:
                        ins._wait_ge(*semval(attach))
                    if o["dma"] is not None:
                        ins.then_inc(dsem[o["dma"]], 16)
                    elif o["seq"] in rank[ename]:
                        ins.then_inc(esem[ename], 1)
                if ename == "sp":
                    for t in final_tokens:
                        eng.wait_ge(dsem[t[1]], t[2])

            @block.tensor
            def _(e):
                replay("pe", e)

            @block.scalar
            def _(e):
                replay("act", e)

            @block.vector
            def _(e):
                replay("dve", e)

            @block.gpsimd
            def _(e):
                replay("pool", e)

            @block.sync
            def _(e):
                replay("sp", e)


class WStream:
    def __init__(self, P, bufs, name, eng="pool"):
        self.P, self.bufs, self.name, self.eng = P, bufs, name, eng
        self.pieces = []
        self.issued = 0

    def add(self, src_ap, view):
        self.pieces.append((src_ap, view))
        return len(self.pieces) - 1

    def acquire(self, k):
        nb = len(self.bufs)
        while self.issued < min(len(self.pieces), k + nb):
            i = self.issued
            src, view = self.pieces[i]
            b = i % nb
            self.P.dma(self.eng, view(self.bufs[b]), src, writes=[(self.name, b)], slot=(self.name, b))
            self.issued += 1
        return self.bufs[k % nb], (self.name, k % nb)


def _mm_group(P, ps_ap, ps_key, pairs, reads):
    n = len(pairs)
    for i, (l, r) in enumerate(pairs):
        P.op("pe", lambda e, l=l, r=r, i=i: e.matmul(ps_ap, l, r, start=(i == 0), stop=(i == n - 1)),
             reads=reads, writes=[ps_key])


def build_phase2(D, F, NTOK, TILE=512):
    KC = D // 128
    NSUB = TILE // 128
    NT = NTOK // TILE
    FG = min(F, 2048)
    NFG = F // FG
    FCG = FG // 128
    CG = D // 512
    RG = 2 if KC >= 2 else 1
    KCR = KC // RG
    nc = bass.Bass("TRN2", target_bir_lowering=False)
    x = nc.dram_tensor("x", [NTOK, D], F32, kind="ExternalInput").ap()
    mixT = nc.dram_tensor("mixT", [D, NTOK], BF16, kind="ExternalInput").ap()
    w_out = nc.dram_tensor("w_out", [D, D], F32, kind="ExternalInput").ap()
    w_up = nc.dram_tensor("w_up", [D, F], F32, kind="ExternalInput").ap()
    w_down = nc.dram_tensor("w_down", [F, D], F32, kind="ExternalInput").ap()
    g2 = nc.dram_tensor("g2rep", [128, D], F32, kind="ExternalInput").ap()
    gf = nc.dram_tensor("gfrep", [128, D], F32, kind="ExternalInput").ap()
    identb = nc.dram_tensor("identb", [128, 128], BF16, kind="ExternalInput").ap()
    y = nc.dram_tensor("y", [NTOK, D], F32, kind="ExternalOutput").ap()

    P = Prog(nc)
    with contextlib.ExitStack() as es:
        def sb(name, shape, dt):
            return es.enter_context(nc.sbuf_tensor("sb_" + name, shape, dt))

        def ps(name, shape, dt=F32):
            return es.enter_context(nc.psum_tensor("ps_" + name, shape, dt))

        h = sb("h", [128, NSUB, D], F32)
        actT = sb("actT", [128, KC, TILE], BF16)
        upT = sb("upT", [128, FCG, TILE], BF16)
        WB = max(KC * 256, FCG * 512, KCR * 512)
        wb = [sb("wb%d" % i, [128, WB], BF16) for i in range(3)]
        grep = sb("grep", [128, D], F32)
        hb = sb("hb", [128, D], BF16)
        rl = [sb("rl%d" % i, [128, TILE], F32) for i in range(2)]
        idb = sb("idb", [128, 128], BF16)
        st = sb("st", [128, 8], F32)
        ps_o = [ps("ps_o%d" % i, [128, 512]) for i in range(4)]
        ps_u = [ps("ps_u%d" % i, [128, 512]) for i in range(2)]
        ps_t = ps("ps_t", [128, 1024], BF16)

        epsc = sb("epsc", [128, 1], F32)
        P.dma("sp", idb[:, :], identb, writes=["idb"], slot="c_id")
        P.op("dve", lambda e: e.memset(epsc[:, :], NORM_EPS), writes=["epsc"])

        ws = WStream(P, wb, "wb")
        sched = []
        for t in range(NT):
            for cg in range(CG):
                for rg in range(RG):
                    src = w_out[rg * KCR * 128:(rg + 1) * KCR * 128, cg * 512:(cg + 1) * 512] \
                        .rearrange("(c p) n -> p c n", p=128)
                    ws.add(src, lambda b: b[:, 0:KCR * 512].rearrange("p (c n) -> p c n", n=512))
            for fg in range(NFG):
                for pc in range(FG // 256):
                    c0 = fg * FG + pc * 256
                    src = w_up[:, c0:c0 + 256].rearrange("(c p) n -> p c n", p=128)
                    ws.add(src, lambda b: b[:, 0:KC * 256].rearrange("p (c n) -> p c n", n=256))
                for cg in range(CG):
                    src = w_down[fg * FG:(fg + 1) * FG, cg * 512:(cg + 1) * 512] \
                        .rearrange("(c p) n -> p c n", p=128)
                    ws.add(src, lambda b: b[:, 0:FCG * 512].rearrange("p (c n) -> p c n", n=512))
        wk = [0]

        def next_w():
            k = wk[0]
            wk[0] += 1
            return ws.acquire(k)

        def rmsnorm_to_T(t, gain_key):
            for i in range(NSUB):
                P.op("act", lambda e, i=i: e.activation(out=hb[:, :], in_=h[:, i, :], func=AF.Square,
                                                      accum_out=st[:, 0:1]),
                     reads=[("h", i)], writes=["hb", "st0"])
                P.op("act", lambda e: e.activation(out=st[:, 1:2], in_=st[:, 0:1], func=AF.Sqrt, scale=1.0 / D,
                                                   bias=epsc[:, 0:1]),
                     reads=["st0", "epsc"], writes=["st1"])
                P.op("dve", lambda e: e.reciprocal(out=st[:, 2:3], in_=st[:, 1:2]),
                     reads=["st1"], writes=["st2"])
                P.op("dve", lambda e, i=i: e.scalar_tensor_tensor(out=hb[:, :], in0=h[:, i, :], scalar=st[:, 2:3],
                                                                in1=grep[:, :], op0=ALU.mult, op1=ALU.mult),
                     reads=[("h", i), "st2", gain_key], writes=["hb"])
                for c8 in range(0, KC, 8):
                    n8 = min(8, KC - c8)
                    for c in range(n8):
                        P.op("pe", lambda e, c=c, c8=c8: e.transpose(ps_t[:, c * 128:(c + 1) * 128],
                                                                     hb[:, (c8 + c) * 128:(c8 + c + 1) * 128],
                                                                     idb[:, :]),
                             reads=["hb", "idb"], writes=["ps_t"])
                    P.op("act", lambda e, c8=c8, n8=n8, i=i: e.copy(
                        out=actT[:, c8:c8 + n8, i * 128:(i + 1) * 128],
                        in_=ps_t[:, 0:n8 * 128].rearrange("p (c n) -> p c n", n=128)),
                         reads=["ps_t"], writes=[("actT", i)])

        out_tok = []
        for t in range(NT):
            r0 = t * TILE
            for i in range(NSUB):
                P.dma("sp", h[:, i, :], x[r0 + i * 128:r0 + (i + 1) * 128, :], writes=[("h", i)], slot=("h", i))
            P.dma("sp", actT[:, :, :], mixT[:, r0:r0 + TILE].rearrange("(c p) n -> p c n", p=128),
                  writes=[("actT", i) for i in range(NSUB)], slot="actT")
            P.dma("sp", grep[:, :], g2, writes=["grep"], slot="grep")
            for cg in range(CG):
                for rg in range(RG):
                    buf, key = next_w()
                    wv = buf[:, 0:KCR * 512].rearrange("p (c n) -> p c n", n=512)
                    for i in range(NSUB):
                        for c in range(KCR):
                            kc = rg * KCR + c
                            first = (rg == 0 and c == 0)
                            last = (rg == RG - 1 and c == KCR - 1)
                            P.op("pe", lambda e, i=i, kc=kc, c=c, wv=wv, first=first, last=last: e.matmul(
                                ps_o[i][:, :], actT[:, kc, i * 128:(i + 1) * 128], wv[:, c, :],
                                start=first, stop=last),
                                 reads=[("actT", i), key], writes=[("ps_o", i)])
                for i in range(NSUB):
                    P.op("dve", lambda e, i=i, cg=cg: e.tensor_tensor(
                        out=h[:, i, cg * 512:(cg + 1) * 512], in0=ps_o[i][:, :],
                        in1=h[:, i, cg * 512:(cg + 1) * 512], op=ALU.add),
                         reads=[("ps_o", i), ("h", i)], writes=[("h", i)])
            rmsnorm_to_T(t, "grep")
            for fg in range(NFG):
                for pc in range(FG // 256):
                    buf, key = next_w()
                    wv = buf[:, 0:KC * 256].rearrange("p (c n) -> p c n", n=256)
                    for half in range(2):
                        fc = pc * 2 + half
                        pu = ps_u[fc % 2]
                        for kc in range(KC):
                            P.op("pe", lambda e, kc=kc, wv=wv, half=half, pu=pu: e.matmul(
                                pu[:, 0:TILE], wv[:, kc, half * 128:(half + 1) * 128], actT[:, kc, :],
                                start=(kc == 0), stop=(kc == KC - 1)),
                                 reads=[("actT", i) for i in range(NSUB)] + [key],
                                 writes=[("ps_u", fc % 2)])
                        r = rl[fc % 2]
                        P.op("act", lambda e, pu=pu, r=r: e.activation(out=r[:, :], in_=pu[:, 0:TILE], func=AF.Relu),
                             reads=[("ps_u", fc % 2)], writes=[("rl", fc % 2)])
                        P.op("dve", lambda e, r=r, fc=fc: e.tensor_tensor(out=upT[:, fc, :], in0=r[:, :], in1=r[:, :],
                                                                        op=ALU.mult),
                             reads=[("rl", fc % 2)], writes=["upT"])
                for cg in range(CG):
                    buf, key = next_w()
                    wv = buf[:, 0:FCG * 512].rearrange("p (c n) -> p c n", n=512)
                    for i in range(NSUB):
                        for fc in range(FCG):
                            P.op("pe", lambda e, i=i, fc=fc, wv=wv: e.matmul(
                                ps_o[i][:, :], upT[:, fc, i * 128:(i + 1) * 128], wv[:, fc, :],
                                start=(fc == 0), stop=(fc == FCG - 1)),
                                 reads=["upT", key], writes=[("ps_o", i)])
                        P.op("dve", lambda e, i=i, cg=cg: e.tensor_tensor(
                            out=h[:, i, cg * 512:(cg + 1) * 512], in0=ps_o[i][:, :],
                            in1=h[:, i, cg * 512:(cg + 1) * 512], op=ALU.add),
                             reads=[("ps_o", i), ("h", i)], writes=[("h", i)])
            P.dma("sp", grep[:, :], gf, writes=["grep"], slot="grep")
            for i in range(NSUB):
                P.op("act", lambda e, i=i: e.activation(out=hb[:, :], in_=h[:, i, :], func=AF.Square,
                                                      accum_out=st[:, 4:5]),
                     reads=[("h", i)], writes=["hb", "st4"])
                P.op("act", lambda e: e.activation(out=st[:, 5:6], in_=st[:, 4:5], func=AF.Sqrt, scale=1.0 / D,
                                                   bias=epsc[:, 0:1]),
                     reads=["st4", "epsc"], writes=["st5"])
                P.op("dve", lambda e: e.reciprocal(out=st[:, 6:7], in_=st[:, 5:6]),
                     reads=["st5"], writes=["st6"])
                P.op("dve", lambda e, i=i: e.scalar_tensor_tensor(out=h[:, i, :], in0=h[:, i, :], scalar=st[:, 6:7],
                                                                in1=grep[:, :], op0=ALU.mult, op1=ALU.mult),
                     reads=[("h", i), "st6", "grep"], writes=[("h", i)])
                tok = P.dma("sp", y[r0 + i * 128:r0 + (i + 1) * 128, :], h[:, i, :], reads=[("h", i)],
                            slot=("yout", i))
                out_tok.append(tok)
        final = {}
        for tk in out_tok:
            final[tk[1]] = tk
        P.emit(list(final.values()))
    return nc


def _ident(dt):
    return np.eye(128, dtype=np.float32).astype(dt)


def run_phase2(x2, mixT, w_out, w_up, w_down, g2, gf, TILE=512):
    n, NTOK, D = x2.shape
    F = w_up.shape[1]
    nc = build_phase2(D, F, NTOK, TILE)
    g2rep = np.ascontiguousarray(np.broadcast_to(g2.reshape(1, D), (128, D)))
    gfrep = np.ascontiguousarray(np.broadcast_to(gf.reshape(1, D), (128, D)))
    ident = _ident(NPBF16)
    in_maps = [dict(x=np.ascontiguousarray(x2[c]), mixT=np.ascontiguousarray(mixT[c]), w_out=w_out, w_up=w_up,
                    w_down=w_down, g2rep=g2rep, gfrep=gfrep, identb=ident) for c in range(n)]
    res = run_bass_kernel_spmd(nc, in_maps, core_ids=list(range(n)))
    return np.stack([res.results[c]["y"] for c in range(n)])


FM_COLS = 2048
TM_COLS = 1536
SM_COLS = 10
NLEV = 6


def build_phase1(D, T, TILE=512, stage=9):
    KC = D // 128
    NSUB = TILE // 128
    NT = T // TILE
    nc = bass.Bass("TRN2", target_bir_lowering=False)

    def din(name, shape, dt=F32):
        return nc.dram_tensor(name, shape, dt, kind="ExternalInput").ap()

    x = din("x", [T, D])
    w_fm = din("w_fm", [D, FM_COLS])
    w_tm = din("w_tm", [D, TM_COLS])
    w_sm = din("w_sm", [D, SM_COLS])
    g1rep_d = din("g1rep", [128, D])
    cst_d = din("cst", [128, 64])
    gmrep_d = din("gmrep", [128, 512])
    gnrep_d = din("gnrep", [128, 512])
    convw_d = din("convw", [128, 48])
    masks_d = din("masks", [128, 4 * 128 + 3 * 512])
    identb_d = din("identb", [128, 128], BF16)
    mix = nc.dram_tensor("mix", [T, 1024], BF16, kind="ExternalOutput").ap()

    P = Prog(nc)
    with contextlib.ExitStack() as es:
        def sb(name, shape, dt=F32):
            return es.enter_context(nc.sbuf_tensor("sb_" + name, shape, dt))

        def ps(name, shape, dt=F32):
            return es.enter_context(nc.psum_tensor("ps_" + name, shape, dt))

        def V(fn, r=(), w=()):
            return P.op("dve", fn, reads=r, writes=w)

        def A(fn, r=(), w=()):
            return P.op("act", fn, reads=r, writes=w)

        def MM(ps_ap, key, pairs, reads):
            n = len(pairs)
            for i, (l, r_) in enumerate(pairs):
                P.op("pe", lambda e, l=l, r_=r_, i=i: e.matmul(ps_ap, l, r_, start=(i == 0), stop=(i == n - 1)),
                     reads=reads, writes=[key])

        def TR(ps_ap, key, in_ap, ident, reads):
            P.op("pe", lambda e: e.transpose(ps_ap, in_ap, ident), reads=reads, writes=[key])

        g1rep = sb("g1rep", [128, D])
        cst = sb("cst", [128, 64])
        gmrep = sb("gmrep", [128, 512])
        gnrep = sb("gnrep", [128, 512])
        convw = sb("convw", [128, 48])
        masks = sb("masks", [128, 4 * 128 + 3 * 512])
        idb = sb("idb", [128, 128], BF16)
        onesb = sb("onesb", [128, 2], BF16)
        wsm = sb("wsm", [128, KC, SM_COLS], BF16)
        Cst = sb("Cst", [128, 2, 512])
        Cb = sb("Cb", [128, 2, 512], BF16)
        nst = sb("nst", [128, 2])
        nbb = sb("nbb", [128, 2], BF16)
        S4 = sb("S4", [128, 4, 128])
        S4b = sb("S4b", [128, 4, 128], BF16)
        halo = sb("halo", [128, 3, 4, 3])
        drv = sb("drv", [128, 32])
        U = masks[:, 0:128]
        Wm = masks[:, 128:256]
        I32 = masks[:, 256:384]
        ones32 = masks[:, 384:512]
        NEG4 = masks[:, 512:1024]
        STRICT4 = masks[:, 1024:1536]
        I4 = masks[:, 1536:2048]
        onec = cst[:, 2:3]
        epsc = cst[:, 3:4]
        epsl2 = cst[:, 4:5]
        xs = sb("xs", [128, D])
        xb = sb("xb", [128, D], BF16)
        xnT = sb("xnT", [128, KC, TILE], BF16)
        wb = [sb("wb%d" % i, [128, KC * 256], BF16) for i in range(2)]
        QmT = sb("QmT", [128, 2, TILE], BF16)
        KmT = sb("KmT", [128, 2, TILE], BF16)
        Gpre = sb("Gpre", [128, 4, TILE + 3])
        Gpost = sb("Gpost", [128, 12, TILE], BF16)
        Vm = sb("Vm", [128, NSUB, 512], BF16)
        Og = sb("Og", [128, NSUB, 512], BF16)
        Zg = sb("Zg", [128, NSUB, 512], BF16)
        smg = sb("smg", [128, NSUB, SM_COLS])
        cacc = [sb("cacc%d" % i, [128, TILE]) for i in range(2)]
        tmpg = [sb("tmpg%d" % i, [128, 256]) for i in range(2)]
        sqb = sb("sqb", [128, 4, TILE], BF16)
        gt = sb("gt", [128, 16, 16])
        gm_ = sb("gm_", [128, 12, 4])
        mixo = sb("mixo", [128, NSUB, 1024], BF16)
        st = sb("st", [128, 16])
        if D >= 4096:
            ov = [xs[:, k * 512:(k + 1) * 512].rearrange("p (h l) -> p h l", l=128) for k in range(8)]
            P.alias("xs", ["gU4", "ET4", "X0", "X1", "Y0", "Y1", "P0", "P1"])
        else:
            ovt = sb("ovt", [128, 8 * 512])
            ov = [ovt[:, k * 512:(k + 1) * 512].rearrange("p (h l) -> p h l", l=128) for k in range(8)]
        gU4, ET4 = ov[0], ov[1]
        Xb_ = [ov[2], ov[3]]
        Yb_ = [ov[4], ov[5]]
        Pb_ = [ov[6], ov[7]]
        ETS = sb("ETS", [128, 4, 128])
        at0 = sb("at0", [128, 4, 128], BF16)
        kdec4 = sb("kdec4", [128, 4, 128], BF16)
        vsc4 = sb("vsc4", [128, 4, 128])
        R0 = sb("R0", [128, 4, 128])
        vnew = sb("vnew", [128, 4, 128], BF16)
        t1 = sb("t1", [128, 512])
        opre = sb("opre", [128, 4, 128])
        osq = sb("osq", [128, 4, 128])
        num = sb("num", [128, 512])
        lfU = sb("lfU", [128, 128])
        EmT = sb("EmT", [128, 128])
        smT = sb("smT", [128, 128], BF16)
        kw = sb("kw", [128, 256], BF16)
        sc = sb("sc", [128, 32])
        pM = [ps("pM%d" % i, [128, 512]) for i in range(2)]
        pA = ps("pA", [128, 512])
        pB = ps("pB", [128, 512])
        pC = ps("pC", [128, 512])
        pD = ps("pD", [128, 512])
        pE = ps("pE", [128, 512])
        pT = ps("pT", [128, 1024], BF16)

        for i, (dst, src, k) in enumerate([(g1rep, g1rep_d, "g1rep"), (cst, cst_d, "cst"), (gmrep, gmrep_d, "gmrep"),
                                           (gnrep, gnrep_d, "gnrep"), (convw, convw_d, "convw"),
                                           (masks, masks_d, "masks"), (idb, identb_d, "idb")]):
            P.dma("sp", dst[:, :], src, writes=[k], slot=("c", i))
        P.dma("pool", wsm[:, :, :], w_sm.rearrange("(c p) n -> p c n", p=128), writes=["wsm"], slot=("c", "wsm"))
        V(lambda e: e.memset(onesb[:, :], 1.0), w=["onesb"])
        V(lambda e: e.memset(Cst[:, :, :], 0.0), w=["Cst"])
        V(lambda e: e.memset(Cb[:, :, :], 0.0), w=["Cb"])
        V(lambda e: e.memset(nst[:, :], 0.0), w=["nst"])
        V(lambda e: e.memset(nbb[:, :], 0.0), w=["nbb"])
        V(lambda e: e.memset(S4[:, :, :], 0.0), w=["S4"])
        V(lambda e: e.memset(S4b[:, :, :], 0.0), w=["S4b"])
        V(lambda e: e.memset(halo[:, :, :, :], 0.0), w=["halo"])
        V(lambda e: e.tensor_scalar(out=drv[:, 0:2], in0=cst[:, 0:2], scalar1=1.0 / GATE_CAP, scalar2=None,
                                    op0=ALU.mult), r=["cst"], w=["drv"])
        A(lambda e: e.activation(out=drv[:, 16:32], in_=cst[:, 16:32], func=AF.Exp), r=["cst"], w=["drvA"])

        ws = WStream(P, wb, "wb")
        for t in range(NT):
            for pc in range(FM_COLS // 256):
                ws.add(w_fm[:, pc * 256:(pc + 1) * 256].rearrange("(c p) n -> p c n", p=128),
                       lambda b: b[:, :].rearrange("p (c n) -> p c n", n=256))
            for pc in range(TM_COLS // 256):
                ws.add(w_tm[:, pc * 256:(pc + 1) * 256].rearrange("(c p) n -> p c n", p=128),
                       lambda b: b[:, :].rearrange("p (c n) -> p c n", n=256))
        wk = [0]

        def next_w():
            k = wk[0]
            wk[0] += 1
            buf, key = ws.acquire(k)
            return buf[:, :].rearrange("p (c n) -> p c n", n=256), key

        xkeys = [("xnT", i) for i in range(NSUB)]
        out_tok = []
        GTK = dict(BETA=0, GG=1, RQ=2, NK=3, RK=4, NCC=5, BRK=6, DEC=7, EDEC=8, NEDEC=9, C1=10, EDL=11, RQ2=12,
                   TMP=13, TMP2=14, DL=15)
        GMK = dict(LI=0, LF=1, BC=2, BL=3, EK=4, EGS=5, WOLD=6, TMP=7, TMP2=8)

        def G(kind, j=None, hh=None):
            k = GTK[kind]
            if j is None:
                return gt[:, k, :]
            if hh is None:
                return gt[:, k, j * 4:(j + 1) * 4]
            return gt[:, k, j * 4 + hh:j * 4 + hh + 1]

        def M_(kind, j=None):
            k = GMK[kind]
            if j is None:
                return gm_[:, k, :]
            return gm_[:, k, j:j + 1]

        for t in range(NT):
            r0 = t * TILE
            for i in range(NSUB):
                P.dma("sp", xs[:, :], x[r0 + i * 128:r0 + (i + 1) * 128, :], writes=["xs"], slot="xs")
                A(lambda e: e.activation(out=xb[:, :], in_=xs[:, :], func=AF.Square, accum_out=st[:, 0:1]),
                  r=["xs"], w=["xb", "st0"])
                A(lambda e: e.activation(out=st[:, 1:2], in_=st[:, 0:1], func=AF.Sqrt, scale=1.0 / D, bias=epsc),
                  r=["st0", "cst"], w=["st1"])
                V(lambda e: e.reciprocal(out=st[:, 2:3], in_=st[:, 1:2]), r=["st1"], w=["st2"])
                V(lambda e: e.scalar_tensor_tensor(out=xb[:, :], in0=xs[:, :], scalar=st[:, 2:3], in1=g1rep[:, :],
                                                   op0=ALU.mult, op1=ALU.mult), r=["xs", "st2", "g1rep"], w=["xb"])
                for c8 in range(0, KC, 8):
                    n8 = min(8, KC - c8)
                    for c in range(n8):
                        TR(pT[:, c * 128:(c + 1) * 128], "pT", xb[:, (c8 + c) * 128:(c8 + c + 1) * 128], idb[:, :],
                           ["xb", "idb"])
                    A(lambda e, c8=c8, n8=n8, i=i: e.copy(out=xnT[:, c8:c8 + n8, i * 128:(i + 1) * 128],
                                                          in_=pT[:, 0:n8 * 128].rearrange("p (c n) -> p c n", n=128)),
                      r=["pT"], w=[("xnT", i)])
            for pc in range(FM_COLS // 256):
                wv, key = next_w()
                for half in range(2):
                    oc = pc * 2 + half
                    pm = pM[oc % 2]
                    pk = ("pM", oc % 2)
                    MM(pm[:, 0:TILE], pk, [(wv[:, kc, half * 128:(half + 1) * 128], xnT[:, kc, :]) for kc in range(KC)],
                       xkeys + [key])
                    if oc < 2:
                        A(lambda e, oc=oc, pm=pm: e.copy(out=QmT[:, oc, :], in_=pm[:, 0:TILE]), r=[pk], w=["QmT"])
                    elif oc < 4:
                        A(lambda e, oc=oc, pm=pm: e.mul(out=KmT[:, oc - 2, :], in_=pm[:, 0:TILE], mul=1.0 / 16.0),
                          r=[pk], w=["KmT"])
                    else:
                        s_, hh = (oc - 4) // 4, (oc - 4) % 4
                        A(lambda e, hh=hh, pm=pm: e.copy(out=Gpre[:, hh, 3:3 + TILE], in_=pm[:, 0:TILE]),
                          r=[pk], w=[("Gpre", hh)])
                        if hh == 3:
                            V(lambda e, s_=s_: e.tensor_copy(out=Gpre[:, :, 0:3], in_=halo[:, s_, :, :]),
                              r=["halo"], w=[("Gpre", h_) for h_ in range(4)])
                            for h_ in range(4):
                                ca = cacc[h_ % 2]
                                ck = ("cacc", h_ % 2)
                                wcol = lambda j_, s_=s_, h_=h_: convw[:, (s_ * 4 + h_) * 4 + j_:(s_ * 4 + h_) * 4 + j_ + 1]
                                V(lambda e, h_=h_, ca=ca, wcol=wcol: e.tensor_scalar(
                                    out=ca[:, :], in0=Gpre[:, h_, 0:TILE], scalar1=wcol(0), scalar2=None, op0=ALU.mult),
                                  r=[("Gpre", h_), "convw"], w=[ck])
                                for j_ in range(1, 4):
                                    V(lambda e, h_=h_, ca=ca, wcol=wcol, j_=j_: e.scalar_tensor_tensor(
                                        out=ca[:, :], in0=Gpre[:, h_, j_:j_ + TILE], scalar=wcol(j_), in1=ca[:, :],
                                        op0=ALU.mult, op1=ALU.add), r=[("Gpre", h_), ck], w=[ck])
                                A(lambda e, s_=s_, h_=h_, ca=ca: e.activation(out=Gpost[:, s_ * 4 + h_, :], in_=ca[:, :],
                                                                            func=AF.Silu), r=[ck], w=[("Gpost", s_)])
                            V(lambda e, s_=s_: e.tensor_copy(out=halo[:, s_, :, :], in_=Gpre[:, :, TILE:TILE + 3]),
                              r=[("Gpre", h_) for h_ in range(4)], w=["halo"])
            for pc in range(TM_COLS // 256):
                wv, key = next_w()
                kind, cpc = pc // 2, pc % 2
                for i in range(NSUB):
                    pm = pM[i % 2]
                    pk = ("pM", i % 2)
                    MM(pm[:, 0:256], pk, [(xnT[:, kc, i * 128:(i + 1) * 128], wv[:, kc, :]) for kc in range(KC)],
                       [("xnT", i), key])
                    cs = slice(cpc * 256, (cpc + 1) * 256)
                    if kind == 0:
                        A(lambda e, i=i, pm=pm, cs=cs: e.copy(out=Vm[:, i, cs], in_=pm[:, 0:256]), r=[pk], w=["Vm"])
                    else:
                        tg = tmpg[i % 2]
                        tk = ("tmpg", i % 2)
                        fn = AF.Sigmoid if kind == 1 else AF.Silu
                        A(lambda e, pm=pm, tg=tg, fn=fn: e.activation(out=tg[:, :], in_=pm[:, 0:256], func=fn),
                          r=[pk], w=[tk])
                        dst = Og if kind == 1 else Zg
                        gsrc = gmrep if kind == 1 else gnrep
                        V(lambda e, i=i, tg=tg, cs=cs, dst=dst, gsrc=gsrc: e.tensor_tensor(
                            out=dst[:, i, cs], in0=tg[:, :], in1=gsrc[:, cs], op=ALU.mult),
                          r=[tk, "gmrep", "gnrep"], w=["Og" if kind == 1 else "Zg"])
            for i in range(NSUB):
                pm = pM[i % 2]
                pk = ("pM", i % 2)
                MM(pm[:, 0:SM_COLS], pk, [(xnT[:, kc, i * 128:(i + 1) * 128], wsm[:, kc, :]) for kc in range(KC)],
                   [("xnT", i), "wsm"])
                A(lambda e, i=i, pm=pm: e.copy(out=smg[:, i, :], in_=pm[:, 0:SM_COLS]), r=[pk], w=["smg"])

            GT, GM = ["gt"], ["gm"]
            if stage < 2:
                for i in range(NSUB):
                    V(lambda e, i=i: e.memset(mixo[:, i, :], 0.0), w=[("mixo", i)])
                    tok = P.dma("sp", mix[r0 + i * 128:r0 + (i + 1) * 128, :], mixo[:, i, :], reads=[("mixo", i)],
                                slot=("mixo", i))
                    out_tok.append(tok)
                continue
            A(lambda e: e.activation(out=M_("TMP"), in_=smg[:, :, 0], func=AF.Tanh, scale=1.0 / GATE_CAP,
                                     bias=drv[:, 0:1]), r=["smg", "drv"], w=GM)
            V(lambda e: e.tensor_scalar(out=M_("LI"), in0=M_("TMP"), scalar1=GATE_CAP, scalar2=None, op0=ALU.mult),
              r=GM, w=GM)
            A(lambda e: e.activation(out=M_("TMP"), in_=smg[:, :, 1], func=AF.Tanh, scale=1.0 / GATE_CAP,
                                     bias=drv[:, 1:2]), r=["smg", "drv"] + GM, w=GM)
            A(lambda e: e.activation(out=M_("TMP2"), in_=M_("TMP"), func=AF.Exp, scale=-GATE_CAP), r=GM, w=GM)
            A(lambda e: e.activation(out=M_("TMP"), in_=M_("TMP2"), func=AF.Ln, bias=onec), r=GM + ["cst"], w=GM)
            V(lambda e: e.tensor_scalar(out=M_("LF"), in0=M_("TMP"), scalar1=-1.0, scalar2=None, op0=ALU.mult),
              r=GM, w=GM)
            MM(pE[:, 0:4], "pE", [(U, M_("LF"))], GM + ["masks"])
            V(lambda e: e.tensor_copy(out=M_("BC"), in_=pE[:, 0:4]), r=["pE"], w=GM)
            MM(pE[:, 4:8], "pE", [(ones32, M_("LF"))], GM + ["masks"])
            V(lambda e: e.tensor_copy(out=M_("BL"), in_=pE[:, 4:8]), r=["pE"], w=GM)
            A(lambda e: e.activation(out=M_("EK"), in_=M_("BC"), func=AF.Exp), r=GM, w=GM)
            A(lambda e: e.activation(out=M_("WOLD"), in_=M_("BL"), func=AF.Exp), r=GM, w=GM)
            V(lambda e: e.tensor_tensor(out=M_("TMP"), in0=M_("BL"), in1=M_("BC"), op=ALU.subtract), r=GM, w=GM)
            V(lambda e: e.tensor_tensor(out=M_("TMP2"), in0=M_("TMP"), in1=M_("LI"), op=ALU.add), r=GM, w=GM)
            A(lambda e: e.activation(out=M_("EGS"), in_=M_("TMP2"), func=AF.Exp), r=GM, w=GM)
            g16 = lambda ap: ap.rearrange("p (s h) -> p s h", h=4)
            A(lambda e: e.activation(out=g16(G("BETA")), in_=smg[:, :, 6:10], func=AF.Sigmoid), r=["smg"], w=GT)
            V(lambda e: e.tensor_tensor(out=g16(G("TMP")), in0=smg[:, :, 2:6], in1=g16(cst[:, 32:48]), op=ALU.add),
              r=["smg", "cst"] + GT, w=GT)
            V(lambda e: e.tensor_scalar(out=G("TMP2"), in0=G("TMP"), scalar1=-1.0, scalar2=None, op0=ALU.mult),
              r=GT, w=GT)
            V(lambda e: e.tensor_tensor(out=G("TMP2"), in0=G("TMP2"), in1=G("TMP"), op=ALU.max), r=GT, w=GT)
            A(lambda e: e.activation(out=G("GG"), in_=G("TMP2"), func=AF.Exp, scale=-1.0), r=GT, w=GT)
            A(lambda e: e.activation(out=G("TMP2"), in_=G("GG"), func=AF.Ln, bias=onec), r=GT + ["cst"], w=GT)
            V(lambda e: e.tensor_scalar(out=G("GG"), in0=G("TMP"), scalar1=0.0, scalar2=None, op0=ALU.max), r=GT, w=GT)
            V(lambda e: e.tensor_tensor(out=G("TMP"), in0=G("GG"), in1=G("TMP2"), op=ALU.add), r=GT, w=GT)
            V(lambda e: e.scalar_tensor_tensor(out=G("GG"), in0=G("TMP"), scalar=-1.0, in1=drv[:, 16:32],
                                               op0=ALU.mult, op1=ALU.mult), r=GT + ["drvA"], w=GT)
            MM(pE[:, 16:32], "pE", [(U, G("GG"))], GT + ["masks"])
            V(lambda e: e.tensor_copy(out=G("DEC"), in_=pE[:, 16:32]), r=["pE"], w=GT)
            MM(pE[:, 32:48], "pE", [(ones32, G("GG"))], GT + ["masks"])
            V(lambda e: e.tensor_copy(out=G("DL"), in_=pE[:, 32:48]), r=["pE"], w=GT)
            A(lambda e: e.activation(out=G("EDEC"), in_=G("DEC"), func=AF.Exp), r=GT, w=GT)
            A(lambda e: e.activation(out=G("EDL"), in_=G("DL"), func=AF.Exp), r=GT, w=GT)
            V(lambda e: e.tensor_scalar(out=G("NEDEC"), in0=G("EDEC"), scalar1=-1.0, scalar2=None, op0=ALU.mult),
              r=GT, w=GT)
            V(lambda e: e.tensor_tensor(out=G("TMP"), in0=G("DL"), in1=G("DEC"), op=ALU.subtract), r=GT, w=GT)
            A(lambda e: e.activation(out=G("C1"), in_=G("TMP"), func=AF.Exp), r=GT, w=GT)
            for s_, dstk in ((0, "RQ"), (1, "NK")):
                V(lambda e, s_=s_: e.tensor_tensor(out=sqb[:, :, :], in0=Gpost[:, s_ * 4:(s_ + 1) * 4, :],
                                                   in1=Gpost[:, s_ * 4:(s_ + 1) * 4, :], op=ALU.mult),
                  r=[("Gpost", s_)], w=["sqb"])
                for j in range(NSUB):
                    for hh in range(4):
                        c = 64 + j * 4 + hh
                        MM(pE[:, c:c + 1], "pE", [(sqb[:, hh, j * 128:(j + 1) * 128], onesb[:, 0:1])], ["sqb", "onesb"])
                A(lambda e, dstk=dstk: e.activation(out=G(dstk), in_=pE[:, 64:80], func=AF.Sqrt, bias=epsl2),
                  r=["pE", "cst"] + GT, w=GT)
            V(lambda e: e.reciprocal(out=G("TMP"), in_=G("RQ")), r=GT, w=GT)
            V(lambda e: e.tensor_scalar(out=G("RQ"), in0=G("TMP"), scalar1=128.0 ** -0.5, scalar2=None, op0=ALU.mult),
              r=GT, w=GT)
            V(lambda e: e.tensor_tensor(out=G("TMP"), in0=G("RQ"), in1=G("RQ"), op=ALU.mult), r=GT, w=GT)
            V(lambda e: e.tensor_scalar(out=G("RQ2"), in0=G("TMP"), scalar1=1.0 / 128.0, scalar2=None, op0=ALU.mult),
              r=GT, w=GT)
            V(lambda e: e.reciprocal(out=G("RK"), in_=G("NK")), r=GT, w=GT)
            V(lambda e: e.tensor_tensor(out=G("BRK"), in0=G("BETA"), in1=G("RK"), op=ALU.mult), r=GT, w=GT)
            V(lambda e: e.scalar_tensor_tensor(out=G("NCC"), in0=G("BRK"), scalar=-1.0, in1=G("RK"),
                                               op0=ALU.mult, op1=ALU.mult), r=GT, w=GT)
            V(lambda e: e.tensor_tensor(out=G("TMP"), in0=G("C1"), in1=G("RK"), op=ALU.mult), r=GT, w=GT)
            V(lambda e: e.tensor_copy(out=G("C1"), in_=G("TMP")), r=GT, w=GT)

            if stage < 3:
                for i in range(NSUB):
                    V(lambda e, i=i: e.memset(mixo[:, i, :], 0.0), w=[("mixo", i)])
            for j in range(NSUB if stage >= 3 else 0):
                cs = slice(j * 128, (j + 1) * 128)
                for hh in range(4):
                    V(lambda e, hh=hh, j=j: e.tensor_scalar(out=gU4[:, hh, :], in0=U, scalar1=G("GG", j, hh),
                                                            scalar2=None, op0=ALU.mult), r=GT + ["masks"], w=["gU4"])
                MM(pA[:, :], "pA", [(Wm, gU4[:, :, :].rearrange("p h l -> p (h l)")), (I32, NEG4)], ["gU4", "masks"])
                A(lambda e: e.activation(out=ET4[:, :, :].rearrange("p h l -> p (h l)"), in_=pA[:, :], func=AF.Exp),
                  r=["pA"], w=["ET4"])
                for hh in range(4):
                    MM(pB[:, hh * 128:(hh + 1) * 128], "pB", [(Gpost[:, 4 + hh, cs], Gpost[:, 4 + hh, cs])],
                       [("Gpost", 1)])
                for hh in range(4):
                    MM(pC[:, hh * 128:(hh + 1) * 128], "pC", [(Gpost[:, 4 + hh, cs], Gpost[:, hh, cs])],
                       [("Gpost", 1), ("Gpost", 0)])
                V(lambda e: e.tensor_tensor(out=ETS[:, :, :].rearrange("p h l -> p (h l)"),
                                            in0=ET4[:, :, :].rearrange("p h l -> p (h l)"), in1=STRICT4, op=ALU.mult),
                  r=["ET4", "masks"], w=["ETS"])
                X, Y, Pm = Xb_[0], Yb_[0], Pb_[0]
                for hh in range(4):
                    V(lambda e, hh=hh, j=j, X=X: e.scalar_tensor_tensor(
                        out=X[:, hh, :], in0=pB[:, hh * 128:(hh + 1) * 128], scalar=G("NCC", j, hh), in1=ETS[:, hh, :],
                        op0=ALU.mult, op1=ALU.mult), r=["pB", "ETS"] + GT, w=["X0"])
                    V(lambda e, hh=hh, j=j: e.scalar_tensor_tensor(
                        out=at0[:, hh, :], in0=pC[:, hh * 128:(hh + 1) * 128], scalar=G("RK", j, hh), in1=ET4[:, hh, :],
                        op0=ALU.mult, op1=ALU.mult), r=["pC", "ET4"] + GT, w=["at0"])
                for hh in range(4):
                    TR(pD[:, hh * 128:(hh + 1) * 128], "pD", X[:, hh, :], I32, ["X0", "masks"])
                A(lambda e, Y=Y: e.copy(out=Y[:, :, :].rearrange("p h l -> p (h l)"), in_=pD[:, :]), r=["pD"], w=["Y0"])
                V(lambda e, Pm=Pm, X=X: e.tensor_tensor(out=Pm[:, :, :].rearrange("p h l -> p (h l)"),
                                                        in0=X[:, :, :].rearrange("p h l -> p (h l)"), in1=I4,
                                                        op=ALU.add), r=["X0", "masks"], w=["P0"])
                for lv in range(1, NLEV + 1):
                    a, b = (lv - 1) % 2, lv % 2
                    Xp, Yp, Pp = Xb_[a], Yb_[a], Pb_[a]
                    Xn, Yn, Pn = Xb_[b], Yb_[b], Pb_[b]
                    xa, ya, pa = "X%d" % a, "Y%d" % a, "P%d" % a
                    xb_, yb_, pb_ = "X%d" % b, "Y%d" % b, "P%d" % b
                    last = (lv == NLEV)
                    for hh in range(4):
                        MM(pD[:, hh * 128:(hh + 1) * 128], "pD", [(Xp[:, hh, :], Yp[:, hh, :])], [xa, ya])
                    if not last:
                        for hh in range(4):
                            MM(pB[:, hh * 128:(hh + 1) * 128], "pB", [(Yp[:, hh, :], Xp[:, hh, :])], [xa, ya])
                    V(lambda e, Yn=Yn: e.tensor_copy(out=Yn[:, :, :].rearrange("p h l -> p (h l)"), in_=pD[:, :]),
                      r=["pD"], w=[yb_])
                    if not last:
                        A(lambda e, Xn=Xn: e.copy(out=Xn[:, :, :].rearrange("p h l -> p (h l)"), in_=pB[:, :]),
                          r=["pB"], w=[xb_])
                    for hh in range(4):
                        MM(pC[:, hh * 128:(hh + 1) * 128], "pC", [(Yn[:, hh, :], Pp[:, hh, :])], [yb_, pa])
                    V(lambda e, Pn=Pn, Pp=Pp: e.tensor_tensor(out=Pn[:, :, :].rearrange("p h l -> p (h l)"),
                                                              in0=pC[:, :],
                                                              in1=Pp[:, :, :].rearrange("p h l -> p (h l)"), op=ALU.add),
                      r=["pC", pa], w=[pb_])
                NT_ = Pb_[NLEV % 2]
                ntk = "P%d" % (NLEV % 2)
                if stage < 4:
                    V(lambda e, j=j: e.memset(mixo[:, j, :], 0.0), w=[("mixo", j)])
                    continue
                V(lambda e, j=j: e.tensor_scalar(out=lfU[:, :], in0=U, scalar1=M_("LF", j), scalar2=None, op0=ALU.mult),
                  r=GM + ["masks"], w=["lfU"])
                MM(pM[0][:, 0:128], ("pM", 0), [(Wm, lfU[:, :]), (I32, NEG4[:, 0:128])], ["lfU", "masks"])
                A(lambda e, j=j: e.activation(out=EmT[:, :], in_=pM[0][:, 0:128], func=AF.Exp, bias=M_("LI", j)),
                  r=[("pM", 0)] + GM, w=["EmT"])
                MM(pM[1][:, 0:128], ("pM", 1), [(KmT[:, 0, cs], QmT[:, 0, cs]), (KmT[:, 1, cs], QmT[:, 1, cs])],
                   ["KmT", "QmT"])
                V(lambda e: e.tensor_tensor(out=smT[:, :], in0=pM[1][:, 0:128], in1=EmT[:, :], op=ALU.mult),
                  r=[("pM", 1), "EmT"], w=["smT"])
                for dc in range(2):
                    TR(pT[:, dc * 128:(dc + 1) * 128], "pT", KmT[:, dc, cs], idb[:, :], ["KmT", "idb"])
                A(lambda e, j=j: e.activation(out=kw[:, :], in_=pT[:, 0:256], func=AF.Copy, scale=M_("EGS", j)),
                  r=["pT"] + GM, w=["kw"])
                for hh in range(4):
                    TR(pT[:, hh * 128:(hh + 1) * 128], "pT", Gpost[:, 4 + hh, cs], idb[:, :], [("Gpost", 1), "idb"])
                for hh in range(4):
                    TR(pT[:, 512 + hh * 128:512 + (hh + 1) * 128], "pT", Gpost[:, 8 + hh, cs], idb[:, :],
                       [("Gpost", 2), "idb"])
                for hh in range(4):
                    A(lambda e, hh=hh, j=j: e.activation(out=kdec4[:, hh, :], in_=pT[:, hh * 128:(hh + 1) * 128],
                                                         func=AF.Copy, scale=G("C1", j, hh)), r=["pT"] + GT, w=["kdec4"])
                    A(lambda e, hh=hh, j=j: e.activation(out=vsc4[:, hh, :], in_=pT[:, 512 + hh * 128:512 + (hh + 1) * 128],
                                                         func=AF.Copy, scale=G("NK", j, hh)), r=["pT"] + GT, w=["vsc4"])
                for hh in range(4):
                    MM(pA[:, hh * 128:(hh + 1) * 128], "pA", [(Gpost[:, 4 + hh, cs], S4b[:, hh, :])], [("Gpost", 1), "S4b"])
                for hh in range(4):
                    MM(pE[:, hh * 128:(hh + 1) * 128], "pE", [(Gpost[:, hh, cs], S4b[:, hh, :])], [("Gpost", 0), "S4b"])
                for hh in range(4):
                    V(lambda e, hh=hh, j=j: e.scalar_tensor_tensor(
                        out=R0[:, hh, :], in0=pA[:, hh * 128:(hh + 1) * 128], scalar=G("NEDEC", j, hh), in1=vsc4[:, hh, :],
                        op0=ALU.mult, op1=ALU.add), r=["pA", "vsc4"] + GT, w=["R0"])
                for hh in range(4):
                    MM(pB[:, hh * 128:(hh + 1) * 128], "pB", [(NT_[:, hh, :], R0[:, hh, :])], [ntk, "R0"])
                for hh in range(4):
                    A(lambda e, hh=hh, j=j: e.activation(out=vnew[:, hh, :], in_=pB[:, hh * 128:(hh + 1) * 128],
                                                         func=AF.Copy, scale=G("BRK", j, hh)), r=["pB"] + GT, w=["vnew"])
                for hh in range(4):
                    MM(pD[:, hh * 128:(hh + 1) * 128], "pD", [(at0[:, hh, :], vnew[:, hh, :])], ["at0", "vnew"])
                for hh in range(4):
                    A(lambda e, hh=hh, j=j: e.activation(out=t1[:, hh * 128:(hh + 1) * 128],
                                                         in_=pE[:, hh * 128:(hh + 1) * 128], func=AF.Copy,
                                                         scale=G("EDEC", j, hh)), r=["pE"] + GT, w=["t1"])
                V(lambda e: e.tensor_tensor(out=opre[:, :, :].rearrange("p h l -> p (h l)"), in0=pD[:, :], in1=t1[:, :],
                                            op=ALU.add), r=["pD", "t1"], w=["opre"])
                for hh in range(4):
                    MM(pA[:, hh * 128:(hh + 1) * 128], "pA", [(kdec4[:, hh, :], vnew[:, hh, :])], ["kdec4", "vnew"])
                for hh in range(4):
                    V(lambda e, hh=hh, j=j: e.scalar_tensor_tensor(
                        out=S4[:, hh, :], in0=S4[:, hh, :], scalar=G("EDL", j, hh), in1=pA[:, hh * 128:(hh + 1) * 128],
                        op0=ALU.mult, op1=ALU.add), r=["pA", "S4"] + GT, w=["S4"])
                A(lambda e: e.copy(out=S4b[:, :, :].rearrange("p h l -> p (h l)"),
                                   in_=S4[:, :, :].rearrange("p h l -> p (h l)")), r=["S4"], w=["S4b"])
                V(lambda e: e.tensor_tensor(out=osq[:, :, :], in0=opre[:, :, :], in1=opre[:, :, :], op=ALU.mult),
                  r=["opre"], w=["osq"])
                V(lambda e: e.tensor_reduce(out=sc[:, 0:4], in_=osq[:, :, :], axis=mybir.AxisListType.X, op=ALU.add),
                  r=["osq"], w=["sc0"])
                V(lambda e, j=j: e.tensor_tensor(out=sc[:, 4:8], in0=sc[:, 0:4], in1=G("RQ2", j), op=ALU.mult),
                  r=["sc0"] + GT, w=["sc1"])
                A(lambda e: e.activation(out=sc[:, 8:12], in_=sc[:, 4:8], func=AF.Ln, bias=epsc), r=["sc1", "cst"],
                  w=["sc2"])
                A(lambda e: e.activation(out=sc[:, 12:16], in_=sc[:, 8:12], func=AF.Exp, scale=-0.5), r=["sc2"], w=["sc3"])
                V(lambda e, j=j: e.tensor_tensor(out=sc[:, 16:20], in0=sc[:, 12:16], in1=G("RQ", j), op=ALU.mult),
                  r=["sc3"] + GT, w=["sc4"])
                for hh in range(4):
                    V(lambda e, hh=hh, j=j: e.scalar_tensor_tensor(
                        out=mixo[:, j, 512 + hh * 128:512 + (hh + 1) * 128], in0=opre[:, hh, :],
                        scalar=sc[:, 16 + hh:17 + hh], in1=Zg[:, j, hh * 128:(hh + 1) * 128], op0=ALU.mult, op1=ALU.mult),
                      r=["opre", "sc4", "Zg"], w=[("mixo", j)])
                MM(pB[:, :], "pB", [(smT[:, :], Vm[:, j, :])], ["smT", "Vm"])
                MM(pC[:, :], "pC", [(QmT[:, 0, cs], Cb[:, 0, :]), (QmT[:, 1, cs], Cb[:, 1, :])], ["QmT", "Cb"])
                MM(pE[:, 0:1], "pE", [(QmT[:, 0, cs], nbb[:, 0:1]), (QmT[:, 1, cs], nbb[:, 1:2])], ["QmT", "nbb"])
                MM(pE[:, 1:2], "pE", [(smT[:, :], onesb[:, 0:1])], ["smT", "onesb"])
                A(lambda e, j=j: e.activation(out=t1[:, :], in_=pC[:, :], func=AF.Copy, scale=M_("EK", j)),
                  r=["pC"] + GM, w=["t1"])
                V(lambda e: e.tensor_tensor(out=num[:, :], in0=pB[:, :], in1=t1[:, :], op=ALU.add), r=["pB", "t1"],
                  w=["num"])
                A(lambda e: e.copy(out=sc[:, 20:22], in_=pE[:, 0:2]), r=["pE"], w=["sc5"])
                V(lambda e, j=j: e.scalar_tensor_tensor(out=sc[:, 22:23], in0=sc[:, 20:21], scalar=M_("EK", j),
                                                        in1=sc[:, 21:22], op0=ALU.mult, op1=ALU.add),
                  r=["sc5"] + GM, w=["sc6"])
                V(lambda e: e.tensor_scalar(out=sc[:, 23:24], in0=sc[:, 22:23], scalar1=-1.0, scalar2=None,
                                            op0=ALU.mult), r=["sc6"], w=["sc7"])
                V(lambda e: e.tensor_tensor(out=sc[:, 23:24], in0=sc[:, 23:24], in1=sc[:, 22:23], op=ALU.max),
                  r=["sc6", "sc7"], w=["sc7"])
                V(lambda e: e.tensor_scalar(out=sc[:, 23:24], in0=sc[:, 23:24], scalar1=1.0, scalar2=None,
                                            op0=ALU.max), r=["sc7"], w=["sc7"])
                V(lambda e: e.reciprocal(out=sc[:, 24:25], in_=sc[:, 23:24]), r=["sc7"], w=["sc8"])
                A(lambda e: e.activation(out=t1[:, :], in_=num[:, :], func=AF.Square, accum_out=sc[:, 25:26]),
                  r=["num"], w=["t1", "sc9"])
                V(lambda e: e.tensor_tensor(out=sc[:, 26:27], in0=sc[:, 24:25], in1=sc[:, 24:25], op=ALU.mult),
                  r=["sc8"], w=["sc10"])
                V(lambda e: e.tensor_tensor(out=sc[:, 27:28], in0=sc[:, 26:27], in1=sc[:, 25:26], op=ALU.mult),
                  r=["sc10", "sc9"], w=["sc11"])
                A(lambda e: e.activation(out=sc[:, 28:29], in_=sc[:, 27:28], func=AF.Ln, scale=1.0 / 512.0, bias=epsc),
                  r=["sc11", "cst"], w=["sc12"])
                A(lambda e: e.activation(out=sc[:, 29:30], in_=sc[:, 28:29], func=AF.Exp, scale=-0.5), r=["sc12"],
                  w=["sc13"])
                V(lambda e: e.tensor_tensor(out=sc[:, 30:31], in0=sc[:, 29:30], in1=sc[:, 24:25], op=ALU.mult),
                  r=["sc13", "sc8"], w=["sc14"])
                V(lambda e, j=j: e.scalar_tensor_tensor(out=mixo[:, j, 0:512], in0=num[:, :], scalar=sc[:, 30:31],
                                                        in1=Og[:, j, :], op0=ALU.mult, op1=ALU.mult),
                  r=["num", "sc14", "Og"], w=[("mixo", j)])
                MM(pB[:, :], "pB", [(kw[:, 0:128], Vm[:, j, :])], ["kw", "Vm"])
                MM(pC[:, :], "pC", [(kw[:, 128:256], Vm[:, j, :])], ["kw", "Vm"])
                MM(pE[:, 8:9], "pE", [(kw[:, 0:128], onesb[:, 0:1])], ["kw", "onesb"])
                MM(pE[:, 9:10], "pE", [(kw[:, 128:256], onesb[:, 0:1])], ["kw", "onesb"])
                V(lambda e, j=j: e.scalar_tensor_tensor(out=Cst[:, 0, :], in0=Cst[:, 0, :], scalar=M_("WOLD", j),
                                                        in1=pB[:, :], op0=ALU.mult, op1=ALU.add),
                  r=["pB", "Cst"] + GM, w=["Cst"])
                V(lambda e, j=j: e.scalar_tensor_tensor(out=Cst[:, 1, :], in0=Cst[:, 1, :], scalar=M_("WOLD", j),
                                                        in1=pC[:, :], op0=ALU.mult, op1=ALU.add),
                  r=["pC", "Cst"] + GM, w=["Cst"])
                A(lambda e: e.copy(out=Cb[:, :, :].rearrange("p a b -> p (a b)"),
                                   in_=Cst[:, :, :].rearrange("p a b -> p (a b)")), r=["Cst"], w=["Cb"])
                V(lambda e, j=j: e.scalar_tensor_tensor(out=nst[:, :], in0=nst[:, :], scalar=M_("WOLD", j),
                                                        in1=pE[:, 8:10], op0=ALU.mult, op1=ALU.add),
                  r=["pE", "nst"] + GM, w=["nst"])
                A(lambda e: e.copy(out=nbb[:, :], in_=nst[:, :]), r=["nst"], w=["nbb"])
            for i in range(NSUB):
                tok = P.dma("sp", mix[r0 + i * 128:r0 + (i + 1) * 128, :], mixo[:, i, :], reads=[("mixo", i)],
                            slot=("mixo", i))
                out_tok.append(tok)
        final = {}
        for tk in out_tok:
            final[tk[1]] = tk
        P.emit(list(final.values()))
    return nc


OFF_MQ, OFF_MK, OFF_MV, OFF_MO, OFF_MI, OFF_MF = 0, 1024, 2048, 4096, 6144, 6148
OFF_GQ, OFF_GK, OFF_GV, OFF_GZ, OFF_GA, OFF_GB = 6152, 6152 + 2048, 6152 + 4096, 12296, 14344, 14360


def make_masks():
    j = np.arange(128)[:, None]
    l = np.arange(128)[None, :]
    U = (j <= l).astype(np.float32)
    Wm = (j > l).astype(np.float32)
    I = np.eye(128, dtype=np.float32)
    ones = np.ones((128, 128), np.float32)
    NEG = np.where(l < j, -30000.0, 0.0).astype(np.float32)
    STRICT = (l > j).astype(np.float32)
    return np.ascontiguousarray(np.concatenate([U, Wm, I, ones] + [NEG] * 4 + [STRICT] * 4 + [I] * 4, axis=1))


def rep128(v):
    v = np.asarray(v, np.float32).reshape(1, -1)
    return np.ascontiguousarray(np.broadcast_to(v, (128, v.shape[1])))


def phase1_group_inputs(g, w_in, norm1_g, i_bias, f_bias, mnorm_g, conv_w, a_log, dt_bias, gnorm_g):
    sl = lambda o, n: slice(o, o + n)
    w_fm = np.concatenate([w_in[:, sl(OFF_MQ + g * 256, 256)], w_in[:, sl(OFF_MK + g * 256, 256)],
                           w_in[:, sl(OFF_GQ + g * 512, 512)], w_in[:, sl(OFF_GK + g * 512, 512)],
                           w_in[:, sl(OFF_GV + g * 512, 512)]], axis=1)
    w_tm = np.concatenate([w_in[:, sl(OFF_MV + g * 512, 512)], w_in[:, sl(OFF_MO + g * 512, 512)],
                           w_in[:, sl(OFF_GZ + g * 512, 512)]], axis=1)
    w_sm = np.concatenate([w_in[:, sl(OFF_MI + g, 1)], w_in[:, sl(OFF_MF + g, 1)], w_in[:, sl(OFF_GA + 4 * g, 4)],
                           w_in[:, sl(OFF_GB + 4 * g, 4)]], axis=1)
    cst = np.zeros((128, 64), np.float32)
    cst[:, 0] = i_bias[g]
    cst[:, 1] = f_bias[g]
    cst[:, 2] = 1.0
    cst[:, 3] = NORM_EPS
    cst[:, 4] = L2_EPS
    cst[:, 16:32] = np.tile(a_log[4 * g:4 * g + 4], 4)[None, :]
    cst[:, 32:48] = np.tile(dt_bias[4 * g:4 * g + 4], 4)[None, :]
    convw = np.zeros((128, 48), np.float32)
    for s in range(3):
        for h in range(4):
            c0 = s * 2048 + (4 * g + h) * 128
            convw[:, (s * 4 + h) * 4:(s * 4 + h) * 4 + 4] = conv_w[:, c0:c0 + 128].T
    return dict(w_fm=np.ascontiguousarray(w_fm), w_tm=np.ascontiguousarray(w_tm), w_sm=np.ascontiguousarray(w_sm),
                g1rep=rep128(norm1_g), cst=cst, gmrep=rep128(mnorm_g[g * 512:(g + 1) * 512]),
                gnrep=rep128(np.tile(gnorm_g, 4)), convw=convw, masks=make_masks(), identb=_ident(NPBF16))


_NC_CACHE = {}


def _get_nc(key, builder):
    if key not in _NC_CACHE:
        _NC_CACHE[key] = builder()
    return _NC_CACHE[key]


def kernel_two_launch(x, norm1_g, w_in, mlstm_i_bias, mlstm_f_bias, mlstm_norm_g, gdn_conv_w, gdn_a_log, gdn_dt_bias,
           gdn_norm_g, w_out, norm2_g, w_up, w_down, norm_f_g):
    x = np.asarray(x, np.float32)
    B, T, D = x.shape
    F = w_up.shape[-1]
    f32 = lambda a: np.ascontiguousarray(np.asarray(a, np.float32))
    w_in0, w_out0, w_up0, w_down0 = f32(w_in[0]), f32(w_out[0]), f32(w_up[0]), f32(w_down[0])
    nc1 = _get_nc(("p1", D, T), lambda: build_phase1(D, T))
    groups = [phase1_group_inputs(g, w_in0, f32(norm1_g[0]), f32(mlstm_i_bias[0]), f32(mlstm_f_bias[0]),
                                  f32(mlstm_norm_g[0]), f32(gdn_conv_w[0]), f32(gdn_a_log[0]), f32(gdn_dt_bias[0]),
                                  f32(gdn_norm_g[0])) for g in range(4)]
    in1 = []
    for c in range(NCORES):
        d = dict(groups[c % 4])
        d["x"] = np.ascontiguousarray(x[c // 4])
        in1.append(d)
    r1 = run_bass_kernel_spmd(nc1, in1, core_ids=list(range(NCORES)))
    NTOK = B * T // NCORES
    mixT = np.empty((NCORES, D, NTOK), NPBF16)
    for c in range(NCORES):
        b, q = (c * NTOK) // T, (c * NTOK) % T
        for g in range(4):
            m = r1.results[b * 4 + g]["mix"][q:q + NTOK]
            mixT[c, g * 512:(g + 1) * 512, :] = m[:, 0:512].T
            mixT[c, 2048 + g * 512:2048 + (g + 1) * 512, :] = m[:, 512:1024].T
    nc2 = _get_nc(("p2", D, F, NTOK), lambda: build_phase2(D, F, NTOK))
    g2rep, gfrep, ident = rep128(norm2_g[0]), rep128(norm_f_g), _ident(NPBF16)
    xf = x.reshape(NCORES, NTOK, D)
    in2 = [dict(x=np.ascontiguousarray(xf[c]), mixT=mixT[c], w_out=w_out0, w_up=w_up0, w_down=w_down0,
                g2rep=g2rep, gfrep=gfrep, identb=ident) for c in range(NCORES)]
    r2 = run_bass_kernel_spmd(nc2, in2, core_ids=list(range(NCORES)))
    y = np.stack([r2.results[c]["y"] for c in range(NCORES)]).reshape(B, T, D)
    return y.astype(np.float32)


def _prod(sh):
    n = 1
    for v in sh:
        n *= v
    return n


def _carve(base, off, shape, dt):
    n = _prod(shape)
    if dt == BF16:
        words = (n + 1) // 2
        v = base[:, off:off + words].bitcast(BF16)
        if 2 * words != n:
            v = v[:, 0:n]
    else:
        words = n
        v = base[:, off:off + n]
    if len(shape) == 2:
        v = v.rearrange("p (a b) -> p a b", b=shape[1])
    elif len(shape) == 3:
        v = v.rearrange("p (a b c) -> p a b c", b=shape[1], c=shape[2])
    return v, off + words


def build_fused(D, F, TP, TO, TILE=512):
    KC = D // 128
    NSUB = TILE // 128
    NTP, NTO = TP // TILE, TO // TILE
    NG = 4
    nc = bass.Bass("TRN2", target_bir_lowering=False)

    def din(name, shape, dt=F32):
        return nc.dram_tensor(name, shape, dt, kind="ExternalInput").ap()

    xp = din("xp", [max(TP, 128), D])
    xo = din("xo", [TO, D])
    w_fm = din("w_fm", [D, NG * FM_COLS])
    w_tm = din("w_tm", [D, NG * TM_COLS])
    w_sm = din("w_sm", [D, NG * SM_COLS])
    g1rep_d = din("g1rep", [128, D])
    cst_d = din("cst", [128, NG * 64])
    gmrep_d = din("gmrep", [128, NG * 512])
    gnrep_d = din("gnrep", [128, 512])
    convw_d = din("convw", [128, NG * 48])
    masks_d = din("masks", [128, 4 * 128 + 3 * 512])
    identb_d = din("identb", [128, 128], BF16)
    DM, KM = 4096, 32
    w_out = din("w_out", [DM, D])
    w_up = din("w_up", [D, F])
    w_down = din("w_down", [F, D])
    g2_d = din("g2rep", [128, D])
    gf_d = din("gfrep", [128, D])
    y = nc.dram_tensor("y", [TO, D], F32, kind="ExternalOutput").ap()
    mix_d = nc.dram_tensor("mix_scratch", [TO, DM], BF16).ap()

    P = Prog(nc)
    out_tok = []
    with contextlib.ExitStack() as es_all:
        with contextlib.ExitStack() as es:
            def ps(name, shape, dt=F32):
                return es.enter_context(nc.psum_tensor("f1_" + name, shape, dt))[:, :]

            def V(fn, r=(), w=()):
                return P.op("dve", fn, reads=r, writes=w)

            def A(fn, r=(), w=()):
                return P.op("act", fn, reads=r, writes=w)

            def MM(ps_ap, key, pairs, reads):
                n = len(pairs)
                for i, (l, r_) in enumerate(pairs):
                    P.op("pe", lambda e, l=l, r_=r_, i=i: e.matmul(ps_ap, l, r_, start=(i == 0), stop=(i == n - 1)),
                         reads=reads, writes=[key])

            def TR(ps_ap, key, in_ap, ident, reads):
                P.op("pe", lambda e: e.transpose(ps_ap, in_ap, ident), reads=reads, writes=[key])

            AW = 53100
            arena_t = es.enter_context(nc.sbuf_tensor("arena1", [128, AW], F32))
            arena = arena_t[:, :]
            off = [0]

            def al(shape, dt=F32):
                v, off[0] = _carve(arena, off[0], shape, dt)
                assert off[0] <= AW, "phase-1 arena overflow"
                return v

            def sub(parent, words0, shape, dt=F32):
                v, _ = _carve(parent, words0, shape, dt)
                return v

            cst4 = al([NG, 64])
            drv4 = al([NG, 32])
            gnrep = al([512])
            convw4 = al([NG, 48])
            masks = al([4 * 128 + 3 * 512])
            idb = al([128], BF16)
            onesb = al([2], BF16)
            wsm = al([KC, NG * SM_COLS], BF16)
            CstG = [al([2, 512]) for _ in range(NG)]
            nstG = [al([2]) for _ in range(NG)]
            S4G = [al([4, 128]) for _ in range(NG)]
            haloG = [al([3, 4, 3]) for _ in range(NG)]
            Cb = al([2, 512], BF16)
            nbb = al([2], BF16)
            S4b = al([4, 128], BF16)
            U = masks[:, 0:128]
            Wm = masks[:, 128:256]
            I32 = masks[:, 256:384]
            ones32 = masks[:, 384:512]
            NEG4 = masks[:, 512:1024]
            STRICT4 = masks[:, 1024:1536]
            I4 = masks[:, 1536:2048]
            xs = al([D])
            assert D >= 4096 or True
            if D >= 4096:
                ovb = xs
                P.alias("xs", ["ET4", "X", "Y", ("P", 0), ("P", 1), ("P", 2), ("P", 3)])
            else:
                ovb = al([4096])
            ET4 = sub(ovb, 0, [4, 128])
            Xc = sub(ovb, 512, [4, 128])
            Yc = sub(ovb, 1024, [4, 128])
            P4 = [sub(ovb, 1536 + k_ * 512, [4, 128]) for k_ in range(4)]
            xb_words = max(D // 2, 3968)
            xbp = al([xb_words])
            xb = sub(xbp, 0, [D], BF16)
            sqb = sub(xbp, 0, [4, TILE], BF16)
            at0 = [sub(xbp, 1024, [4, 128], BF16), sub(xbp, 1280, [4, 128], BF16)]
            kdec4 = [sub(xbp, 1536, [4, 128], BF16), sub(xbp, 1792, [4, 128], BF16)]
            vnew = sub(xbp, 2048, [4, 128], BF16)
            smT = [sub(xbp, 2304, [128], BF16), sub(xbp, 2368, [128], BF16)]
            kw = [sub(xbp, 2432, [256], BF16), sub(xbp, 2560, [256], BF16)]
            Pbb = [sub(xbp, 2688, [4, 128], BF16), sub(xbp, 2944, [4, 128], BF16)]
            assert 3200 <= xb_words and 4 * TILE // 2 <= 1024
            P.alias("xb", ["sqb", ("at0", 0), ("at0", 1), ("kdec4", 0), ("kdec4", 1), "vnew", ("smT", 0), ("smT", 1),
                           ("kw", 0), ("kw", 1), "Pb0", "Pb1"])
            g1w = max(D, 12 * TILE // 2 + NSUB * 256)
            g1p = al([g1w])
            g1rep = sub(g1p, 0, [D])
            Gpost = sub(g1p, 0, [12, TILE], BF16)
            Vm = sub(g1p, 12 * TILE // 2, [NSUB, 512], BF16)
            P.alias("g1rep", [("Gpost", 0), ("Gpost", 1), ("Gpost", 2), "Vm"])
            xnT = al([KC, TILE], BF16)
            wb = [al([KC * 256], BF16) for _ in range(2)]
            QmT = al([2, TILE], BF16)
            KmT = al([2, TILE], BF16)
            Gpre = al([4, TILE + 3])
            Og = al([NSUB, 512], BF16)
            Zg = al([NSUB, 512], BF16)
            smg = al([NSUB, NG * SM_COLS])
            cacc = [al([TILE]) for _ in range(2)]
            assert TILE >= 512
            tmpg = [al([256]) for _ in range(2)]
            gt = al([16, 16])
            gm_ = al([12, 4])
            mixo = al([NSUB, 1024], BF16)
            st = al([16])
            ETS = al([4, 128])
            gU4 = ETS
            vsc4 = [al([4, 128]), sub(cacc[0], 0, [4, 128])]
            P.alias(("cacc", 0), [("vsc4", 1)])
            R0 = al([4, 128])
            t1 = al([512])
            opre = al([4, 128])
            num = al([512])
            lfU = al([128])
            EmT = al([128])
            sc = al([32])
            gmrep = al([512])
            print("phase-1 arena words used:", off[0], "of", AW)
            pM = [ps("pM%d" % i, [128, 512]) for i in range(2)]
            pA = ps("pA", [128, 512])
            pB = ps("pB", [128, 512])
            pC = ps("pC", [128, 512])
            pD = ps("pD", [128, 512])
            pE = ps("pE", [128, 512])
            pT = ps("pT", [128, 1024], BF16)

            for i, (dst, src, k) in enumerate([(cst4, cst_d.rearrange("p (g c) -> p g c", c=64), "cst"),
                                               (gnrep, gnrep_d, "gnrep"),
                                               (convw4, convw_d.rearrange("p (g c) -> p g c", c=48), "convw"),
                                               (masks, masks_d, "masks"), (idb, identb_d, "idb")]):
                P.dma("sp", dst, src, writes=[k], slot=("c", i))
            P.dma("pool", wsm, w_sm.rearrange("(c p) n -> p c n", p=128), writes=["wsm"], slot=("c", "wsm"))
            V(lambda e: e.memset(onesb, 1.0), w=["onesb"])
            for g in range(NG):
                V(lambda e, g=g: e.memset(CstG[g], 0.0), w=[("Cst", g)])
                V(lambda e, g=g: e.memset(nstG[g], 0.0), w=[("nst", g)])
                V(lambda e, g=g: e.memset(S4G[g], 0.0), w=[("S4", g)])
                V(lambda e, g=g: e.memset(haloG[g], 0.0), w=[("halo", g)])
            V(lambda e: e.tensor_scalar(out=drv4[:, :, 0:2], in0=cst4[:, :, 0:2], scalar1=1.0 / GATE_CAP, scalar2=None,
                                        op0=ALU.mult), r=["cst"], w=["drv"])
            A(lambda e: e.activation(out=drv4[:, :, 16:32], in_=cst4[:, :, 16:32], func=AF.Exp), r=["cst"], w=["drvA"])

            def fm_list(t_):
                if t_ >= NTP:
                    return tuple(range(FM_COLS // 256))
                return (1, 2, 3, 4, 5, 6, 7) if t_ == NTP - 1 else (1, 4, 5, 6, 7)

            ws = WStream(P, wb, "wb")
            wview = lambda b: b.rearrange("p (c n) -> p c n", n=256)
            for t in range(NTP + NTO):
                own = t >= NTP
                for g in range(NG):
                    for pc in [4, 5] + [pc_ for pc_ in fm_list(t) if pc_ not in (4, 5)]:
                        c0 = g * FM_COLS + pc * 256
                        ws.add(w_fm[:, c0:c0 + 256].rearrange("(c p) n -> p c n", p=128), wview)
                    for pc in (range(TM_COLS // 256) if own else (0, 1)):
                        c0 = g * TM_COLS + pc * 256
                        ws.add(w_tm[:, c0:c0 + 256].rearrange("(c p) n -> p c n", p=128), wview)
            wk = [0]

            def next_w():
                k = wk[0]
                wk[0] += 1
                buf, key = ws.acquire(k)
                return wview(buf), key

            xkeys = [("xnT", i) for i in range(NSUB)]
            GTK = dict(BETA=0, GG=1, RQ=2, NK=3, RK=4, NCC=5, BRK=6, DEC=7, EDEC=8, NEDEC=9, C1=10, EDL=11, RQ2=12,
                       TMP=13, TMP2=14, DL=15)
            GMK = dict(LI=0, LF=1, BC=2, BL=3, EK=4, EGS=5, WOLD=6, TMP=7, TMP2=8)

            def G(kind, j=None, hh=None):
                k = GTK[kind]
                if j is None:
                    return gt[:, k, :]
                if hh is None:
                    return gt[:, k, j * 4:(j + 1) * 4]
                return gt[:, k, j * 4 + hh:j * 4 + hh + 1]

            def M_(kind, j=None):
                k = GMK[kind]
                if j is None:
                    return gm_[:, k, :]
                return gm_[:, k, j:j + 1]

            flat = lambda ap: ap.rearrange("p h l -> p (h l)")
            GT, GM = ["gt"], ["gm"]

            for t in range(NTP + NTO):
                own = t >= NTP
                xsrc = xo if own else xp
                r0 = (t - NTP) * TILE if own else t * TILE
                P.dma("sp", g1rep, g1rep_d, writes=["g1rep"], slot="g1rep")
                for i in range(NSUB):
                    P.dma("sp", xs, xsrc[r0 + i * 128:r0 + (i + 1) * 128, :], writes=["xs"], slot="xs")
                    A(lambda e: e.activation(out=xb, in_=xs, func=AF.Square, accum_out=st[:, 0:1]),
                      r=["xs"], w=["xb", "st0"])
                    A(lambda e: e.activation(out=st[:, 1:2], in_=st[:, 0:1], func=AF.Sqrt, scale=1.0 / D,
                                             bias=cst4[:, 0, 3:4]), r=["st0", "cst"], w=["st1"])
                    V(lambda e: e.reciprocal(out=st[:, 2:3], in_=st[:, 1:2]), r=["st1"], w=["st2"])
                    V(lambda e: e.scalar_tensor_tensor(out=xb, in0=xs, scalar=st[:, 2:3], in1=g1rep,
                                                       op0=ALU.mult, op1=ALU.mult), r=["xs", "st2", "g1rep"], w=["xb"])
                    for c8 in range(0, KC, 8):
                        n8 = min(8, KC - c8)
                        for c in range(n8):
                            TR(pT[:, c * 128:(c + 1) * 128], "pT", xb[:, (c8 + c) * 128:(c8 + c + 1) * 128], idb,
                               ["xb", "idb"])
                        A(lambda e, c8=c8, n8=n8, i=i: e.copy(out=xnT[:, c8:c8 + n8, i * 128:(i + 1) * 128],
                                                              in_=pT[:, 0:n8 * 128].rearrange("p (c n) -> p c n", n=128)),
                          r=["pT"], w=[("xnT", i)])
                for i in range(NSUB):
                    pm = pM[i % 2]
                    pk = ("pM", i % 2)
                    MM(pm[:, 0:NG * SM_COLS], pk, [(xnT[:, kc, i * 128:(i + 1) * 128], wsm[:, kc, :]) for kc in range(KC)],
                       [("xnT", i), "wsm"])
                    A(lambda e, i=i, pm=pm: e.copy(out=smg[:, i, :], in_=pm[:, 0:NG * SM_COLS]), r=[pk], w=["smg"])
                for g in range(NG):
                    Cst, nst, S4, halo = CstG[g], nstG[g], S4G[g], haloG[g]
                    sg0 = g * SM_COLS
                    kC, kn, kS, kh = ("Cst", g), ("nst", g), ("S4", g), ("halo", g)
                    cst, drv, convw = cst4[:, g, :], drv4[:, g, :], convw4[:, g, :]
                    onec, epsc, epsl2 = cst[:, 2:3], cst[:, 3:4], cst[:, 4:5]
                    A(lambda e, Cst=Cst: e.copy(out=Cb.rearrange("p a b -> p (a b)"),
                                                in_=Cst.rearrange("p a b -> p (a b)")), r=[kC], w=["Cb"])
                    A(lambda e, nst=nst: e.copy(out=nbb, in_=nst), r=[kn], w=["nbb"])
                    A(lambda e, S4=S4: e.copy(out=flat(S4b), in_=flat(S4)), r=[kS], w=["S4b"])
                    if own:
                        P.dma("sp", gmrep, gmrep_d[:, g * 512:(g + 1) * 512], writes=["gmrep"], slot="gmrep")
                    def emit_fm(pcs, own=own, halo=halo, kh=kh, convw=convw):
                        for pc in pcs:
                            wv, key = next_w()
                            for half in range(2):
                                oc = pc * 2 + half
                                pm = pM[oc % 2]
                                pk = ("pM", oc % 2)
                                MM(pm[:, 0:TILE], pk,
                                   [(wv[:, kc, half * 128:(half + 1) * 128], xnT[:, kc, :]) for kc in range(KC)],
                                   xkeys + [key])
                                if oc < 2:
                                    A(lambda e, oc=oc, pm=pm: e.copy(out=QmT[:, oc, :], in_=pm[:, 0:TILE]), r=[pk], w=["QmT"])
                                elif oc < 4:
                                    A(lambda e, oc=oc, pm=pm: e.mul(out=KmT[:, oc - 2, :], in_=pm[:, 0:TILE], mul=1.0 / 16.0),
                                      r=[pk], w=["KmT"])
                                else:
                                    s_, hh = (oc - 4) // 4, (oc - 4) % 4
                                    A(lambda e, hh=hh, pm=pm: e.copy(out=Gpre[:, hh, 3:3 + TILE], in_=pm[:, 0:TILE]),
                                      r=[pk], w=[("Gpre", hh)])
                                    if hh == 3:
                                        do_conv = own or s_ != 0
                                        if do_conv:
                                            V(lambda e, s_=s_, halo=halo: e.tensor_copy(out=Gpre[:, :, 0:3],
                                                                                        in_=halo[:, s_, :, :]),
                                              r=[kh], w=[("Gpre", h_) for h_ in range(4)])
                                        for h_ in (range(4) if do_conv else ()):
                                            ca = cacc[h_ % 2]
                                            ck = ("cacc", h_ % 2)
                                            wcol = lambda j_, s_=s_, h_=h_, convw=convw: \
                                                convw[:, (s_ * 4 + h_) * 4 + j_:(s_ * 4 + h_) * 4 + j_ + 1]
                                            V(lambda e, h_=h_, ca=ca, wcol=wcol: e.tensor_scalar(
                                                out=ca, in0=Gpre[:, h_, 0:TILE], scalar1=wcol(0), scalar2=None,
                                                op0=ALU.mult), r=[("Gpre", h_), "convw"], w=[ck])
                                            for j_ in range(1, 4):
                                                V(lambda e, h_=h_, ca=ca, wcol=wcol, j_=j_: e.scalar_tensor_tensor(
                                                    out=ca, in0=Gpre[:, h_, j_:j_ + TILE], scalar=wcol(j_), in1=ca,
                                                    op0=ALU.mult, op1=ALU.add), r=[("Gpre", h_), ck], w=[ck])
                                            A(lambda e, s_=s_, h_=h_, ca=ca: e.activation(out=Gpost[:, s_ * 4 + h_, :], in_=ca,
                                                                                        func=AF.Silu),
                                              r=[ck], w=[("Gpost", s_)])
                                        V(lambda e, s_=s_, halo=halo: e.tensor_copy(out=halo[:, s_, :, :],
                                                                                    in_=Gpre[:, :, TILE:TILE + 3]),
                                          r=[("Gpre", h_) for h_ in range(4)], w=[kh])
                    def emit_tm(own=own):
                        for pc in (range(TM_COLS // 256) if own else (0, 1)):
                            wv, key = next_w()
                            kind, cpc = pc // 2, pc % 2
                            for i in range(NSUB):
                                pm = pM[i % 2]
                                pk = ("pM", i % 2)
                                MM(pm[:, 0:256], pk, [(xnT[:, kc, i * 128:(i + 1) * 128], wv[:, kc, :]) for kc in range(KC)],
                                   [("xnT", i), key])
                                cs = slice(cpc * 256, (cpc + 1) * 256)
                                if kind == 0:
                                    A(lambda e, i=i, pm=pm, cs=cs: e.copy(out=Vm[:, i, cs], in_=pm[:, 0:256]), r=[pk], w=["Vm"])
                                else:
                                    tg = tmpg[i % 2]
                                    tk = ("tmpg", i % 2)
                                    fn = AF.Sigmoid if kind == 1 else AF.Silu
                                    A(lambda e, pm=pm, tg=tg, fn=fn: e.activation(out=tg, in_=pm[:, 0:256], func=fn),
                                      r=[pk], w=[tk])
                                    dst = Og if kind == 1 else Zg
                                    gsrc = gmrep if kind == 1 else gnrep
                                    V(lambda e, i=i, tg=tg, cs=cs, dst=dst, gsrc=gsrc: e.tensor_tensor(
                                        out=dst[:, i, cs], in0=tg, in1=gsrc[:, cs], op=ALU.mult),
                                      r=[tk, "gmrep", "gnrep"], w=["Og" if kind == 1 else "Zg"])

                    def gate_prep1(own=own, drv=drv, cst=cst, onec=onec, epsl2=epsl2, sg0=sg0):
                        A(lambda e, drv=drv, sg0=sg0: e.activation(out=M_("TMP"), in_=smg[:, :, sg0], func=AF.Tanh, scale=1.0 / GATE_CAP,
                                                          bias=drv[:, 0:1]), r=["smg", "drv"], w=GM)
                        V(lambda e: e.tensor_scalar(out=M_("LI"), in0=M_("TMP"), scalar1=GATE_CAP, scalar2=None,
                                                    op0=ALU.mult), r=GM, w=GM)
                        A(lambda e, drv=drv, sg0=sg0: e.activation(out=M_("TMP"), in_=smg[:, :, sg0 + 1], func=AF.Tanh, scale=1.0 / GATE_CAP,
                                                          bias=drv[:, 1:2]), r=["smg", "drv"] + GM, w=GM)
                        A(lambda e: e.activation(out=M_("TMP2"), in_=M_("TMP"), func=AF.Exp, scale=-GATE_CAP), r=GM, w=GM)
                        A(lambda e, onec=onec: e.activation(out=M_("TMP"), in_=M_("TMP2"), func=AF.Ln, bias=onec),
                          r=GM + ["cst"], w=GM)
                        V(lambda e: e.tensor_scalar(out=M_("LF"), in0=M_("TMP"), scalar1=-1.0, scalar2=None, op0=ALU.mult),
                          r=GM, w=GM)
                        MM(pE[:, 0:4], "pE", [(U, M_("LF"))], GM + ["masks"])
                        V(lambda e: e.tensor_copy(out=M_("BC"), in_=pE[:, 0:4]), r=["pE"], w=GM)
                        MM(pE[:, 4:8], "pE", [(ones32, M_("LF"))], GM + ["masks"])
                        V(lambda e: e.tensor_copy(out=M_("BL"), in_=pE[:, 4:8]), r=["pE"], w=GM)
                        A(lambda e: e.activation(out=M_("EK"), in_=M_("BC"), func=AF.Exp), r=GM, w=GM)
                        A(lambda e: e.activation(out=M_("WOLD"), in_=M_("BL"), func=AF.Exp), r=GM, w=GM)
                        V(lambda e: e.tensor_tensor(out=M_("TMP"), in0=M_("BL"), in1=M_("BC"), op=ALU.subtract), r=GM, w=GM)
                        V(lambda e: e.tensor_tensor(out=M_("TMP2"), in0=M_("TMP"), in1=M_("LI"), op=ALU.add), r=GM, w=GM)
                        A(lambda e: e.activation(out=M_("EGS"), in_=M_("TMP2"), func=AF.Exp), r=GM, w=GM)
                        g16 = lambda ap: ap.rearrange("p (s h) -> p s h", h=4)
                        A(lambda e, sg0=sg0: e.activation(out=g16(G("BETA")), in_=smg[:, :, sg0 + 6:sg0 + 10], func=AF.Sigmoid), r=["smg"], w=GT)
                        V(lambda e, cst=cst, sg0=sg0: e.tensor_tensor(out=g16(G("TMP")), in0=smg[:, :, sg0 + 2:sg0 + 6], in1=g16(cst[:, 32:48]),
                                                             op=ALU.add), r=["smg", "cst"] + GT, w=GT)
                        V(lambda e: e.tensor_scalar(out=G("TMP2"), in0=G("TMP"), scalar1=-1.0, scalar2=None, op0=ALU.mult),
                          r=GT, w=GT)
                        V(lambda e: e.tensor_tensor(out=G("TMP2"), in0=G("TMP2"), in1=G("TMP"), op=ALU.max), r=GT, w=GT)
                        A(lambda e: e.activation(out=G("GG"), in_=G("TMP2"), func=AF.Exp, scale=-1.0), r=GT, w=GT)
                        A(lambda e, onec=onec: e.activation(out=G("TMP2"), in_=G("GG"), func=AF.Ln, bias=onec),
                          r=GT + ["cst"], w=GT)
                        V(lambda e: e.tensor_scalar(out=G("GG"), in0=G("TMP"), scalar1=0.0, scalar2=None, op0=ALU.max),
                          r=GT, w=GT)
                        V(lambda e: e.tensor_tensor(out=G("TMP"), in0=G("GG"), in1=G("TMP2"), op=ALU.add), r=GT, w=GT)
                        V(lambda e, drv=drv: e.scalar_tensor_tensor(out=G("GG"), in0=G("TMP"), scalar=-1.0, in1=drv[:, 16:32],
                                                                    op0=ALU.mult, op1=ALU.mult), r=GT + ["drvA"], w=GT)
                        MM(pE[:, 16:32], "pE", [(U, G("GG"))], GT + ["masks"])
                        V(lambda e: e.tensor_copy(out=G("DEC"), in_=pE[:, 16:32]), r=["pE"], w=GT)
                        MM(pE[:, 32:48], "pE", [(ones32, G("GG"))], GT + ["masks"])
                        V(lambda e: e.tensor_copy(out=G("DL"), in_=pE[:, 32:48]), r=["pE"], w=GT)
                        A(lambda e: e.activation(out=G("EDEC"), in_=G("DEC"), func=AF.Exp), r=GT, w=GT)
                        A(lambda e: e.activation(out=G("EDL"), in_=G("DL"), func=AF.Exp), r=GT, w=GT)
                        V(lambda e: e.tensor_scalar(out=G("NEDEC"), in0=G("EDEC"), scalar1=-1.0, scalar2=None, op0=ALU.mult),
                          r=GT, w=GT)
                        V(lambda e: e.tensor_tensor(out=G("TMP"), in0=G("DL"), in1=G("DEC"), op=ALU.subtract), r=GT, w=GT)
                        A(lambda e: e.activation(out=G("C1"), in_=G("TMP"), func=AF.Exp), r=GT, w=GT)
                        for s_, dstk in ((1, "NK"),):
                            V(lambda e, s_=s_: e.tensor_tensor(out=sqb, in0=Gpost[:, s_ * 4:(s_ + 1) * 4, :],
                                                               in1=Gpost[:, s_ * 4:(s_ + 1) * 4, :], op=ALU.mult),
                              r=[("Gpost", s_)], w=["sqb"])
                            for j in range(NSUB):
                                for hh in range(4):
                                    c = 64 + j * 4 + hh
                                    MM(pE[:, c:c + 1], "pE", [(sqb[:, hh, j * 128:(j + 1) * 128], onesb[:, 0:1])],
                                       ["sqb", "onesb"])
                            A(lambda e, dstk=dstk, epsl2=epsl2: e.activation(out=G(dstk), in_=pE[:, 64:80], func=AF.Sqrt,
                                                                           bias=epsl2), r=["pE", "cst"] + GT, w=GT)
                        V(lambda e: e.reciprocal(out=G("RK"), in_=G("NK")), r=GT, w=GT)
                        V(lambda e: e.tensor_tensor(out=G("BRK"), in0=G("BETA"), in1=G("RK"), op=ALU.mult), r=GT, w=GT)
                        V(lambda e: e.scalar_tensor_tensor(out=G("NCC"), in0=G("BRK"), scalar=-1.0, in1=G("RK"),
                                                           op0=ALU.mult, op1=ALU.mult), r=GT, w=GT)
                        V(lambda e: e.tensor_tensor(out=G("TMP"), in0=G("C1"), in1=G("RK"), op=ALU.mult), r=GT, w=GT)
                        V(lambda e: e.tensor_copy(out=G("C1"), in_=G("TMP")), r=GT, w=GT)

                    def gate_prep2(epsl2=epsl2):
                        for s_, dstk in ((0, "RQ"),):
                            V(lambda e, s_=s_: e.tensor_tensor(out=sqb, in0=Gpost[:, s_ * 4:(s_ + 1) * 4, :],
                                                               in1=Gpost[:, s_ * 4:(s_ + 1) * 4, :], op=ALU.mult),
                              r=[("Gpost", s_)], w=["sqb"])
                            for j in range(NSUB):
                                for hh in range(4):
                                    c = 64 + j * 4 + hh
                                    MM(pE[:, c:c + 1], "pE", [(sqb[:, hh, j * 128:(j + 1) * 128], onesb[:, 0:1])],
                                       ["sqb", "onesb"])
                            A(lambda e, dstk=dstk, epsl2=epsl2: e.activation(out=G(dstk), in_=pE[:, 64:80], func=AF.Sqrt,
                                                                           bias=epsl2), r=["pE", "cst"] + GT, w=GT)
                        if True:
                            V(lambda e: e.reciprocal(out=G("TMP"), in_=G("RQ")), r=GT, w=GT)
                            V(lambda e: e.tensor_scalar(out=G("RQ"), in0=G("TMP"), scalar1=128.0 ** -0.5, scalar2=None,
                                                        op0=ALU.mult), r=GT, w=GT)
                            V(lambda e: e.tensor_tensor(out=G("TMP"), in0=G("RQ"), in1=G("RQ"), op=ALU.mult), r=GT, w=GT)
                            V(lambda e: e.tensor_scalar(out=G("RQ2"), in0=G("TMP"), scalar1=1.0 / 128.0, scalar2=None,
                                                        op0=ALU.mult), r=GT, w=GT)
                    def chainA(j):
                        cs = slice(j * 128, (j + 1) * 128)
                        Pj, pk = P4[j], ("P", j)
                        for hh in range(4):
                            V(lambda e, hh=hh: e.tensor_scalar(out=gU4[:, hh, :], in0=U, scalar1=G("GG", j, hh),
                                                               scalar2=None, op0=ALU.mult), r=GT + ["masks"], w=["ETS"])
                        MM(pA, "pA", [(Wm, flat(gU4)), (I32, NEG4)], ["ETS", "masks"])
                        A(lambda e: e.activation(out=flat(ET4), in_=pA, func=AF.Exp), r=["pA"], w=["ET4"])
                        for hh in range(4):
                            MM(pB[:, hh * 128:(hh + 1) * 128], "pB", [(Gpost[:, 4 + hh, cs], Gpost[:, 4 + hh, cs])],
                               [("Gpost", 1)])
                        V(lambda e: e.tensor_tensor(out=flat(ETS), in0=flat(ET4), in1=STRICT4, op=ALU.mult),
                          r=["ET4", "masks"], w=["ETS"])
                        for hh in range(4):
                            V(lambda e, hh=hh: e.scalar_tensor_tensor(
                                out=Xc[:, hh, :], in0=pB[:, hh * 128:(hh + 1) * 128], scalar=G("NCC", j, hh),
                                in1=ETS[:, hh, :], op0=ALU.mult, op1=ALU.mult), r=["pB", "ETS"] + GT, w=["X"])
                        for hh in range(4):
                            TR(pD[:, hh * 128:(hh + 1) * 128], "pD", Xc[:, hh, :], I32, ["X", "masks"])
                        A(lambda e: e.copy(out=flat(Yc), in_=pD), r=["pD"], w=["Y"])
                        V(lambda e: e.tensor_tensor(out=flat(Pj), in0=flat(Xc), in1=I4, op=ALU.add),
                          r=["X", "masks"], w=[pk])
                        for lv in range(1, NLEV + 1):
                            last = (lv == NLEV)
                            for hh in range(4):
                                MM(pD[:, hh * 128:(hh + 1) * 128], "pD", [(Xc[:, hh, :], Yc[:, hh, :])], ["X", "Y"])
                            if not last:
                                for hh in range(4):
                                    MM(pB[:, hh * 128:(hh + 1) * 128], "pB", [(Yc[:, hh, :], Xc[:, hh, :])], ["X", "Y"])
                            V(lambda e: e.tensor_copy(out=flat(Yc), in_=pD), r=["pD"], w=["Y"])
                            if not last:
                                A(lambda e: e.copy(out=flat(Xc), in_=pB), r=["pB"], w=["X"])
                            for hh in range(4):
                                MM(pC[:, hh * 128:(hh + 1) * 128], "pC", [(Yc[:, hh, :], Pj[:, hh, :])], ["Y", pk])
                            V(lambda e: e.tensor_tensor(out=flat(Pj), in0=pC, in1=flat(Pj), op=ALU.add),
                              r=["pC", pk], w=[pk])

                    def late(j, own=own):
                        jp = j % 2
                        cs = slice(j * 128, (j + 1) * 128)
                        if own:
                            for hh in range(4):
                                V(lambda e, hh=hh: e.tensor_scalar(out=gU4[:, hh, :], in0=U, scalar1=G("GG", j, hh),
                                                                   scalar2=None, op0=ALU.mult), r=GT + ["masks"], w=["ETS"])
                            MM(pA, "pA", [(Wm, flat(gU4)), (I32, NEG4)], ["ETS", "masks"])
                            A(lambda e: e.activation(out=flat(ET4), in_=pA, func=AF.Exp), r=["pA"], w=["ET4"])
                            for hh in range(4):
                                MM(pC[:, hh * 128:(hh + 1) * 128], "pC", [(Gpost[:, 4 + hh, cs], Gpost[:, hh, cs])],
                                   [("Gpost", 1), ("Gpost", 0)])
                            for hh in range(4):
                                V(lambda e, hh=hh: e.scalar_tensor_tensor(
                                    out=at0[jp][:, hh, :], in0=pC[:, hh * 128:(hh + 1) * 128], scalar=G("RK", j, hh),
                                    in1=ET4[:, hh, :], op0=ALU.mult, op1=ALU.mult), r=["pC", "ET4"] + GT, w=[("at0", jp)])
                            V(lambda e: e.tensor_scalar(out=lfU, in0=U, scalar1=M_("LF", j), scalar2=None, op0=ALU.mult),
                              r=GM + ["masks"], w=["lfU"])
                            MM(pB[:, 0:128], "pB", [(Wm, lfU), (I32, NEG4[:, 0:128])], ["lfU", "masks"])
                            A(lambda e: e.activation(out=EmT, in_=pB[:, 0:128], func=AF.Exp, bias=M_("LI", j)),
                              r=["pB"] + GM, w=["EmT"])
                            MM(pD[:, 0:128], "pD", [(KmT[:, 0, cs], QmT[:, 0, cs]), (KmT[:, 1, cs], QmT[:, 1, cs])],
                               ["KmT", "QmT"])
                            V(lambda e: e.tensor_tensor(out=smT[jp], in0=pD[:, 0:128], in1=EmT, op=ALU.mult),
                              r=["pD", "EmT"], w=[("smT", jp)])
                        for dc in range(2):
                            TR(pT[:, dc * 128:(dc + 1) * 128], "pT", KmT[:, dc, cs], idb, ["KmT", "idb"])
                        A(lambda e: e.activation(out=kw[jp], in_=pT[:, 0:256], func=AF.Copy, scale=M_("EGS", j)),
                          r=["pT"] + GM, w=[("kw", jp)])
                        for hh in range(4):
                            TR(pT[:, hh * 128:(hh + 1) * 128], "pT", Gpost[:, 4 + hh, cs], idb, [("Gpost", 1), "idb"])
                        for hh in range(4):
                            TR(pT[:, 512 + hh * 128:512 + (hh + 1) * 128], "pT", Gpost[:, 8 + hh, cs], idb,
                               [("Gpost", 2), "idb"])
                        for hh in range(4):
                            A(lambda e, hh=hh: e.activation(out=kdec4[jp][:, hh, :], in_=pT[:, hh * 128:(hh + 1) * 128],
                                                            func=AF.Copy, scale=G("C1", j, hh)),
                              r=["pT"] + GT, w=[("kdec4", jp)])
                            A(lambda e, hh=hh: e.activation(out=vsc4[jp][:, hh, :],
                                                            in_=pT[:, 512 + hh * 128:512 + (hh + 1) * 128],
                                                            func=AF.Copy, scale=G("NK", j, hh)),
                              r=["pT"] + GT, w=[("vsc4", jp)])

                    def dep(j, S4=S4, Cst=Cst, nst=nst, kS=kS, kC=kC, kn=kn, epsc=epsc, own=own):
                        jp = j % 2
                        cs = slice(j * 128, (j + 1) * 128)
                        NT_ = P4[j]
                        ntk = ("P", j)
                        q0, q1 = pM[0], pM[1]
                        k0, k1 = ("pM", 0), ("pM", 1)
                        for hh in range(4):
                            MM(q0[:, hh * 128:(hh + 1) * 128], k0, [(Gpost[:, 4 + hh, cs], S4b[:, hh, :])],
                               [("Gpost", 1), "S4b"])
                        if own:
                            for hh in range(4):
                                MM(pE[:, hh * 128:(hh + 1) * 128], "pE", [(Gpost[:, hh, cs], S4b[:, hh, :])],
                                   [("Gpost", 0), "S4b"])
                        for hh in range(4):
                            V(lambda e, hh=hh: e.scalar_tensor_tensor(
                                out=R0[:, hh, :], in0=q0[:, hh * 128:(hh + 1) * 128], scalar=G("NEDEC", j, hh),
                                in1=vsc4[jp][:, hh, :], op0=ALU.mult, op1=ALU.add), r=[k0, ("vsc4", jp)] + GT, w=["R0"])
                        for hh in range(4):
                            MM(q1[:, hh * 128:(hh + 1) * 128], k1, [(NT_[:, hh, :], R0[:, hh, :])], [ntk, "R0"])
                        for hh in range(4):
                            A(lambda e, hh=hh: e.activation(out=vnew[:, hh, :], in_=q1[:, hh * 128:(hh + 1) * 128],
                                                            func=AF.Copy, scale=G("BRK", j, hh)), r=[k1] + GT, w=["vnew"])
                        if own:
                            for hh in range(4):
                                MM(q0[:, hh * 128:(hh + 1) * 128], k0, [(at0[jp][:, hh, :], vnew[:, hh, :])],
                                   [("at0", jp), "vnew"])
                            for hh in range(4):
                                A(lambda e, hh=hh: e.activation(out=t1[:, hh * 128:(hh + 1) * 128],
                                                                in_=pE[:, hh * 128:(hh + 1) * 128], func=AF.Copy,
                                                                scale=G("EDEC", j, hh)), r=["pE"] + GT, w=["t1"])
                            V(lambda e: e.tensor_tensor(out=flat(opre), in0=q0, in1=t1, op=ALU.add),
                              r=[k0, "t1"], w=["opre"])
                        for hh in range(4):
                            MM(q1[:, hh * 128:(hh + 1) * 128], k1, [(kdec4[jp][:, hh, :], vnew[:, hh, :])],
                               [("kdec4", jp), "vnew"])
                        for hh in range(4):
                            V(lambda e, hh=hh: e.scalar_tensor_tensor(
                                out=S4[:, hh, :], in0=S4[:, hh, :], scalar=G("EDL", j, hh),
                                in1=q1[:, hh * 128:(hh + 1) * 128], op0=ALU.mult, op1=ALU.add),
                              r=[k1, kS] + GT, w=[kS])
                        A(lambda e: e.copy(out=flat(S4b), in_=flat(S4)), r=[kS], w=["S4b"])
                        if own:
                            V(lambda e: e.tensor_tensor(out=R0, in0=opre, in1=opre, op=ALU.mult), r=["opre"], w=["R0"])
                            V(lambda e: e.tensor_reduce(out=sc[:, 0:4], in_=R0, axis=mybir.AxisListType.X, op=ALU.add),
                              r=["R0"], w=["sc0"])
                            V(lambda e: e.tensor_tensor(out=sc[:, 4:8], in0=sc[:, 0:4], in1=G("RQ2", j), op=ALU.mult),
                              r=["sc0"] + GT, w=["sc1"])
                            A(lambda e: e.activation(out=sc[:, 8:12], in_=sc[:, 4:8], func=AF.Ln, bias=epsc),
                              r=["sc1", "cst"], w=["sc2"])
                            A(lambda e: e.activation(out=sc[:, 12:16], in_=sc[:, 8:12], func=AF.Exp, scale=-0.5),
                              r=["sc2"], w=["sc3"])
                            V(lambda e: e.tensor_tensor(out=sc[:, 16:20], in0=sc[:, 12:16], in1=G("RQ", j), op=ALU.mult),
                              r=["sc3"] + GT, w=["sc4"])
                            for hh in range(4):
                                V(lambda e, hh=hh: e.scalar_tensor_tensor(
                                    out=mixo[:, j, 512 + hh * 128:512 + (hh + 1) * 128], in0=opre[:, hh, :],
                                    scalar=sc[:, 16 + hh:17 + hh], in1=Zg[:, j, hh * 128:(hh + 1) * 128],
                                    op0=ALU.mult, op1=ALU.mult), r=["opre", "sc4", "Zg"], w=[("mixo", j)])
                            MM(q0, k0, [(smT[jp], Vm[:, j, :])], [("smT", jp), "Vm"])
                            MM(q1, k1, [(QmT[:, 0, cs], Cb[:, 0, :]), (QmT[:, 1, cs], Cb[:, 1, :])], ["QmT", "Cb"])
                            MM(pE[:, 0:1], "pE", [(QmT[:, 0, cs], nbb[:, 0:1]), (QmT[:, 1, cs], nbb[:, 1:2])],
                               ["QmT", "nbb"])
                            MM(pE[:, 1:2], "pE", [(smT[jp], onesb[:, 0:1])], [("smT", jp), "onesb"])
                            A(lambda e: e.activation(out=t1, in_=q1, func=AF.Copy, scale=M_("EK", j)),
                              r=[k1] + GM, w=["t1"])
                            V(lambda e: e.tensor_tensor(out=num, in0=q0, in1=t1, op=ALU.add), r=[k0, "t1"], w=["num"])
                            A(lambda e: e.copy(out=sc[:, 20:22], in_=pE[:, 0:2]), r=["pE"], w=["sc5"])
                            V(lambda e: e.scalar_tensor_tensor(out=sc[:, 22:23], in0=sc[:, 20:21], scalar=M_("EK", j),
                                                               in1=sc[:, 21:22], op0=ALU.mult, op1=ALU.add),
                              r=["sc5"] + GM, w=["sc6"])
                            V(lambda e: e.tensor_scalar(out=sc[:, 23:24], in0=sc[:, 22:23], scalar1=-1.0, scalar2=None,
                                                        op0=ALU.mult), r=["sc6"], w=["sc7"])
                            V(lambda e: e.tensor_tensor(out=sc[:, 23:24], in0=sc[:, 23:24], in1=sc[:, 22:23], op=ALU.max),
                              r=["sc6", "sc7"], w=["sc7"])
                            V(lambda e: e.tensor_scalar(out=sc[:, 23:24], in0=sc[:, 23:24], scalar1=1.0, scalar2=None,
                                                        op0=ALU.max), r=["sc7"], w=["sc7"])
                            V(lambda e: e.reciprocal(out=sc[:, 24:25], in_=sc[:, 23:24]), r=["sc7"], w=["sc8"])
                            A(lambda e: e.activation(out=t1, in_=num, func=AF.Square, accum_out=sc[:, 25:26]),
                              r=["num"], w=["t1", "sc9"])
                            V(lambda e: e.tensor_tensor(out=sc[:, 26:27], in0=sc[:, 24:25], in1=sc[:, 24:25], op=ALU.mult),
                              r=["sc8"], w=["sc10"])
                            V(lambda e: e.tensor_tensor(out=sc[:, 27:28], in0=sc[:, 26:27], in1=sc[:, 25:26], op=ALU.mult),
                              r=["sc10", "sc9"], w=["sc11"])
                            A(lambda e: e.activation(out=sc[:, 28:29], in_=sc[:, 27:28], func=AF.Ln, scale=1.0 / 512.0,
                                                     bias=epsc), r=["sc11", "cst"], w=["sc12"])
                            A(lambda e: e.activation(out=sc[:, 29:30], in_=sc[:, 28:29], func=AF.Exp, scale=-0.5),
                              r=["sc12"], w=["sc13"])
                            V(lambda e: e.tensor_tensor(out=sc[:, 30:31], in0=sc[:, 29:30], in1=sc[:, 24:25], op=ALU.mult),
                              r=["sc13", "sc8"], w=["sc14"])
                            V(lambda e: e.scalar_tensor_tensor(out=mixo[:, j, 0:512], in0=num, scalar=sc[:, 30:31],
                                                               in1=Og[:, j, :], op0=ALU.mult, op1=ALU.mult),
                              r=["num", "sc14", "Og"], w=[("mixo", j)])
                        MM(q0, k0, [(kw[jp][:, 0:128], Vm[:, j, :])], [("kw", jp), "Vm"])
                        MM(q1, k1, [(kw[jp][:, 128:256], Vm[:, j, :])], [("kw", jp), "Vm"])
                        MM(pE[:, 8:9], "pE", [(kw[jp][:, 0:128], onesb[:, 0:1])], [("kw", jp), "onesb"])
                        MM(pE[:, 9:10], "pE", [(kw[jp][:, 128:256], onesb[:, 0:1])], [("kw", jp), "onesb"])
                        V(lambda e: e.scalar_tensor_tensor(out=Cst[:, 0, :], in0=Cst[:, 0, :], scalar=M_("WOLD", j),
                                                           in1=q0, op0=ALU.mult, op1=ALU.add), r=[k0, kC] + GM, w=[kC])
                        V(lambda e: e.scalar_tensor_tensor(out=Cst[:, 1, :], in0=Cst[:, 1, :], scalar=M_("WOLD", j),
                                                           in1=q1, op0=ALU.mult, op1=ALU.add), r=[k1, kC] + GM, w=[kC])
                        A(lambda e: e.copy(out=Cb.rearrange("p a b -> p (a b)"), in_=Cst.rearrange("p a b -> p (a b)")),
                          r=[kC], w=["Cb"])
                        V(lambda e: e.scalar_tensor_tensor(out=nst, in0=nst, scalar=M_("WOLD", j), in1=pE[:, 8:10],
                                                           op0=ALU.mult, op1=ALU.add), r=["pE", kn] + GM, w=[kn])
                        A(lambda e: e.copy(out=nbb, in_=nst), r=[kn], w=["nbb"])

                    gk_p = (4, 5)
                    emit_fm(gk_p)
                    P.begin_capture()
                    emit_fm([pc for pc in fm_list(t) if pc not in gk_p])
                    emit_tm()
                    rest_ = P.end_capture()
                    P.begin_capture()
                    gate_prep1()
                    for j in range(NSUB):
                        chainA(j)
                    chains_ = P.end_capture()
                    P.play(rest_, chains_)
                    if own:
                        gate_prep2()
                    lates, deps = [], []
                    for j in range(NSUB):
                        P.begin_capture()
                        late(j)
                        lates.append(P.end_capture())
                        P.begin_capture()
                        dep(j)
                        deps.append(P.end_capture())
                    P.play(lates[0])
                    for j in range(NSUB):
                        P.play(deps[j], lates[j + 1] if j + 1 < NSUB else [])
                    if own:
                        for i in range(NSUB):
                            rr = slice(r0 + i * 128, r0 + (i + 1) * 128)
                            P.dma("sp", mix_d[rr, g * 512:(g + 1) * 512], mixo[:, i, 0:512], reads=[("mixo", i)],
                                  writes=["mix_d"], slot=("mixo", i, 0))
                            P.dma("sp", mix_d[rr, 2048 + g * 512:2048 + (g + 1) * 512], mixo[:, i, 512:1024],
                                  reads=[("mixo", i)], writes=["mix_d"], slot=("mixo", i, 1))

        P.barrier()
        with contextlib.ExitStack() as es:
            def sb(name, shape, dt):
                return es.enter_context(nc.sbuf_tensor("f2_" + name, shape, dt))

            def ps2(name, shape, dt=F32):
                return es.enter_context(nc.psum_tensor("f2_" + name, shape, dt))

            NT = NTO
            FG = min(F, 2048)
            NFG = F // FG
            FCG = FG // 128
            CG = D // 512
            RG = 2
            KCR = KM // RG
            h = sb("h", [128, NSUB, D], F32)
            actT = sb("actT", [128, max(KC, KM), TILE], BF16)
            upT = sb("upT", [128, FCG, TILE], BF16)
            WB = max(KC * 256, FCG * 512, KCR * 512)
            wb2 = [sb("wb%d" % i, [128, WB], BF16) for i in range(3)]
            grep = sb("grep", [128, D], F32)
            hb = sb("hb", [128, max(D, DM)], BF16)
            rl = [sb("rl%d" % i, [128, TILE], F32) for i in range(2)]
            idb2 = sb("idb", [128, 128], BF16)
            st2 = sb("st", [128, 8], F32)
            epsc2 = sb("epsc", [128, 1], F32)
            ps_o = [ps2("ps_o%d" % i, [128, 512]) for i in range(4)]
            ps_u = [ps2("ps_u%d" % i, [128, 512]) for i in range(2)]
            ps_t = ps2("ps_t", [128, 1024], BF16)
            P.dma("sp", idb2[:, :], identb_d, writes=["idb2"], slot="c_id2")
            P.op("dve", lambda e: e.memset(epsc2[:, :], NORM_EPS), writes=["epsc2"])
            ws2 = WStream(P, wb2, "wb2")
            for t in range(NT):
                for cg in range(CG):
                    for rg in range(RG):
                        src = w_out[rg * KCR * 128:(rg + 1) * KCR * 128, cg * 512:(cg + 1) * 512] \
                            .rearrange("(c p) n -> p c n", p=128)
                        ws2.add(src, lambda b: b[:, 0:KCR * 512].rearrange("p (c n) -> p c n", n=512))
                for fg in range(NFG):
                    for pc in range(FG // 256):
                        c0 = fg * FG + pc * 256
                        src = w_up[:, c0:c0 + 256].rearrange("(c p) n -> p c n", p=128)
                        ws2.add(src, lambda b: b[:, 0:KC * 256].rearrange("p (c n) -> p c n", n=256))
                    for cg in range(CG):
                        src = w_down[fg * FG:(fg + 1) * FG, cg * 512:(cg + 1) * 512] \
                            .rearrange("(c p) n -> p c n", p=128)
                        ws2.add(src, lambda b: b[:, 0:FCG * 512].rearrange("p (c n) -> p c n", n=512))
            wk2 = [0]

            def next_w2():
                k = wk2[0]
                wk2[0] += 1
                return ws2.acquire(k)

            def transpose_hb(i, nch):
                for c8 in range(0, nch, 8):
                    n8 = min(8, nch - c8)
                    for c in range(n8):
                        P.op("pe", lambda e, c=c, c8=c8: e.transpose(ps_t[:, c * 128:(c + 1) * 128],
                                                                     hb[:, (c8 + c) * 128:(c8 + c + 1) * 128],
                                                                     idb2[:, :]),
                             reads=["hb", "idb2"], writes=["ps_t"])
                    P.op("act", lambda e, c8=c8, n8=n8, i=i: e.copy(
                        out=actT[:, c8:c8 + n8, i * 128:(i + 1) * 128],
                        in_=ps_t[:, 0:n8 * 128].rearrange("p (c n) -> p c n", n=128)),
                         reads=["ps_t"], writes=[("actT", i)])

            def rms_scale(i, c0, gkey):
                P.op("act", lambda e, i=i: e.activation(out=hb[:, 0:D], in_=h[:, i, :], func=AF.Square,
                                                      accum_out=st2[:, c0:c0 + 1]),
                     reads=[("h", i)], writes=["hb", ("st2", c0)])
                P.op("act", lambda e: e.activation(out=st2[:, c0 + 1:c0 + 2], in_=st2[:, c0:c0 + 1], func=AF.Sqrt,
                                                   scale=1.0 / D, bias=epsc2[:, 0:1]),
                     reads=[("st2", c0), "epsc2"], writes=[("st2", c0 + 1)])
                P.op("dve", lambda e: e.reciprocal(out=st2[:, c0 + 2:c0 + 3], in_=st2[:, c0 + 1:c0 + 2]),
                     reads=[("st2", c0 + 1)], writes=[("st2", c0 + 2)])

            for t in range(NT):
                r0 = t * TILE
                for i in range(NSUB):
                    P.dma("sp", h[:, i, :], xo[r0 + i * 128:r0 + (i + 1) * 128, :], writes=[("h", i)], slot=("h", i))
                P.dma("sp", grep[:, :], g2_d, writes=["grep"], slot="grep")
                for i in range(NSUB):
                    P.dma("sp", hb[:, 0:DM], mix_d[r0 + i * 128:r0 + (i + 1) * 128, :], reads=["mix_d"], writes=["hb"],
                          slot="hb")
                    transpose_hb(i, KM)
                for cg in range(CG):
                    for rg in range(RG):
                        buf, key = next_w2()
                        wv = buf[:, 0:KCR * 512].rearrange("p (c n) -> p c n", n=512)
                        for i in range(NSUB):
                            for c in range(KCR):
                                kc = rg * KCR + c
                                first = (rg == 0 and c == 0)
                                last = (rg == RG - 1 and c == KCR - 1)
                                P.op("pe", lambda e, i=i, kc=kc, c=c, wv=wv, first=first, last=last: e.matmul(
                                    ps_o[i][:, :], actT[:, kc, i * 128:(i + 1) * 128], wv[:, c, :],
                                    start=first, stop=last), reads=[("actT", i), key], writes=[("ps_o", i)])
                    for i in range(NSUB):
                        P.op("dve", lambda e, i=i, cg=cg: e.tensor_tensor(
                            out=h[:, i, cg * 512:(cg + 1) * 512], in0=ps_o[i][:, :],
                            in1=h[:, i, cg * 512:(cg + 1) * 512], op=ALU.add),
                             reads=[("ps_o", i), ("h", i)], writes=[("h", i)])
                for i in range(NSUB):
                    rms_scale(i, 0, "grep")
                    P.op("dve", lambda e, i=i: e.scalar_tensor_tensor(out=hb[:, 0:D], in0=h[:, i, :], scalar=st2[:, 2:3],
                                                                    in1=grep[:, :], op0=ALU.mult, op1=ALU.mult),
                         reads=[("h", i), ("st2", 2), "grep"], writes=["hb"])
                    transpose_hb(i, KC)
                for fg in range(NFG):
                    for pc in range(FG // 256):
                        buf, key = next_w2()
                        wv = buf[:, 0:KC * 256].rearrange("p (c n) -> p c n", n=256)
                        for half in range(2):
                            fc = pc * 2 + half
                            pu = ps_u[fc % 2]
                            for kc in range(KC):
                                P.op("pe", lambda e, kc=kc, wv=wv, half=half, pu=pu: e.matmul(
                                    pu[:, 0:TILE], wv[:, kc, half * 128:(half + 1) * 128], actT[:, kc, :],
                                    start=(kc == 0), stop=(kc == KC - 1)),
                                     reads=[("actT", i) for i in range(NSUB)] + [key], writes=[("ps_u", fc % 2)])
                            r = rl[fc % 2]
                            P.op("act", lambda e, pu=pu, r=r: e.activation(out=r[:, :], in_=pu[:, 0:TILE], func=AF.Relu),
                                 reads=[("ps_u", fc % 2)], writes=[("rl", fc % 2)])
                            P.op("dve", lambda e, r=r, fc=fc: e.tensor_tensor(out=upT[:, fc, :], in0=r[:, :], in1=r[:, :],
                                                                            op=ALU.mult),
                                 reads=[("rl", fc % 2)], writes=["upT"])
                    for cg in range(CG):
                        buf, key = next_w2()
                        wv = buf[:, 0:FCG * 512].rearrange("p (c n) -> p c n", n=512)
                        for i in range(NSUB):
                            for fc in range(FCG):
                                P.op("pe", lambda e, i=i, fc=fc, wv=wv: e.matmul(
                                    ps_o[i][:, :], upT[:, fc, i * 128:(i + 1) * 128], wv[:, fc, :],
                                    start=(fc == 0), stop=(fc == FCG - 1)), reads=["upT", key], writes=[("ps_o", i)])
                            P.op("dve", lambda e, i=i, cg=cg: e.tensor_tensor(
                                out=h[:, i, cg * 512:(cg + 1) * 512], in0=ps_o[i][:, :],
                                in1=h[:, i, cg * 512:(cg + 1) * 512], op=ALU.add),
                                 reads=[("ps_o", i), ("h", i)], writes=[("h", i)])
                P.dma("sp", grep[:, :], gf_d, writes=["grep"], slot="grep")
                for i in range(NSUB):
                    rms_scale(i, 4, "grep")
                    P.op("dve", lambda e, i=i: e.scalar_tensor_tensor(out=h[:, i, :], in0=h[:, i, :], scalar=st2[:, 6:7],
                                                                    in1=grep[:, :], op0=ALU.mult, op1=ALU.mult),
                         reads=[("h", i), ("st2", 6), "grep"], writes=[("h", i)])
                    tok = P.dma("sp", y[r0 + i * 128:r0 + (i + 1) * 128, :], h[:, i, :], reads=[("h", i)],
                                slot=("yout", i))
                    out_tok.append(tok)
            final = {}
            for tk in out_tok:
                final[tk[1]] = tk
            P.emit(list(final.values()))
    return nc


def fused_inputs(s_idx, xb_full, TP, TO, shared):
    T, D = xb_full.shape
    o = s_idx * TO
    xp = np.zeros((max(TP, 128), D), np.float32)
    if o > 0:
        xp[TP - o:TP] = xb_full[0:o]
    d = dict(shared)
    d["xp"] = xp
    d["xo"] = np.ascontiguousarray(xb_full[o:o + TO])
    return d


def fused_shared_inputs(w_in, norm1_g, i_bias, f_bias, mnorm_g, conv_w, a_log, dt_bias, gnorm_g, w_out, w_up, w_down,
                        norm2_g, norm_f_g):
    gs = [phase1_group_inputs(g, w_in, norm1_g, i_bias, f_bias, mnorm_g, conv_w, a_log, dt_bias, gnorm_g)
          for g in range(4)]
    cat = lambda k: np.ascontiguousarray(np.concatenate([g_[k] for g_ in gs], axis=1))
    return dict(w_fm=cat("w_fm"), w_tm=cat("w_tm"), w_sm=cat("w_sm"), g1rep=gs[0]["g1rep"], cst=cat("cst"),
                gmrep=cat("gmrep"), gnrep=gs[0]["gnrep"], convw=cat("convw"), masks=gs[0]["masks"],
                identb=gs[0]["identb"], w_out=w_out, w_up=w_up, w_down=w_down, g2rep=rep128(norm2_g),
                gfrep=rep128(norm_f_g))


def kernel(x, norm1_g, w_in, mlstm_i_bias, mlstm_f_bias, mlstm_norm_g, gdn_conv_w, gdn_a_log, gdn_dt_bias,
           gdn_norm_g, w_out, norm2_g, w_up, w_down, norm_f_g):
    x = np.asarray(x, np.float32)
    B, T, D = x.shape
    F = w_up.shape[-1]
    f32 = lambda a: np.ascontiguousarray(np.asarray(a, np.float32))
    per_b = NCORES // B
    TO = T // per_b
    TP = (per_b - 1) * TO
    nc = _get_nc(("fused", D, F, TP, TO), lambda: build_fused(D, F, TP, TO))
    shared = fused_shared_inputs(f32(w_in[0]), f32(norm1_g[0]), f32(mlstm_i_bias[0]), f32(mlstm_f_bias[0]),
                                 f32(mlstm_norm_g[0]), f32(gdn_conv_w[0]), f32(gdn_a_log[0]), f32(gdn_dt_bias[0]),
                                 f32(gdn_norm_g[0]), f32(w_out[0]), f32(w_up[0]), f32(w_down[0]), f32(norm2_g[0]),
                                 f32(norm_f_g))
    in_maps = [fused_inputs(c % per_b, x[c // per_b], TP, TO, shared) for c in range(NCORES)]
    res = run_bass_kernel_spmd(nc, in_maps, core_ids=list(range(NCORES)))
    y = np.stack([res.results[c]["y"] for c in range(NCORES)]).reshape(B, T, D)
    return y.astype(np.float32)
```
